# Optimizing a Trainium2 kernel written in Bass

```python
import math
import jax, jax.numpy as jnp
from jax import lax
import numpy as np

D_MODEL = 1024
BATCH = 4
SEQ = 8192
DEPTH = 1

RET_WIDTH = D_MODEL
RET_HEADS = 8
RET_HEAD_DIM = RET_WIDTH // RET_HEADS
RET_CHUNK = 128
ROPE_BASE = 10000.0
LRU_WIDTH = D_MODEL
LRU_BLOCKS = 8
LRU_BLOCK_DIM = LRU_WIDTH // LRU_BLOCKS
LRU_C = 8.0
CONV_WIDTH = 4
MIX_WIDTH = RET_WIDTH + LRU_WIDTH
IN_SPLITS = (RET_WIDTH, 2 * RET_WIDTH, 3 * RET_WIDTH, 4 * RET_WIDTH, 4 * RET_WIDTH + LRU_WIDTH)
IN_WIDTH = 4 * RET_WIDTH + 2 * LRU_WIDTH
NORM_EPS = 1e-6

kernel_name = "hybrid_retention_rglru_parallel_heads"


def rms_norm(x, g):
    x32 = x.astype(jnp.float32)
    y = x32 * lax.rsqrt(jnp.mean(x32 * x32, axis=-1, keepdims=True) + NORM_EPS)
    return (y * g.astype(jnp.float32)).astype(x.dtype)


def rotary(x):
    T, d = x.shape[1], x.shape[-1]
    inv_freq = ROPE_BASE ** (-jnp.arange(0, d, 2, dtype=jnp.float32) / d)
    ang = jnp.arange(T, dtype=jnp.float32)[:, None] * inv_freq[None, :]
    cos = jnp.cos(ang)[None, :, None, :]
    sin = jnp.sin(ang)[None, :, None, :]
    x1, x2 = x[..., : d // 2], x[..., d // 2:]
    return jnp.concatenate([x1 * cos - x2 * sin, x1 * sin + x2 * cos], axis=-1)


def chunkwise_retention(q, k, v):
    B, T, H, d = q.shape
    C = RET_CHUNK
    nc = T // C
    log_g = jnp.log1p(-jnp.exp2(-5.0 - jnp.arange(H, dtype=jnp.float32)))
    idx = jnp.arange(C, dtype=jnp.float32)
    rel = idx[:, None] - idx[None, :]
    decay_mat = jnp.where(rel[None] >= 0,
                          jnp.exp(jnp.maximum(rel, 0.0)[None] * log_g[:, None, None]), 0.0)
    zeta = jnp.exp((C - 1 - idx)[None, :] * log_g[:, None])
    xi = jnp.exp((idx + 1)[None, :] * log_g[:, None])
    chunk_decay = jnp.exp(C * log_g)

    qc = q.reshape(B, nc, C, H, d)
    kc = k.reshape(B, nc, C, H, d)
    vc = v.reshape(B, nc, C, H, d)

    scores = jnp.einsum('bnqhd,bnkhd->bnhqk', qc, kc) * decay_mat[None, None]
    intra = jnp.einsum('bnhqk,bnkhe->bnqhe', scores, vc)

    kv = jnp.einsum('bnkhd,hk,bnkhe->nbhde', kc, zeta, vc)

    def step(state, kv_n):
        new_state = chunk_decay[None, :, None, None] * state + kv_n
        return new_state, state

    _, prev_states = lax.scan(step, jnp.zeros((B, H, d, d), jnp.float32), kv)
    prev_states = jnp.moveaxis(prev_states, 0, 1)
    cross = jnp.einsum('bnqhd,bnhde->bnqhe', qc, prev_states) * xi.T[None, None, :, :, None]
    return (intra + cross).reshape(B, T, H, d)


def group_norm_heads(o):
    mu = jnp.mean(o, axis=-1, keepdims=True)
    var = jnp.mean(jnp.square(o - mu), axis=-1, keepdims=True)
    return (o - mu) * lax.rsqrt(var + NORM_EPS)


def causal_depthwise_conv(x, w, b):
    T = x.shape[1]
    xp = jnp.pad(x, ((0, 0), (CONV_WIDTH - 1, 0), (0, 0)))
    out = b[None, None, :]
    for j in range(CONV_WIDTH):
        out = out + xp[:, j:j + T, :] * w[j][None, None, :]
    return out


def block_diag_linear(x, w, b):
    B, T, _ = x.shape
    xb = x.reshape(B, T, LRU_BLOCKS, LRU_BLOCK_DIM)
    y = jnp.einsum('btni,nio->btno', xb, w) + b[None, None]
    return y.reshape(B, T, LRU_WIDTH)


def rg_lru(x, gate_a_w, gate_a_b, gate_x_w, gate_x_b, lru_lambda):
    r = jax.nn.sigmoid(block_diag_linear(x, gate_a_w, gate_a_b))
    i = jax.nn.sigmoid(block_diag_linear(x, gate_x_w, gate_x_b))
    log_a = -LRU_C * r * jax.nn.softplus(-lru_lambda)[None, None, :]
    a = jnp.exp(log_a)
    mult = jnp.sqrt(jnp.maximum(1.0 - jnp.exp(2.0 * log_a), 0.0))
    b = mult * (i * x)

    def combine(left, right):
        a_l, b_l = left
        a_r, b_r = right
        return a_l * a_r, a_r * b_l + b_r

    _, h = lax.associative_scan(combine, (a, b), axis=1)
    return h


def setup_inputs(seed: int = 0) -> dict:
    key = jax.random.key(seed)
    ks = jax.random.split(key, 14)
    f32 = jnp.float32
    x = jax.random.normal(ks[0], (BATCH, SEQ, D_MODEL), f32)
    norm_in_g = 1.0 + 0.05 * jax.random.normal(ks[1], (D_MODEL,), f32)
    w_in = jax.random.normal(ks[2], (D_MODEL, IN_WIDTH), f32) * D_MODEL ** -0.5
    conv_w = jax.random.normal(ks[3], (CONV_WIDTH, LRU_WIDTH), f32) * CONV_WIDTH ** -0.5
    conv_b = 0.01 * jax.random.normal(ks[4], (LRU_WIDTH,), f32)
    gate_a_w = jax.random.normal(ks[5], (LRU_BLOCKS, LRU_BLOCK_DIM, LRU_BLOCK_DIM), f32) * LRU_BLOCK_DIM ** -0.5
    gate_a_b = 0.01 * jax.random.normal(ks[6], (LRU_BLOCKS, LRU_BLOCK_DIM), f32)
    gate_x_w = jax.random.normal(ks[7], (LRU_BLOCKS, LRU_BLOCK_DIM, LRU_BLOCK_DIM), f32) * LRU_BLOCK_DIM ** -0.5
    gate_x_b = 0.01 * jax.random.normal(ks[8], (LRU_BLOCKS, LRU_BLOCK_DIM), f32)
    a_c = jax.random.uniform(ks[9], (LRU_WIDTH,), f32, minval=0.9, maxval=0.999)
    a0 = a_c ** (1.0 / LRU_C)
    lru_lambda = jnp.log(a0) - jnp.log1p(-a0)
    w_out = jax.random.normal(ks[10], (MIX_WIDTH, D_MODEL), f32) * MIX_WIDTH ** -0.5
    norm_out_g = 1.0 + 0.05 * jax.random.normal(ks[11], (D_MODEL,), f32)
    return {"x": x, "norm_in_g": norm_in_g, "w_in": w_in, "conv_w": conv_w, "conv_b": conv_b,
            "gate_a_w": gate_a_w, "gate_a_b": gate_a_b, "gate_x_w": gate_x_w, "gate_x_b": gate_x_b,
            "lru_lambda": lru_lambda, "w_out": w_out, "norm_out_g": norm_out_g}


def reference(x, norm_in_g, w_in, conv_w, conv_b, gate_a_w, gate_a_b, gate_x_w, gate_x_b,
              lru_lambda, w_out, norm_out_g):
    B, T, _ = x.shape
    f32 = jnp.float32
    h = x
    for _layer in range(DEPTH):
        xn = rms_norm(h, norm_in_g)
        proj = jnp.einsum('btd,de->bte', xn, w_in).astype(f32)
        q, k, v, g_ret, x_lru, g_lru = jnp.split(proj, IN_SPLITS, axis=-1)

        q = rotary(q.reshape(B, T, RET_HEADS, RET_HEAD_DIM))
        k = rotary(k.reshape(B, T, RET_HEADS, RET_HEAD_DIM)) * RET_HEAD_DIM ** -0.5
        v = v.reshape(B, T, RET_HEADS, RET_HEAD_DIM)
        o_ret = group_norm_heads(chunkwise_retention(q, k, v)).reshape(B, T, RET_WIDTH)
        y_ret = o_ret * jax.nn.silu(g_ret)

        xc = causal_depthwise_conv(x_lru, conv_w.astype(f32), conv_b.astype(f32))
        o_lru = rg_lru(xc, gate_a_w.astype(f32), gate_a_b.astype(f32), gate_x_w.astype(f32),
                       gate_x_b.astype(f32), lru_lambda.astype(f32))
        y_lru = o_lru * jax.nn.silu(g_lru)

        mixed = jnp.concatenate([y_ret, y_lru], axis=-1).astype(x.dtype)
        h = h + jnp.einsum('bte,ed->btd', mixed, w_out)
    return rms_norm(h, norm_out_g)
```

```python
import math
from contextlib import ExitStack

import numpy as np
import ml_dtypes

import concourse.bass as bass
import concourse.mybir as mybir
from concourse.bass_utils import run_bass_kernel_spmd

F32 = mybir.dt.float32
BF16 = mybir.dt.bfloat16
AF = mybir.ActivationFunctionType
ALU = mybir.AluOpType
AX = mybir.AxisListType

D_MODEL = 1024
NH = 8
NCB = 8
EPS = 1e-6


class Buf:
    __slots__ = ("name", "w", "rs")

    def __init__(self, name):
        self.name = name
        self.w = None
        self.rs = []


class _FakeIns:
    def __init__(self, name, out):
        self.name = name
        self.out = out


class _FakeEng:
    def __getattr__(self, name):
        def f(*a, **k):
            out = k.get("out", a[0] if a else None)
            return _FakeIns(name, out)
        return f


class _Op:
    __slots__ = ("idx", "eng", "fn", "dur", "kind", "preds", "succs", "npred", "ready", "start", "fin", "pos", "dsem", "dval")

    def __init__(self, idx, eng, fn, dur, kind):
        self.idx = idx
        self.eng = eng
        self.fn = fn
        self.dur = dur
        self.kind = kind
        self.preds = {}
        self.succs = []
        self.npred = 0
        self.ready = 0.0
        self.start = 0.0
        self.fin = 0.0
        self.pos = 0
        self.dsem = -1
        self.dval = 0


class Sched:
    ENG = ("pe", "act", "dve", "pool", "sp")
    LAT = 0.5

    def __init__(self, nc, stack, n_dma_sems=16):
        self.nc = nc
        self.sem = {e: stack.enter_context(nc.semaphore("s_" + e)) for e in self.ENG}
        self.dsem = [stack.enter_context(nc.semaphore("d%d" % i)) for i in range(n_dma_sems)]
        self.ops = []
        self.labels = None
        self.fake = _FakeEng()

    def _cost(self, eng, fn):
        try:
            info = fn(self.fake)
            ap = info.out
            cols = float(ap.free_size())
            nbytes = float(ap.nbytes())
        except Exception:
            cols, nbytes = 512.0, 65536.0
        if eng == "pe":
            return 0.07 + cols / 2400.0
        if eng == "dve":
            return 0.07 + cols / 960.0
        if eng == "act":
            return 0.2 + cols / 1200.0
        if eng == "pool":
            return 0.25 + cols / 420.0
        return 2.0 + nbytes / 150e3

    def _add(self, eng, fn, reads, writes, kind):
        op = _Op(len(self.ops), eng, fn, self._cost(eng if kind == "op" else "sp", fn), kind)
        preds = op.preds

        def add(p, wait, why=None):
            if p is op:
                return
            preds[p] = preds.get(p, False) or wait
            if self.labels is not None:
                self.labels[(p.idx, op.idx)] = why
        for b in reads:
            if b.w is not None:
                add(b.w, True, "RAW " + b.name)
            if b.name.startswith(("bk", "tp")):
                for r in b.rs:
                    if r.eng != eng or r.kind != "op":
                        add(r, True, "RR " + b.name)
        for b in writes:
            if b.w is not None:
                add(b.w, not (eng == "pe" and b.w.eng == "pe" and b.w.kind == "op" and kind == "op"), "WAW " + b.name)
            for r in b.rs:
                add(r, not (eng == "pe" and r.eng == "pe" and r.kind == "op" and kind == "op"), "WAR " + b.name)
        for b in reads:
            b.rs.append(op)
        for b in writes:
            b.w = op
            b.rs = []
        for p in preds:
            p.succs.append(op)
        op.npred = len(preds)
        self.ops.append(op)
        return op

    def op(self, eng, fn, reads=(), writes=()):
        return self._add(eng, fn, reads, writes, "op")

    def dma(self, eng, fn, reads=(), writes=()):
        return self._add("sp", fn, reads, writes, "dma")

    PRIO_WINDOW = 0.3

    def _schedule(self):
        import heapq
        bl = [0.0] * len(self.ops)
        for op in reversed(self.ops):
            m = 0.0
            for sx in op.succs:
                v = bl[sx.idx] + (self.LAT if sx.eng != op.eng else 0.05)
                if v > m:
                    m = v
            bl[op.idx] = m + op.dur
        W = self.PRIO_WINDOW
        ready = {e: [] for e in self.ENG}
        free = {e: 0.0 for e in self.ENG}
        order = {e: [] for e in self.ENG}
        for op in self.ops:
            if op.npred == 0:
                ready[op.eng].append(op)
        remaining = len(self.ops)
        while remaining:
            best = None
            for e in self.ENG:
                lst = ready[e]
                if not lst:
                    continue
                t0 = min(max(free[e], o.ready) for o in lst)
                cand = None
                for o in lst:
                    st = max(free[e], o.ready)
                    if st <= t0 + W:
                        k = (-bl[o.idx], o.idx) if W > 0 else (st, o.idx)
                        if cand is None or k < cand[0]:
                            cand = (k, o, st)
                key = (t0, cand[1].idx)
                if best is None or key < best[0]:
                    best = (key, e, cand[1], cand[2])
            assert best is not None, "scheduler deadlock"
            _, e, op, st = best
            ready[e].remove(op)
            op.start = st
            if op.kind == "dma":
                free[e] = op.start + 0.06
                op.fin = op.start + op.dur
            else:
                op.fin = op.start + op.dur
                free[e] = op.fin
            op.pos = len(order[e])
            order[e].append(op)
            remaining -= 1
            for sx in op.succs:
                sx.npred -= 1
                if not sx.preds[op]:
                    r = op.start
                else:
                    r = op.fin + (self.LAT if (sx.eng != op.eng or op.kind == "dma") else 0.2)
                if r > sx.ready:
                    sx.ready = r
                if sx.npred == 0:
                    ready[sx.eng].append(sx)
        self.order = order
        self.est_us = max(op.fin for op in self.ops)

    def emit(self):
        self._schedule()
        order = self.order
        nd = len(self.dsem)
        cnt = 0
        dcnt = [0] * nd
        dlast = [None] * nd
        extra = {}
        k = 0
        for op in order["sp"]:
            if op.kind == "dma":
                i = k % nd
                k += 1
                dcnt[i] += 16
                op.dsem, op.dval = i, dcnt[i]
                if dlast[i] is not None:
                    extra[op] = dlast[i]
                dlast[i] = op
        idx_on = {}
        for e in self.ENG:
            c = 0
            for op in order[e]:
                if op.kind == "op":
                    c += 1
                    idx_on[op] = c
        prog = {e: [] for e in self.ENG}
        for e in self.ENG:
            seen = {}
            for op in order[e]:
                waits = [p for p, w in op.preds.items() if w]
                if op in extra:
                    waits.append(extra[op])
                for p in waits:
                    if p.kind == "dma":
                        key, val, sem = ("d", p.dsem), p.dval, self.dsem[p.dsem]
                    else:
                        key, val, sem = p.eng, idx_on[p], self.sem[p.eng]
                    if seen.get(key, 0) >= val:
                        continue
                    seen[key] = val
                    prog[e].append(("w", sem, val))
                if op.kind == "dma":
                    prog[e].append(("o", op.fn, self.dsem[op.dsem], 16))
                else:
                    prog[e].append(("o", op.fn, self.sem[e], 1))
            if e == "sp":
                for i in range(nd):
                    if dcnt[i] > 0 and seen.get(("d", i), 0) < dcnt[i]:
                        prog[e].append(("w", self.dsem[i], dcnt[i]))
        nc = self.nc

        def replay(en, items):
            for it in items:
                if it[0] == "w":
                    en.wait_ge(it[1], it[2])
                else:
                    it[1](en).then_inc(it[2], it[3])

        with nc.Block() as block:
            @block.tensor
            def _(en):
                replay(en, prog["pe"])

            @block.scalar
            def _(en):
                replay(en, prog["act"])

            @block.vector
            def _(en):
                replay(en, prog["dve"])

            @block.gpsimd
            def _(en):
                replay(en, prog["pool"])

            @block.sync
            def _(en):
                replay(en, prog["sp"])


NWS = 2


def build_program(NT):
    assert NT % 4 == 0
    NB = NT // 4
    TH = NT * 128
    log_g = [math.log1p(-2.0 ** (-5.0 - h)) for h in range(NH)]
    GAM = [math.exp(128.0 * lg) for lg in log_g]

    nc = bass.Bass("TRN2", target_bir_lowering=False)
    din = lambda n, s, d=F32: nc.dram_tensor(n, s, d, kind="ExternalInput").ap()
    x_main = din("x_main", [TH, D_MODEL])
    x_pre = din("x_pre", [TH, D_MODEL])
    tab_main = din("tab_main", [TH, 256])
    tab_pre = din("tab_pre", [TH, 256])
    w_in = din("w_in", [D_MODEL, 6 * D_MODEL])
    w_out = din("w_out", [2 * D_MODEL, D_MODEL])
    gin_d = din("gin", [128, 8])
    gout_d = din("gout", [128, D_MODEL])
    convw_d = din("convw", [128, 8, 4])
    convb_d = din("convb", [128, 8])
    gaw_d = din("gaw", [8, 128, 128])
    gxw_d = din("gxw", [8, 128, 128])
    gab_d = din("gab", [128, 8])
    gxb_d = din("gxb", [128, 8])
    lam_d = din("lam", [128, 8])
    flag_d = din("flag", [128, 1])
    ident_d = din("ident", [128, 128], BF16)
    causal_d = din("causal", [128, 128])
    cdec_d = din("cdec", [128, 8])
    xibc_d = din("xibc", [128, 8, 128])
    zeta_d = din("zeta", [128, 8])
    zpre_d = din("zpre", [128, NT, 8])
    out_d = nc.dram_tensor("out", [TH, D_MODEL], F32, kind="ExternalOutput").ap()

    with ExitStack() as st:
        S = Sched(nc, st)
        sb = lambda n, s, d=F32: st.enter_context(nc.sbuf_tensor("sb_" + n, s, d))
        ps = lambda n, s, d=F32: st.enter_context(nc.psum_tensor("ps_" + n, s, d))

        Wt = sb("W", [128, 32768], BF16)
        W3 = Wt[:].rearrange("p (k c) -> p k c", c=4096)
        Wf = Wt[:].rearrange("p (k c) -> p k c", c=1024)
        GW = sb("GW", [128, 2, 8, 128], BF16)
        YR = sb("YR", [128, 8, TH], BF16)
        xt = [sb("xt%d" % i, [128, 1024]) for i in range(2)]
        tab = [sb("tab%d" % i, [128, 256]) for i in range(2)]
        xs = sb("xs", [128, 1024], BF16)
        junk = xs
        xnT = sb("xnT", [128, 8, 512], BF16)
        RWall = sb("RWall", [128, max(8192, 3072 * NWS)], BF16)
        YL = RWall[:, 0:4096].rearrange("p (c t) -> p c t", c=8)

        def mkset(i):
            d = {"i": i}
            v4 = lambda k: RWall[:, i * 3072 + k * 512: i * 3072 + (k + 1) * 512].rearrange("p (h d) -> p h d", h=4)
            d["qr"], d["vb"], d["qT"], d["kT"], d["Pm"], d["ybf"] = v4(0), v4(1), v4(2), v4(3), v4(4), v4(5)
            d["kr"] = sb("kr%d" % i, [128, 4, 128], BF16)[:]
            d["vh"] = sb("vh%d" % i, [128, 4, 128], BF16)[:]
            F1 = sb("F1_%d" % i, [128, 512])
            F2 = sb("F2_%d" % i, [128, 512])
            a0 = sb("acc0_%d" % i, [128, 512])
            a1 = sb("acc1_%d" % i, [128, 512])
            xl_ = sb("xl_%d" % i, [128, 515])
            d["xcb"] = sb("xcb_%d" % i, [128, 512], BF16)[:]
            d["stt"] = sb("stt_%d" % i, [128, 32])
            d["t1"] = F1[:].rearrange("p (h d) -> p h d", h=4)
            d["t2"] = F2[:].rearrange("p (h d) -> p h d", h=4)
            d["sg"] = a0[:].rearrange("p (h d) -> p h d", h=4)
            d["thi"], d["a2"], d["hh"], d["av"] = F1[:], F2[:], F2[:], a0[:]
            d["acc"] = [a0[:], a1[:]]
            d["xl"] = xl_[:]
            d["thr"] = xl_[:, 0:512]
            return d

        WS = [mkset(i_) for i_ in range(NWS)]
        S32 = sb("S32", [128, 8, 128])
        Sbf = sb("Sbf", [128, 8, 128], BF16)
        hist = sb("hist", [128, 8, 3])
        hstate = sb("hstate", [128, 8])
        GX = sb("GX", [128, 1024])
        xibc = GX[:].rearrange("p (h d) -> p h d", h=8)
        gout = GX[:]
        causal = sb("causal", [128, 128])
        cdec = sb("cdec", [128, 8])
        ident = sb("ident", [128, 128], BF16)
        gin = sb("gin", [128, 8])
        gink = sb("gink", [128, 8])
        convw = sb("convw", [128, 8, 4])
        convb = sb("convb", [128, 8])
        hba = sb("hba", [128, 8])
        hbx = sb("hbx", [128, 8])
        lam = sb("lam", [128, 8])
        nsp = sb("nsp", [128, 8])
        hnsp = sb("hnsp", [128, 8])
        zeta = sb("zeta", [128, 8])
        zpre = sb("zpre", [128, NT, 8])
        flag = sb("flag", [128, 1])
        neghalf = sb("neghalf", [128, 8])
        ss = sb("ss", [128, 4])
        rstd = sb("rstd", [128, 4])
        rstd_all = sb("rstd_all", [128, NT])
        fz = sb("fz", [128, 2])
        sixteenth = sb("sixteenth", [128, 1])

        tpA = ps("tpA", [128, 8, 128], BF16)
        tpB = ps("tpB", [128, 8, 128], BF16)
        BK = [ps("bk%d" % i, [128, 512]) for i in range(6)]

        B = {}

        def bf(name):
            if name not in B:
                B[name] = Buf(name)
            return B[name]

        def wb(w, name):
            return bf("%s_%d" % (name, w["i"]))

        def fence(old, new):
            S.op("pool", lambda e: e.memset(fz[:, 0:1], 0.0), reads=[bf(n) for n in old], writes=[bf(n) for n in new] + [bf("fz")])

        def small_load(dst, src, name):
            S.dma("sp", lambda e: e.dma_start(out=dst, in_=src), writes=[bf(name)])

        small_load(gin[:], gin_d[:, :], "gin")
        small_load(convw[:], convw_d[:, :, :], "convw")
        small_load(convb[:], convb_d[:, :], "convb")
        small_load(hba[:], gab_d[:, :], "hba")
        small_load(hbx[:], gxb_d[:, :], "hbx")
        small_load(lam[:], lam_d[:, :], "lam")
        small_load(flag[:], flag_d[:, :], "flag")
        small_load(ident[:], ident_d[:, :], "ident")
        small_load(causal[:], causal_d[:, :], "causal")
        small_load(cdec[:], cdec_d[:, :], "cdec")
        small_load(xibc, xibc_d[:, :, :], "GX")
        small_load(zeta[:], zeta_d[:, :], "zeta")
        small_load(zpre[:], zpre_d[:, :, :], "zpre")

        S.op("dve", lambda e: e.memset(neghalf[:], -0.5), writes=[bf("neghalf")])
        S.op("dve", lambda e: e.memset(sixteenth[:], 1.0 / 16.0), writes=[bf("sixteenth")])
        S.op("dve", lambda e: e.memset(S32[:], 0.0), writes=[bf("S32_0"), bf("S32_1")])
        S.op("dve", lambda e: e.memset(hist[:], 0.0), writes=[bf("hist")])
        S.op("dve", lambda e: e.memset(hstate[:], 0.0), writes=[bf("hstate%d" % c_) for c_ in range(8)])
        S.op("dve", lambda e: e.tensor_scalar(out=hba[:], in0=hba[:], scalar1=0.5, scalar2=None, op0=ALU.mult),
             reads=[bf("hba")], writes=[bf("hba")])
        S.op("dve", lambda e: e.tensor_scalar(out=hbx[:], in0=hbx[:], scalar1=0.5, scalar2=None, op0=ALU.mult),
             reads=[bf("hbx")], writes=[bf("hbx")])
        S.op("dve", lambda e: e.tensor_scalar(out=gink[:], in0=gin[:], scalar1=128.0 ** -0.5, scalar2=None, op0=ALU.mult),
             reads=[bf("gin")], writes=[bf("gink")])
        S.op("act", lambda e: e.activation(out=nsp[:], in_=lam[:], func=AF.Exp, scale=-1.0),
             reads=[bf("lam")], writes=[bf("nsp")])
        S.op("act", lambda e: e.activation(out=hnsp[:], in_=nsp[:], func=AF.Ln, bias=1.0),
             reads=[bf("nsp")], writes=[bf("hnsp")])
        S.op("dve", lambda e: e.tensor_scalar(out=nsp[:], in0=hnsp[:], scalar1=-8.0, scalar2=None, op0=ALU.mult),
             reads=[bf("hnsp")], writes=[bf("nsp")])
        S.op("dve", lambda e: e.tensor_scalar(out=hnsp[:], in0=hnsp[:], scalar1=-4.0, scalar2=None, op0=ALU.mult),
             reads=[bf("hnsp"), bf("nsp")], writes=[bf("hnsp")])

        for gi_, gd in enumerate((gaw_d, gxw_d)):
            S.dma("sp", lambda e, gd=gd, gi_=gi_: e.dma_start(out=xt[gi_][:].rearrange("p (n o) -> p n o", n=8),
                                                              in_=gd.rearrange("n i o -> i n o")),
                  writes=[bf("xt%d" % gi_)])
            S.op("dve", lambda e, gi_=gi_: e.tensor_copy(out=GW[:, gi_, :, :],
                                                         in_=xt[gi_][:].rearrange("p (n o) -> p n o", n=8)),
                 reads=[bf("xt%d" % gi_)], writes=[bf("GW")])

        wl_cnt = [0]

        def load_w(chunk, dst, src, r0, c0, scale_ap=None, scale_name=None):
            i = wl_cnt[0] % 2
            wl_cnt[0] += 1
            S.dma("sp", lambda e: e.dma_start(out=xt[i][:], in_=src[r0:r0 + 128, c0:c0 + 1024]),
                  writes=[bf("xt%d" % i)])
            eng = "dve" if (wl_cnt[0] % 2 == 0) else "act"
            if scale_ap is None:
                if eng == "dve":
                    S.op(eng, lambda e: e.tensor_copy(out=dst, in_=xt[i][:]), reads=[bf("xt%d" % i)], writes=[bf("Wc%d" % chunk)])
                else:
                    S.op(eng, lambda e: e.activation(out=dst, in_=xt[i][:], func=AF.Copy), reads=[bf("xt%d" % i)], writes=[bf("Wc%d" % chunk)])
            else:
                if eng == "dve":
                    S.op(eng, lambda e: e.tensor_scalar(out=dst, in0=xt[i][:], scalar1=scale_ap, scalar2=None, op0=ALU.mult),
                         reads=[bf("xt%d" % i), bf(scale_name)], writes=[bf("Wc%d" % chunk)])
                else:
                    S.op(eng, lambda e: e.activation(out=dst, in_=xt[i][:], func=AF.Copy, scale=scale_ap),
                         reads=[bf("xt%d" % i), bf(scale_name)], writes=[bf("Wc%d" % chunk)])

        for kc in range(8):
            load_w(kc * 4 + 0, W3[:, kc, 0:1024], w_in, kc * 128, 1024, gink[:, kc:kc + 1], "gink")
        for kc in range(8):
            load_w(kc * 4 + 1, W3[:, kc, 1024:2048], w_in, kc * 128, 2048, gin[:, kc:kc + 1], "gin")
        for kc in range(8):
            load_w(kc * 4 + 2, W3[:, kc, 2048:3072], w_in, kc * 128, 4096, gin[:, kc:kc + 1], "gin")

        def front_g(src, tabsrc, t, slot, j, need_tab, mode="P"):
            bx = bf("xt%d" % slot)
            S.dma("sp", lambda e: e.dma_start(out=xt[slot][:], in_=src[t * 128:(t + 1) * 128, :]), writes=[bx])
            if need_tab:
                S.dma("sp", lambda e: e.dma_start(out=tab[slot][:], in_=tabsrc[t * 128:(t + 1) * 128, :]),
                      writes=[bf("tab%d" % slot)])
            if mode == "P":
                rs, rsb = rstd[:, 0:1], bf("rstd")
            else:
                rs, rsb = rstd_all[:, t:t + 1], bf("rsa%d" % t)
            if mode != "L":
                S.op("pool", lambda e: e.memset(ss[:, 0:1], 0.0), writes=[bf("ss")])
                S.op("act", lambda e: e.activation(out=junk[:], in_=xt[slot][:], func=AF.Square, accum_out=ss[:, 0:1]),
                     reads=[bx, bf("ss")], writes=[bf("xs"), bf("ss")])
                yield
                S.op("dve", lambda e: e.tensor_scalar(out=rs, in0=ss[:, 0:1], scalar1=1.0 / D_MODEL, scalar2=EPS,
                                                      op0=ALU.mult, op1=ALU.add), reads=[bf("ss")], writes=[rsb])
                S.op("pool", lambda e: e.tensor_tensor(out=rs, in0=rs, in1=neghalf[:, 0:1], op=ALU.pow),
                     reads=[rsb, bf("neghalf")], writes=[rsb])
                yield
            S.op("act", lambda e: e.activation(out=xs[:], in_=xt[slot][:], func=AF.Copy, scale=rs),
                 reads=[bx, rsb], writes=[bf("xs")])
            yield
            for kc in range(8):
                S.op("pe", lambda e, kc=kc: e.transpose(out=tpA[:, kc, :], in_=xs[:, kc * 128:(kc + 1) * 128],
                                                        identity=ident[:]),
                     reads=[bf("xs"), bf("ident")], writes=[bf("tpA")])
            yield
            S.op("dve", lambda e: e.tensor_copy(out=xnT[:, :, j * 128:(j + 1) * 128], in_=tpA[:]),
                 reads=[bf("tpA")], writes=[bf("xnT%d" % j)])

        def run(*gens):
            gens = [g for g in gens if g is not None]
            while gens:
                for g in list(gens):
                    try:
                        next(g)
                    except StopIteration:
                        gens.remove(g)

        def front(*a):
            run(front_g(*a))

        def proj_tm(bank, j, wcol):
            for kc in range(8):
                S.op("pe", lambda e, kc=kc: e.matmul(BK[bank][:], lhsT=xnT[:, kc, j * 128:(j + 1) * 128],
                                                     rhs=W3[:, kc, wcol:wcol + 512], start=(kc == 0), stop=(kc == 7)),
                     reads=[bf("xnT%d" % j), bf("Wc%d" % (kc * 4 + wcol // 1024))], writes=[bf("bk%d" % bank)])

        def rotary(bank, slot, w, dst, dst_buf):
            src = BK[bank][:].rearrange("p (h d) -> p h d", h=4)
            tb = bf("tab%d" % slot)
            bb = bf("bk%d" % bank)
            t1, t2 = w["t1"], w["t2"]
            S.op("dve", lambda e: e.tensor_tensor(out=t1, in0=src,
                                                  in1=tab[slot][:, 0:128].unsqueeze(1).broadcast_to([128, 4, 128]),
                                                  op=ALU.mult), reads=[bb, tb], writes=[wb(w, "F1")])
            S.op("dve", lambda e: e.tensor_tensor(out=t2[:, :, 0:64], in0=src[:, :, 64:128],
                                                  in1=tab[slot][:, 128:192].unsqueeze(1).broadcast_to([128, 4, 64]),
                                                  op=ALU.mult), reads=[bb, tb], writes=[wb(w, "F2")])
            S.op("dve", lambda e: e.tensor_tensor(out=t2[:, :, 64:128], in0=src[:, :, 0:64],
                                                  in1=tab[slot][:, 192:256].unsqueeze(1).broadcast_to([128, 4, 64]),
                                                  op=ALU.mult), reads=[bb, tb], writes=[wb(w, "F2")])
            S.op("pool", lambda e: e.tensor_tensor(out=dst, in0=t1, in1=t2, op=ALU.add),
                 reads=[wb(w, "F1"), wb(w, "F2")], writes=[dst_buf])

        def v_evac(bank, hp, w, plain, zt=None):
            bb = bf("bk%d" % bank)
            src = BK[bank][:].rearrange("p (h d) -> p h d", h=4)
            if plain:
                S.op("act", lambda e: e.activation(out=w["vb"], in_=src, func=AF.Copy), reads=[bb], writes=[wb(w, "vb")])
            zap = zeta[:, hp * 4:hp * 4 + 4] if zt is None else zpre[:, zt, hp * 4:hp * 4 + 4]
            S.op("dve", lambda e: e.tensor_tensor(out=w["vh"], in0=src,
                                                  in1=zap.unsqueeze(2).broadcast_to([128, 4, 128]),
                                                  op=ALU.mult), reads=[bb, bf("zeta"), bf("zpre")], writes=[wb(w, "vh")])

        def kv_update(hp, w, make_bf):
            kr, vh = w["kr"], w["vh"]
            for hl in range(4):
                S.op("pe", lambda e, hl=hl: e.matmul(BK[4][:, hl * 128:(hl + 1) * 128], lhsT=kr[:, hl, :], rhs=vh[:, hl, :],
                                                     start=True, stop=True),
                     reads=[wb(w, "kr"), wb(w, "vh")], writes=[bf("bk4")])
            for hl in range(4):
                h = hp * 4 + hl
                S.op("dve", lambda e, hl=hl, h=h: e.scalar_tensor_tensor(out=S32[:, h, :], in0=S32[:, h, :], scalar=float(GAM[h]),
                                                                         in1=BK[4][:, hl * 128:(hl + 1) * 128],
                                                                         op0=ALU.mult, op1=ALU.add),
                     reads=[bf("S32_%d" % hp), bf("bk4")], writes=[bf("S32_%d" % hp)])
            if make_bf:
                S.op("act", lambda e: e.activation(out=Sbf[:, hp * 4:hp * 4 + 4, :], in_=S32[:, hp * 4:hp * 4 + 4, :], func=AF.Copy),
                     reads=[bf("S32_%d" % hp)], writes=[bf("Sbf%d" % hp)])

        xall = [bf("xnT%d" % j) for j in range(4)]

        def lruA(cb, w, full, xlw, glw):
            pa = cb % 2
            pg = 2 + (cb % 2)
            xl, acc = w["xl"], w["acc"]
            for kc in range(8):
                S.op("pe", lambda e, kc=kc: e.matmul(BK[pa][:], lhsT=xlw(kc, cb)[0], rhs=xnT[:, kc, :], start=(kc == 0), stop=(kc == 7)),
                     reads=[bf("Wc%d" % xlw(kc, cb)[1])] + xall, writes=[bf("bk%d" % pa)])
            yield
            if full:
                for kc in range(8):
                    S.op("pe", lambda e, kc=kc: e.matmul(BK[pg][:], lhsT=glw(kc, cb)[0], rhs=xnT[:, kc, :], start=(kc == 0), stop=(kc == 7)),
                         reads=[bf("Wc%d" % glw(kc, cb)[1])] + xall, writes=[bf("bk%d" % pg)])
                yield
            S.op("pool", lambda e: e.tensor_copy(out=xl[:, 0:3], in_=hist[:, cb, :]), reads=[bf("hist")], writes=[wb(w, "xl")])
            if full:
                S.op("act", lambda e: e.activation(out=xl[:, 3:515], in_=BK[pa][:], func=AF.Copy),
                     reads=[bf("bk%d" % pa)], writes=[wb(w, "xl")])
            else:
                S.op("dve", lambda e: e.tensor_copy(out=xl[:, 3:515], in_=BK[pa][:]),
                     reads=[bf("bk%d" % pa)], writes=[wb(w, "xl")])
            S.op("pool", lambda e: e.tensor_copy(out=hist[:, cb, :], in_=xl[:, 512:515]), reads=[wb(w, "xl")], writes=[bf("hist")])
            yield
            S.op("act", lambda e: e.activation(out=acc[0], in_=xl[:, 3:515], func=AF.Identity,
                                               scale=convw[:, cb, 3:4], bias=convb[:, cb:cb + 1]),
                 reads=[wb(w, "xl"), bf("convw"), bf("convb")], writes=[wb(w, "acc0")])
            yield
            for n_, j_ in enumerate((2, 1, 0)):
                src_i = n_ % 2
                dst_i = 1 - src_i
                S.op("dve", lambda e, j_=j_, src_i=src_i, dst_i=dst_i: e.scalar_tensor_tensor(
                    out=acc[dst_i], in0=xl[:, j_:j_ + 512], scalar=convw[:, cb, j_:j_ + 1], in1=acc[src_i],
                    op0=ALU.mult, op1=ALU.add),
                    reads=[wb(w, "xl"), bf("convw"), wb(w, "acc%d" % src_i)], writes=[wb(w, "acc%d" % dst_i)])
                yield
            S.op("act", lambda e: e.activation(out=w["xcb"], in_=acc[1], func=AF.Copy), reads=[wb(w, "acc1")], writes=[wb(w, "xcb")])

        def lruB(cb, w, full):
            pg = 2 + (cb % 2)
            xc, thr, thi, av, a2, hh, xcb = w["acc"][1], w["thr"], w["thi"], w["av"], w["a2"], w["hh"], w["xcb"]
            bxc = wb(w, "acc1")
            S.op("pe", lambda e: e.matmul(BK[4][:], lhsT=GW[:, 0, cb, :], rhs=xcb, start=True, stop=True),
                 reads=[bf("GW"), wb(w, "xcb")], writes=[bf("bk4")])
            S.op("pe", lambda e: e.matmul(BK[5][:], lhsT=GW[:, 1, cb, :], rhs=xcb, start=True, stop=True),
                 reads=[bf("GW"), wb(w, "xcb")], writes=[bf("bk5")])
            yield
            S.op("act", lambda e: e.activation(out=thr, in_=BK[4][:], func=AF.Tanh, scale=0.5, bias=hba[:, cb:cb + 1]),
                 reads=[bf("bk4"), bf("hba")], writes=[wb(w, "xl")])
            S.op("act", lambda e: e.activation(out=thi, in_=BK[5][:], func=AF.Tanh, scale=0.5, bias=hbx[:, cb:cb + 1]),
                 reads=[bf("bk5"), bf("hbx")], writes=[wb(w, "F1")])
            yield
            S.op("act", lambda e: e.activation(out=av, in_=thr, func=AF.Exp, scale=hnsp[:, cb:cb + 1], bias=hnsp[:, cb:cb + 1]),
                 reads=[wb(w, "xl"), bf("hnsp")], writes=[wb(w, "acc0")])
            S.op("dve", lambda e: e.tensor_tensor(out=a2, in0=av, in1=av, op=ALU.mult),
                 reads=[wb(w, "acc0")], writes=[wb(w, "F2")])
            yield
            S.op("dve", lambda e: e.tensor_scalar(out=a2, in0=a2, scalar1=1.0, scalar2=-1.0 / 16.0, op0=ALU.min, op1=ALU.mult),
                 reads=[wb(w, "F2")], writes=[wb(w, "F2")])
            yield
            S.op("act", lambda e: e.activation(out=a2, in_=a2, func=AF.Sqrt, bias=sixteenth[:, 0:1]),
                 reads=[wb(w, "F2"), bf("sixteenth")], writes=[wb(w, "F2")])
            yield
            S.op("dve", lambda e: e.scalar_tensor_tensor(out=thi, in0=thi, scalar=1.0, in1=xc, op0=ALU.add, op1=ALU.mult),
                 reads=[wb(w, "F1"), bxc], writes=[wb(w, "F1")])
            S.op("dve", lambda e: e.tensor_tensor(out=thi, in0=thi, in1=a2, op=ALU.mult),
                 reads=[wb(w, "F1"), wb(w, "F2")], writes=[wb(w, "F1")])
            yield
            S.op("dve", lambda e: e.tensor_tensor_scan(out=hh, data0=av, data1=thi, initial=hstate[:, cb:cb + 1],
                                                       op0=ALU.mult, op1=ALU.add),
                 reads=[wb(w, "acc0"), wb(w, "F1"), bf("hstate%d" % cb)], writes=[wb(w, "F2")])
            S.op("pool", lambda e: e.tensor_copy(out=hstate[:, cb:cb + 1], in_=hh[:, 511:512]),
                 reads=[wb(w, "F2")], writes=[bf("hstate%d" % cb)])
            yield
            if full:
                S.op("act", lambda e: e.activation(out=thr, in_=BK[pg][:], func=AF.Tanh, scale=0.5),
                     reads=[bf("bk%d" % pg)], writes=[wb(w, "xl")])
                yield
                S.op("dve", lambda e: e.scalar_tensor_tensor(out=thr, in0=thr, scalar=1.0, in1=BK[pg][:], op0=ALU.add, op1=ALU.mult),
                     reads=[wb(w, "xl"), bf("bk%d" % pg)], writes=[wb(w, "xl")])
                S.op("dve", lambda e: e.tensor_tensor(out=YL[:, cb, :], in0=thr, in1=hh, op=ALU.mult),
                     reads=[wb(w, "xl"), wb(w, "F2")], writes=[bf("YL")])

        def lru_block(full, xlw, glw=None):
            run(lruA(0, WS[0], full, xlw, glw))
            for cb in range(8):
                run(lruB(cb, WS[cb % NWS], full),
                    lruA(cb + 1, WS[(cb + 1) % NWS], full, xlw, glw) if cb + 1 < 8 else None)

        def retA(t, hp, w, slot, j):
            proj_tm(0, j, 2048 + hp * 512)
            yield
            rotary(0, slot, w, w["qr"], wb(w, "qr"))
            yield
            proj_tm(1, j, 0 + hp * 512)
            yield
            rotary(1, slot, w, w["kr"], wb(w, "kr"))
            yield
            proj_tm(2, j, 1024 + hp * 512)
            yield
            v_evac(2, hp, w, True)
            yield
            proj_tm(2, j, 3072 + hp * 512)
            yield
            S.op("act", lambda e: e.activation(out=w["sg"], in_=BK[2][:].rearrange("p (h d) -> p h d", h=4), func=AF.Silu),
                 reads=[bf("bk2")], writes=[wb(w, "acc0")])

        def retB(t, hp, w):
            qr, kr, vb, qT, kT, Pm, ybf, sg, t1, t2 = (w[k] for k in ("qr", "kr", "vb", "qT", "kT", "Pm", "ybf", "sg", "t1", "t2"))
            stt = w["stt"]
            bst = wb(w, "stt")
            for hl in range(4):
                S.op("pe", lambda e, hl=hl: e.transpose(out=tpB[:, hl, :], in_=qr[:, hl, :], identity=ident[:]),
                     reads=[wb(w, "qr"), bf("ident")], writes=[bf("tpB")])
            for hl in range(4):
                S.op("pe", lambda e, hl=hl: e.transpose(out=tpB[:, 4 + hl, :], in_=kr[:, hl, :], identity=ident[:]),
                     reads=[wb(w, "kr"), bf("ident")], writes=[bf("tpB")])
            yield
            S.op("dve", lambda e: e.tensor_tensor(out=qT, in0=tpB[:, 0:4, :], in1=xibc[:, hp * 4:hp * 4 + 4, :], op=ALU.mult),
                 reads=[bf("tpB"), bf("GX")], writes=[wb(w, "qT")])
            S.op("act", lambda e: e.activation(out=kT, in_=tpB[:, 4:8, :], func=AF.Copy),
                 reads=[bf("tpB")], writes=[wb(w, "kT")])
            yield
            for hl in range(4):
                S.op("pe", lambda e, hl=hl: e.matmul(BK[5][:, hl * 128:(hl + 1) * 128], lhsT=kT[:, hl, :], rhs=qT[:, hl, :],
                                                     start=True, stop=True),
                     reads=[wb(w, "kT"), wb(w, "qT")], writes=[bf("bk5")])
            yield
            for hl in range(4):
                h = hp * 4 + hl
                S.op("dve", lambda e, hl=hl, h=h: e.scalar_tensor_tensor(out=Pm[:, hl, :], in0=BK[5][:, hl * 128:(hl + 1) * 128],
                                                                         scalar=cdec[:, h:h + 1], in1=causal[:],
                                                                         op0=ALU.mult, op1=ALU.mult),
                     reads=[bf("bk5"), bf("cdec"), bf("causal")], writes=[wb(w, "Pm")])
            yield
            for hl in range(4):
                h = hp * 4 + hl
                S.op("pe", lambda e, hl=hl: e.matmul(BK[3][:, hl * 128:(hl + 1) * 128], lhsT=Pm[:, hl, :], rhs=vb[:, hl, :],
                                                     start=True, stop=False),
                     reads=[wb(w, "Pm"), wb(w, "vb")], writes=[bf("bk3")])
                S.op("pe", lambda e, hl=hl, h=h: e.matmul(BK[3][:, hl * 128:(hl + 1) * 128], lhsT=qT[:, hl, :], rhs=Sbf[:, h, :],
                                                          start=False, stop=True),
                     reads=[wb(w, "qT"), bf("Sbf%d" % hp)], writes=[bf("bk3")])
            yield
            kv_update(hp, w, True)
            yield
            po = BK[3][:].rearrange("p (h d) -> p h d", h=4)
            S.op("dve", lambda e: e.tensor_reduce(out=stt[:, 0:4], in_=po, axis=AX.X, op=ALU.add), reads=[bf("bk3")], writes=[bst])
            S.op("act", lambda e: e.activation(out=t2, in_=po, func=AF.Square), reads=[bf("bk3")], writes=[wb(w, "F2")])
            yield
            S.op("dve", lambda e: e.tensor_reduce(out=stt[:, 4:8], in_=t2, axis=AX.X, op=ALU.add),
                 reads=[wb(w, "F2"), bst], writes=[bst])
            S.op("dve", lambda e: e.tensor_scalar(out=stt[:, 8:12], in0=stt[:, 0:4], scalar1=1.0 / 128.0, scalar2=None, op0=ALU.mult),
                 reads=[bst], writes=[bst])
            S.op("dve", lambda e: e.tensor_tensor(out=stt[:, 12:16], in0=stt[:, 8:12], in1=stt[:, 8:12], op=ALU.mult),
                 reads=[bst], writes=[bst])
            S.op("dve", lambda e: e.scalar_tensor_tensor(out=stt[:, 16:20], in0=stt[:, 4:8], scalar=1.0 / 128.0, in1=stt[:, 12:16],
                                                         op0=ALU.mult, op1=ALU.subtract), reads=[bst], writes=[bst])
            S.op("dve", lambda e: e.tensor_scalar(out=stt[:, 16:20], in0=stt[:, 16:20], scalar1=0.0, scalar2=EPS, op0=ALU.max, op1=ALU.add),
                 reads=[bst], writes=[bst])
            yield
            S.op("pool", lambda e: e.tensor_tensor(out=stt[:, 20:24], in0=stt[:, 16:20], in1=neghalf[:, 0:4], op=ALU.pow),
                 reads=[bst, bf("neghalf")], writes=[bst])
            yield
            S.op("pool", lambda e: e.tensor_tensor(out=sg, in0=sg, in1=stt[:, 20:24].unsqueeze(2).broadcast_to([128, 4, 128]),
                                                   op=ALU.mult), reads=[wb(w, "acc0"), bst], writes=[wb(w, "acc0")])
            yield
            for hl in range(4):
                S.op("dve", lambda e, hl=hl: e.scalar_tensor_tensor(out=ybf[:, hl, :], in0=po[:, hl, :], scalar=stt[:, 8 + hl:9 + hl],
                                                                    in1=sg[:, hl, :], op0=ALU.subtract, op1=ALU.mult),
                     reads=[bf("bk3"), bst, wb(w, "acc0")], writes=[wb(w, "ybf")])
            yield
            for hl in range(4):
                S.op("pe", lambda e, hl=hl: e.transpose(out=tpB[:, hl, :], in_=ybf[:, hl, :], identity=ident[:]),
                     reads=[wb(w, "ybf"), bf("ident")], writes=[bf("tpB")])
            yield
            S.op("act", lambda e: e.activation(out=YR[:, hp * 4:hp * 4 + 4, t * 128:(t + 1) * 128], in_=tpB[:, 0:4, :], func=AF.Copy),
                 reads=[bf("tpB")], writes=[bf("YR")])

        def preA(t, hp, w, slot, j):
            proj_tm(0, j, 0 + hp * 512)
            yield
            rotary(0, slot, w, w["kr"], wb(w, "kr"))
            yield
            proj_tm(1, j, 1024 + hp * 512)
            yield
            v_evac(1, hp, w, False, zt=t)

        def preB(t, hp, w):
            kr, vh = w["kr"], w["vh"]
            for hl in range(4):
                S.op("pe", lambda e, hl=hl: e.matmul(BK[2 + hp][:, hl * 128:(hl + 1) * 128], lhsT=kr[:, hl, :], rhs=vh[:, hl, :],
                                                     start=(t == 0 and hl == 0), stop=(t == NT - 1 and hl == 3)),
                     reads=[wb(w, "kr"), wb(w, "vh")], writes=[bf("bk%d" % (2 + hp))])
            yield

        def pipeline(n, stA, stB, pre_front):
            run(pre_front(-1))
            run(stA(0))
            for u in range(n):
                run(stB(u), stA(u + 1) if u + 1 < n else None, pre_front(u))

        xlw_P = lambda kc, cb: (W3[:, kc, 2048 + cb * 128: 2048 + (cb + 1) * 128], kc * 4 + 2)
        for blk in range(NB):
            units = [(blk * 4 + j, hp) for j in range(4) for hp in range(2)]

            def pf(u, blk=blk, units=units):
                if u == -1:
                    return front_g(x_pre, tab_pre, blk * 4, (blk * 4) % 2, 0, True)
                t, hp = units[u]
                if hp == 0 and (t % 4) < 3:
                    return front_g(x_pre, tab_pre, t + 1, (t + 1) % 2, (t + 1) % 4, True)
                return None

            pipeline(len(units),
                     lambda u, units=units: preA(units[u][0], units[u][1], WS[u % NWS], units[u][0] % 2, units[u][0] % 4),
                     lambda u, units=units: preB(units[u][0], units[u][1], WS[u % NWS]),
                     pf)
            lru_block(False, xlw_P)

        for kc in range(8):
            load_w(kc * 4 + 2, W3[:, kc, 2048:3072], w_in, kc * 128, 0, gin[:, kc:kc + 1], "gin")
        for kc in range(8):
            load_w(kc * 4 + 3, W3[:, kc, 3072:4096], w_in, kc * 128, 3072, gin[:, kc:kc + 1], "gin")

        for hp in range(2):
            S.op("dve", lambda e, hp=hp: e.tensor_scalar(out=S32[:, hp * 4:hp * 4 + 4, :],
                                                         in0=BK[2 + hp][:].rearrange("p (h d) -> p h d", h=4),
                                                         scalar1=flag[:, 0:1], scalar2=None, op0=ALU.mult),
                 reads=[bf("bk%d" % (2 + hp)), bf("flag")], writes=[bf("S32_%d" % hp)])
        hs_all = [bf("hstate%d" % cb) for cb in range(8)]
        S.op("dve", lambda e: e.tensor_scalar(out=hstate[:], in0=hstate[:], scalar1=flag[:, 0:1], scalar2=None, op0=ALU.mult),
             reads=hs_all + [bf("flag")], writes=hs_all)
        S.op("dve", lambda e: e.tensor_scalar(out=hist[:], in0=hist[:], scalar1=flag[:, 0:1], scalar2=None, op0=ALU.mult),
             reads=[bf("hist"), bf("flag")], writes=[bf("hist")])
        for hp in range(2):
            S.op("act", lambda e, hp=hp: e.activation(out=Sbf[:, hp * 4:hp * 4 + 4, :], in_=S32[:, hp * 4:hp * 4 + 4, :], func=AF.Copy),
                 reads=[bf("S32_%d" % hp)], writes=[bf("Sbf%d" % hp)])

        unitsR = [(t, hp) for t in range(NT) for hp in range(2)]

        def pfR(u):
            if u == -1:
                return front_g(x_main, tab_main, 0, 0, 0, True, "R")
            t, hp = unitsR[u]
            if hp == 0 and t + 1 < NT:
                return front_g(x_main, tab_main, t + 1, (t + 1) % 2, (t + 1) % 4, True, "R")
            return None

        pipeline(len(unitsR),
                 lambda u: retA(unitsR[u][0], unitsR[u][1], WS[u % NWS], unitsR[u][0] % 2, unitsR[u][0] % 4),
                 lambda u: retB(unitsR[u][0], unitsR[u][1], WS[u % NWS]),
                 pfR)

        fence(["%s_%d" % (n_, i_) for n_ in ("qr", "vb", "qT", "kT", "Pm", "ybf") for i_ in range(NWS)], ["YL"])
        for kc in range(8):
            load_w(24 + kc, Wf[:, 24 + kc, :], w_in, kc * 128, 4096, gin[:, kc:kc + 1], "gin")
        for kc in range(8):
            load_w(kc, Wf[:, kc, :], w_in, kc * 128, 5120, gin[:, kc:kc + 1], "gin")
        for kc in range(16):
            load_w(8 + kc, Wf[:, 8 + kc, :], w_out, kc * 128, 0)
        small_load(gout, gout_d[:, :], "GX")
        xlw_L = lambda kc, cb: (Wf[:, 24 + kc, cb * 128:(cb + 1) * 128], 24 + kc)
        glw_L = lambda kc, cb: (Wf[:, kc, cb * 128:(cb + 1) * 128], kc)

        for blk in range(NB):
            for j in range(4):
                t = blk * 4 + j
                front(x_main, tab_main, t, t % 2, j, False, "L")
            lru_block(True, xlw_L, glw_L)
            for j in range(4):
                t = blk * 4 + j
                slot = t % 2
                bx = bf("xt%d" % slot)
                S.dma("sp", lambda e, t=t, slot=slot: e.dma_start(out=xt[slot][:], in_=x_main[t * 128:(t + 1) * 128, :]), writes=[bx])
                for ng in range(2):
                    bank = 2 * (j % 2) + ng
                    for kc in range(16):
                        if kc < 8:
                            lhs = YR[:, kc, t * 128:(t + 1) * 128]
                            rd = [bf("YR")]
                        else:
                            lhs = YL[:, kc - 8, j * 128:(j + 1) * 128]
                            rd = [bf("YL")]
                        S.op("pe", lambda e, lhs=lhs, kc=kc, ng=ng, bank=bank: e.matmul(
                            BK[bank][:], lhsT=lhs, rhs=Wf[:, 8 + kc, ng * 512:(ng + 1) * 512], start=(kc == 0), stop=(kc == 15)),
                            reads=rd + [bf("Wc%d" % (8 + kc))], writes=[bf("bk%d" % bank)])
                    S.op("dve", lambda e, slot=slot, ng=ng, bank=bank: e.tensor_tensor(
                        out=xt[slot][:, ng * 512:(ng + 1) * 512], in0=BK[bank][:], in1=xt[slot][:, ng * 512:(ng + 1) * 512], op=ALU.add),
                        reads=[bf("bk%d" % bank), bx], writes=[bx])
                S.op("pool", lambda e: e.memset(ss[:, 1:2], 0.0), writes=[bf("ss2")])
                S.op("act", lambda e, slot=slot: e.activation(out=junk[:], in_=xt[slot][:], func=AF.Square, accum_out=ss[:, 1:2]),
                     reads=[bx, bf("ss2")], writes=[bf("xs"), bf("ss2")])
                S.op("dve", lambda e: e.tensor_scalar(out=rstd[:, 1:2], in0=ss[:, 1:2], scalar1=1.0 / D_MODEL, scalar2=EPS,
                                                      op0=ALU.mult, op1=ALU.add), reads=[bf("ss2")], writes=[bf("rstd2")])
                S.op("pool", lambda e: e.tensor_tensor(out=rstd[:, 1:2], in0=rstd[:, 1:2], in1=neghalf[:, 0:1], op=ALU.pow),
                     reads=[bf("rstd2"), bf("neghalf")], writes=[bf("rstd2")])
                S.op("dve", lambda e, slot=slot: e.scalar_tensor_tensor(out=xt[slot][:], in0=xt[slot][:], scalar=rstd[:, 1:2], in1=gout,
                                                                        op0=ALU.mult, op1=ALU.mult),
                     reads=[bx, bf("rstd2"), bf("GX")], writes=[bx])
                S.dma("sp", lambda e, t=t, slot=slot: e.dma_start(out=out_d[t * 128:(t + 1) * 128, :], in_=xt[slot][:]), reads=[bx])

        S.emit()
    return nc


def _rot_table(T):
    d = 128
    inv_freq = (10000.0 ** (-np.arange(0, d, 2, dtype=np.float32) / np.float32(d))).astype(np.float32)
    ang = (np.arange(T, dtype=np.float32)[:, None] * inv_freq[None, :]).astype(np.float32).astype(np.float64)
    c = np.cos(ang)
    s = np.sin(ang)
    return np.concatenate([c, c, -s, s], axis=1).astype(np.float32)


def _consts():
    h = np.arange(NH, dtype=np.float64)
    log_g = np.log1p(-np.exp2(-5.0 - h))
    s = np.arange(128, dtype=np.float64)
    causal = (s[None, :] >= s[:, None]).astype(np.float32)
    cdec = np.exp(-(s[:, None] + 1.0) * log_g[None, :])
    xi = np.exp((s[None, None, :] + 1.0) * log_g[None, :, None]) * np.ones((128, 1, 1))
    zeta = np.exp((127.0 - s[:, None]) * log_g[None, :])
    return causal, cdec.astype(np.float32), xi.astype(np.float32), zeta.astype(np.float32)


_PROG_CACHE = {}


def kernel(x, norm_in_g, w_in, conv_w, conv_b, gate_a_w, gate_a_b, gate_x_w, gate_x_b, lru_lambda, w_out, norm_out_g):
    x = np.asarray(x, dtype=np.float32)
    Bsz, T, Dm = x.shape
    assert Dm == D_MODEL and Bsz * 2 == 8
    TH = T // 2
    NT = TH // 128
    f32 = lambda a: np.ascontiguousarray(np.asarray(a, dtype=np.float32))
    tabfull = _rot_table(T)
    causal, cdec, xi, zeta = _consts()
    lg = np.log1p(-np.exp2(-5.0 - np.arange(NH, dtype=np.float64)))
    pos = np.arange(TH, dtype=np.float64)
    zpre = np.exp((TH - 1.0 - pos)[:, None] * lg[None, :]).reshape(NT, 128, NH).transpose(1, 0, 2)
    zpre = np.ascontiguousarray(zpre.astype(np.float32))
    shared = {
        "w_in": f32(w_in),
        "w_out": f32(w_out),
        "gin": f32(np.asarray(norm_in_g).reshape(8, 128).T),
        "gout": f32(np.broadcast_to(np.asarray(norm_out_g)[None, :], (128, D_MODEL))),
        "convw": f32(np.asarray(conv_w).reshape(4, 8, 128).transpose(2, 1, 0)),
        "convb": f32(np.asarray(conv_b).reshape(8, 128).T),
        "gaw": f32(gate_a_w),
        "gxw": f32(gate_x_w),
        "gab": f32(np.asarray(gate_a_b).T),
        "gxb": f32(np.asarray(gate_x_b).T),
        "lam": f32(np.asarray(lru_lambda).reshape(8, 128).T),
        "ident": np.eye(128, dtype=np.float32).astype(ml_dtypes.bfloat16),
        "causal": causal, "cdec": cdec, "xibc": xi, "zeta": zeta,
    }
    in_maps = []
    for c in range(8):
        b, half = c // 2, c % 2
        m = dict(shared)
        m["x_main"] = np.ascontiguousarray(x[b, half * TH:(half + 1) * TH])
        m["x_pre"] = np.ascontiguousarray(x[b, 0:TH])
        m["tab_main"] = np.ascontiguousarray(tabfull[half * TH:(half + 1) * TH])
        m["tab_pre"] = np.ascontiguousarray(tabfull[0:TH])
        m["flag"] = np.full((128, 1), float(half), dtype=np.float32)
        m["zpre"] = zpre
        in_maps.append(m)
    if NT not in _PROG_CACHE:
        _PROG_CACHE[NT] = build_program(NT)
    nc = _PROG_CACHE[NT]
    res = run_bass_kernel_spmd(nc, in_maps, core_ids=list(range(8)))
    out = np.empty((Bsz, T, D_MODEL), dtype=np.float32)
    for c in range(8):
        b, half = c // 2, c % 2
        out[b, half * TH:(half + 1) * TH] = res.results[c]["out"]
    return out
```

```python
import math
from contextlib import ExitStack

import numpy as np
import ml_dtypes

import concourse.bass as bass
import concourse.mybir as mybir
from concourse.bass_utils import run_bass_kernel_spmd

F32 = mybir.dt.float32
BF16 = mybir.dt.bfloat16
AF = mybir.ActivationFunctionType
ALU = mybir.AluOpType
AX = mybir.AxisListType

D_MODEL = 1024
NH = 8
NCB = 8
EPS = 1e-6


class Buf:
    __slots__ = ("name", "w", "rs")

    def __init__(self, name):
        self.name = name
        self.w = None
        self.rs = []


class _FakeIns:
    def __init__(self, name, out):
        self.name = name
        self.out = out


class _FakeEng:
    def __getattr__(self, name):
        def f(*a, **k):
            out = k.get("out", a[0] if a else None)
            return _FakeIns(name, out)
        return f


class _Op:
    __slots__ = ("idx", "eng", "fn", "dur", "kind", "preds", "succs", "npred", "ready", "start", "fin", "pos", "dsem", "dval")

    def __init__(self, idx, eng, fn, dur, kind):
        self.idx = idx
        self.eng = eng
        self.fn = fn
        self.dur = dur
        self.kind = kind
        self.preds = {}
        self.succs = []
        self.npred = 0
        self.ready = 0.0
        self.start = 0.0
        self.fin = 0.0
        self.pos = 0
        self.dsem = -1
        self.dval = 0


class Sched:
    ENG = ("pe", "act", "dve", "pool", "sp")
    LAT = 0.5

    def __init__(self, nc, stack, n_dma_sems=16):
        self.nc = nc
        self.sem = {e: stack.enter_context(nc.semaphore("s_" + e)) for e in self.ENG}
        self.dsem = [stack.enter_context(nc.semaphore("d%d" % i)) for i in range(n_dma_sems)]
        self.ops = []
        self.labels = None
        self.fake = _FakeEng()

    def _cost(self, eng, fn):
        try:
            info = fn(self.fake)
            ap = info.out
            cols = float(ap.free_size())
            nbytes = float(ap.nbytes())
        except Exception:
            cols, nbytes = 512.0, 65536.0
        if eng == "pe":
            return 0.07 + cols / 2400.0
        if eng == "dve":
            return 0.07 + cols / 960.0
        if eng == "act":
            return 0.2 + cols / 1200.0
        if eng == "pool":
            return 0.25 + cols / 420.0
        return 2.0 + nbytes / 150e3

    def _add(self, eng, fn, reads, writes, kind):
        op = _Op(len(self.ops), eng, fn, self._cost(eng if kind == "op" else "sp", fn), kind)
        preds = op.preds

        def add(p, wait, why=None):
            if p is op:
                return
            preds[p] = preds.get(p, False) or wait
            if self.labels is not None:
                self.labels[(p.idx, op.idx)] = why
        for b in reads:
            if b.w is not None:
                add(b.w, True, "RAW " + b.name)
            if b.name.startswith(("bk", "tp")):
                for r in b.rs:
                    if r.eng != eng or r.kind != "op":
                        add(r, True, "RR " + b.name)
        for b in writes:
            if b.w is not None:
                add(b.w, not (eng == "pe" and b.w.eng == "pe" and b.w.kind == "op" and kind == "op"), "WAW " + b.name)
            for r in b.rs:
                add(r, not (eng == "pe" and r.eng == "pe" and r.kind == "op" and kind == "op"), "WAR " + b.name)
        for b in reads:
            b.rs.append(op)
        for b in writes:
            b.w = op
            b.rs = []
        for p in preds:
            p.succs.append(op)
        op.npred = len(preds)
        self.ops.append(op)
        return op

    def op(self, eng, fn, reads=(), writes=()):
        return self._add(eng, fn, reads, writes, "op")

    def dma(self, eng, fn, reads=(), writes=()):
        return self._add(eng if eng in ("sp", "act") else "sp", fn, reads, writes, "dma")

    PRIO_WINDOW = 0.3

    def _schedule(self):
        import heapq
        bl = [0.0] * len(self.ops)
        for op in reversed(self.ops):
            m = 0.0
            for sx in op.succs:
                v = bl[sx.idx] + (self.LAT if sx.eng != op.eng else 0.05)
                if v > m:
                    m = v
            bl[op.idx] = m + op.dur
        W = self.PRIO_WINDOW
        ready = {e: [] for e in self.ENG}
        free = {e: 0.0 for e in self.ENG}
        order = {e: [] for e in self.ENG}
        for op in self.ops:
            if op.npred == 0:
                ready[op.eng].append(op)
        remaining = len(self.ops)
        while remaining:
            best = None
            for e in self.ENG:
                lst = ready[e]
                if not lst:
                    continue
                t0 = min(max(free[e], o.ready) for o in lst)
                cand = None
                for o in lst:
                    st = max(free[e], o.ready)
                    if st <= t0 + W:
                        k = (-bl[o.idx], o.idx) if W > 0 else (st, o.idx)
                        if cand is None or k < cand[0]:
                            cand = (k, o, st)
                key = (t0, cand[1].idx)
                if best is None or key < best[0]:
                    best = (key, e, cand[1], cand[2])
            assert best is not None, "scheduler deadlock"
            _, e, op, st = best
            ready[e].remove(op)
            op.start = st
            if op.kind == "dma":
                free[e] = op.start + 0.06
                op.fin = op.start + op.dur
            else:
                op.fin = op.start + op.dur
                free[e] = op.fin
            op.pos = len(order[e])
            order[e].append(op)
            remaining -= 1
            for sx in op.succs:
                sx.npred -= 1
                if not sx.preds[op]:
                    r = op.start
                else:
                    r = op.fin + (self.LAT if (sx.eng != op.eng or op.kind == "dma") else 0.2)
                if r > sx.ready:
                    sx.ready = r
                if sx.npred == 0:
                    ready[sx.eng].append(sx)
        self.order = order
        self.est_us = max(op.fin for op in self.ops)

    def emit(self):
        self._schedule()
        order = self.order
        nd = len(self.dsem)
        cnt = 0
        dcnt = [0] * nd
        dlast = [None] * nd
        extra = {}
        k = 0
        all_dmas = sorted((o for e_ in self.ENG for o in order[e_] if o.kind == "dma"), key=lambda o: (o.start, o.idx))
        for op in all_dmas:
            if op.kind == "dma":
                i = k % nd
                k += 1
                dcnt[i] += 16
                op.dsem, op.dval = i, dcnt[i]
                if dlast[i] is not None:
                    extra[op] = dlast[i]
                dlast[i] = op
        idx_on = {}
        for e in self.ENG:
            c = 0
            for op in order[e]:
                if op.kind == "op":
                    c += 1
                    idx_on[op] = c
        prog = {e: [] for e in self.ENG}
        for e in self.ENG:
            seen = {}
            for op in order[e]:
                waits = [p for p, w in op.preds.items() if w]
                if op in extra:
                    waits.append(extra[op])
                for p in waits:
                    if p.kind == "dma":
                        key, val, sem = ("d", p.dsem), p.dval, self.dsem[p.dsem]
                    else:
                        key, val, sem = p.eng, idx_on[p], self.sem[p.eng]
                    if seen.get(key, 0) >= val:
                        continue
                    seen[key] = val
                    prog[e].append(("w", sem, val))
                if op.kind == "dma":
                    prog[e].append(("o", op.fn, self.dsem[op.dsem], 16))
                else:
                    prog[e].append(("o", op.fn, self.sem[e], 1))
            if e == "sp":
                for i in range(nd):
                    if dcnt[i] > 0 and seen.get(("d", i), 0) < dcnt[i]:
                        prog[e].append(("w", self.dsem[i], dcnt[i]))
        nc = self.nc

        def replay(en, items):
            for it in items:
                if it[0] == "w":
                    en.wait_ge(it[1], it[2])
                else:
                    it[1](en).then_inc(it[2], it[3])

        with nc.Block() as block:
            @block.tensor
            def _(en):
                replay(en, prog["pe"])

            @block.scalar
            def _(en):
                replay(en, prog["act"])

            @block.vector
            def _(en):
                replay(en, prog["dve"])

            @block.gpsimd
            def _(en):
                replay(en, prog["pool"])

            @block.sync
            def _(en):
                replay(en, prog["sp"])


NWS = 2


def build_program(NT):
    assert NT % 4 == 0
    NB = NT // 4
    TH = NT * 128
    log_g = [math.log1p(-2.0 ** (-5.0 - h)) for h in range(NH)]
    GAM = [math.exp(128.0 * lg) for lg in log_g]

    nc = bass.Bass("TRN2", target_bir_lowering=False)
    din = lambda n, s, d=F32: nc.dram_tensor(n, s, d, kind="ExternalInput").ap()
    x_main = din("x_main", [TH, D_MODEL])
    x_pre = din("x_pre", [TH, D_MODEL])
    tab_main = din("tab_main", [TH, 256])
    tab_pre = din("tab_pre", [TH, 256])
    w_in = din("w_in", [D_MODEL, 6 * D_MODEL])
    w_out = din("w_out", [2 * D_MODEL, D_MODEL])
    gin_d = din("gin", [128, 8])
    gout_d = din("gout", [128, D_MODEL])
    convw_d = din("convw", [128, 8, 4])
    convb_d = din("convb", [128, 8])
    gaw_d = din("gaw", [8, 128, 128])
    gxw_d = din("gxw", [8, 128, 128])
    gab_d = din("gab", [128, 8])
    gxb_d = din("gxb", [128, 8])
    lam_d = din("lam", [128, 8])
    flag_d = din("flag", [128, 1])
    ident_d = din("ident", [128, 128], BF16)
    causal_d = din("causal", [128, 128])
    cdec_d = din("cdec", [128, 8])
    xibc_d = din("xibc", [128, 8, 128])
    zeta_d = din("zeta", [128, 8])
    zpre_d = din("zpre", [128, NT, 8])
    out_d = nc.dram_tensor("out", [TH, D_MODEL], F32, kind="ExternalOutput").ap()

    with ExitStack() as st:
        S = Sched(nc, st)
        sb = lambda n, s, d=F32: st.enter_context(nc.sbuf_tensor("sb_" + n, s, d))
        ps = lambda n, s, d=F32: st.enter_context(nc.psum_tensor("ps_" + n, s, d))

        Wt = sb("W", [128, 32768], BF16)
        W3 = Wt[:].rearrange("p (k c) -> p k c", c=4096)
        Wf = Wt[:].rearrange("p (k c) -> p k c", c=1024)
        GW = sb("GW", [128, 2, 8, 128], BF16)
        YR = sb("YR", [128, 8, TH], BF16)
        xt = [sb("xt%d" % i, [128, 1024]) for i in range(2)]
        tab = [sb("tab%d" % i, [128, 256]) for i in range(2)]
        xs = sb("xs", [128, 1024], BF16)
        junk = xs
        xnT = sb("xnT", [128, 8, 512], BF16)
        RWall = sb("RWall", [128, max(8192, 3072 * NWS)], BF16)
        YL = RWall[:, 0:4096].rearrange("p (c t) -> p c t", c=8)

        def mkset(i):
            d = {"i": i}
            v4 = lambda k: RWall[:, i * 3072 + k * 512: i * 3072 + (k + 1) * 512].rearrange("p (h d) -> p h d", h=4)
            d["qr"], d["vb"], d["qT"], d["kT"], d["Pm"], d["ybf"] = v4(0), v4(1), v4(2), v4(3), v4(4), v4(5)
            d["kr"] = sb("kr%d" % i, [128, 4, 128], BF16)[:]
            d["vh"] = sb("vh%d" % i, [128, 4, 128], BF16)[:]
            F1 = sb("F1_%d" % i, [128, 512])
            F2 = sb("F2_%d" % i, [128, 512])
            a0 = sb("acc0_%d" % i, [128, 512])
            a1 = sb("acc1_%d" % i, [128, 512])
            xl_ = sb("xl_%d" % i, [128, 515])
            d["xcb"] = sb("xcb_%d" % i, [128, 512], BF16)[:]
            d["stt"] = sb("stt_%d" % i, [128, 32])
            d["t1"] = F1[:].rearrange("p (h d) -> p h d", h=4)
            d["t2"] = F2[:].rearrange("p (h d) -> p h d", h=4)
            d["sg"] = a0[:].rearrange("p (h d) -> p h d", h=4)
            d["thi"], d["a2"], d["hh"], d["av"] = F1[:], F2[:], F2[:], a0[:]
            d["acc"] = [a0[:], a1[:]]
            d["xl"] = xl_[:]
            d["thr"] = xl_[:, 0:512]
            return d

        WS = [mkset(i_) for i_ in range(NWS)]
        S32 = sb("S32", [128, 8, 128])
        Sbf = sb("Sbf", [128, 8, 128], BF16)
        hist = sb("hist", [128, 8, 3])
        hstate = sb("hstate", [128, 8])
        GX = sb("GX", [128, 1024])
        xibc = GX[:].rearrange("p (h d) -> p h d", h=8)
        gout = GX[:]
        causal = sb("causal", [128, 128])
        cdec = sb("cdec", [128, 8])
        ident = sb("ident", [128, 128], BF16)
        gin = sb("gin", [128, 8])
        gink = sb("gink", [128, 8])
        convw = sb("convw", [128, 8, 4])
        convb = sb("convb", [128, 8])
        hba = sb("hba", [128, 8])
        hbx = sb("hbx", [128, 8])
        lam = sb("lam", [128, 8])
        nsp = sb("nsp", [128, 8])
        hnsp = sb("hnsp", [128, 8])
        zeta = sb("zeta", [128, 8])
        zpre = sb("zpre", [128, NT, 8])
        flag = sb("flag", [128, 1])
        neghalf = sb("neghalf", [128, 8])
        ss = sb("ss", [128, 4])
        rstd = sb("rstd", [128, 4])
        rstd_all = sb("rstd_all", [128, NT])
        fz = sb("fz", [128, 2])
        sixteenth = sb("sixteenth", [128, 1])

        tpA = ps("tpA", [128, 8, 128], BF16)
        tpB = ps("tpB", [128, 8, 128], BF16)
        BK = [ps("bk%d" % i, [128, 512]) for i in range(6)]

        B = {}

        def bf(name):
            if name not in B:
                B[name] = Buf(name)
            return B[name]

        def wb(w, name):
            return bf("%s_%d" % (name, w["i"]))

        def fence(old, new):
            S.op("pool", lambda e: e.memset(fz[:, 0:1], 0.0), reads=[bf(n) for n in old], writes=[bf(n) for n in new] + [bf("fz")])

        def small_load(dst, src, name):
            S.dma("sp", lambda e: e.dma_start(out=dst, in_=src), writes=[bf(name)])

        small_load(gin[:], gin_d[:, :], "gin")
        small_load(convw[:], convw_d[:, :, :], "convw")
        small_load(convb[:], convb_d[:, :], "convb")
        small_load(hba[:], gab_d[:, :], "hba")
        small_load(hbx[:], gxb_d[:, :], "hbx")
        small_load(lam[:], lam_d[:, :], "lam")
        small_load(flag[:], flag_d[:, :], "flag")
        small_load(ident[:], ident_d[:, :], "ident")
        small_load(causal[:], causal_d[:, :], "causal")
        small_load(cdec[:], cdec_d[:, :], "cdec")
        small_load(xibc, xibc_d[:, :, :], "GX")
        small_load(zeta[:], zeta_d[:, :], "zeta")
        small_load(zpre[:], zpre_d[:, :, :], "zpre")

        S.op("dve", lambda e: e.memset(neghalf[:], -0.5), writes=[bf("neghalf")])
        S.op("dve", lambda e: e.memset(sixteenth[:], 1.0 / 16.0), writes=[bf("sixteenth")])
        S.op("dve", lambda e: e.memset(S32[:], 0.0), writes=[bf("S32_0"), bf("S32_1")])
        S.op("dve", lambda e: e.memset(hist[:], 0.0), writes=[bf("hist")])
        S.op("dve", lambda e: e.memset(hstate[:], 0.0), writes=[bf("hstate%d" % c_) for c_ in range(8)])
        S.op("dve", lambda e: e.tensor_scalar(out=hba[:], in0=hba[:], scalar1=0.5, scalar2=None, op0=ALU.mult),
             reads=[bf("hba")], writes=[bf("hba")])
        S.op("dve", lambda e: e.tensor_scalar(out=hbx[:], in0=hbx[:], scalar1=0.5, scalar2=None, op0=ALU.mult),
             reads=[bf("hbx")], writes=[bf("hbx")])
        S.op("dve", lambda e: e.tensor_scalar(out=gink[:], in0=gin[:], scalar1=128.0 ** -0.5, scalar2=None, op0=ALU.mult),
             reads=[bf("gin")], writes=[bf("gink")])
        S.op("act", lambda e: e.activation(out=nsp[:], in_=lam[:], func=AF.Exp, scale=-1.0),
             reads=[bf("lam")], writes=[bf("nsp")])
        S.op("act", lambda e: e.activation(out=hnsp[:], in_=nsp[:], func=AF.Ln, bias=1.0),
             reads=[bf("nsp")], writes=[bf("hnsp")])
        S.op("dve", lambda e: e.tensor_scalar(out=nsp[:], in0=hnsp[:], scalar1=-8.0, scalar2=None, op0=ALU.mult),
             reads=[bf("hnsp")], writes=[bf("nsp")])
        S.op("dve", lambda e: e.tensor_scalar(out=hnsp[:], in0=hnsp[:], scalar1=-4.0, scalar2=None, op0=ALU.mult),
             reads=[bf("hnsp"), bf("nsp")], writes=[bf("hnsp")])

        for gi_, gd in enumerate((gaw_d, gxw_d)):
            S.dma("sp", lambda e, gd=gd, gi_=gi_: e.dma_start(out=xt[gi_][:].rearrange("p (n o) -> p n o", n=8),
                                                              in_=gd.rearrange("n i o -> i n o")),
                  writes=[bf("xt%d" % gi_)])
            S.op("dve", lambda e, gi_=gi_: e.tensor_copy(out=GW[:, gi_, :, :],
                                                         in_=xt[gi_][:].rearrange("p (n o) -> p n o", n=8)),
                 reads=[bf("xt%d" % gi_)], writes=[bf("GW")])

        wl_cnt = [0]

        def load_w(chunk, dst, src, r0, c0, scale_ap=None, scale_name=None):
            i = wl_cnt[0] % 2
            wl_cnt[0] += 1
            S.dma("sp", lambda e: e.dma_start(out=xt[i][:], in_=src[r0:r0 + 128, c0:c0 + 1024]),
                  writes=[bf("xt%d" % i)])
            eng = "dve" if (wl_cnt[0] % 2 == 0) else "act"
            if scale_ap is None:
                if eng == "dve":
                    S.op(eng, lambda e: e.tensor_copy(out=dst, in_=xt[i][:]), reads=[bf("xt%d" % i)], writes=[bf("Wc%d" % chunk)])
                else:
                    S.op(eng, lambda e: e.activation(out=dst, in_=xt[i][:], func=AF.Copy), reads=[bf("xt%d" % i)], writes=[bf("Wc%d" % chunk)])
            else:
                if eng == "dve":
                    S.op(eng, lambda e: e.tensor_scalar(out=dst, in0=xt[i][:], scalar1=scale_ap, scalar2=None, op0=ALU.mult),
                         reads=[bf("xt%d" % i), bf(scale_name)], writes=[bf("Wc%d" % chunk)])
                else:
                    S.op(eng, lambda e: e.activation(out=dst, in_=xt[i][:], func=AF.Copy, scale=scale_ap),
                         reads=[bf("xt%d" % i), bf(scale_name)], writes=[bf("Wc%d" % chunk)])

        for kc in range(8):
            load_w(kc * 4 + 0, W3[:, kc, 0:1024], w_in, kc * 128, 1024, gink[:, kc:kc + 1], "gink")
        for kc in range(8):
            load_w(kc * 4 + 1, W3[:, kc, 1024:2048], w_in, kc * 128, 2048, gin[:, kc:kc + 1], "gin")
        for kc in range(8):
            load_w(kc * 4 + 2, W3[:, kc, 2048:3072], w_in, kc * 128, 4096, gin[:, kc:kc + 1], "gin")

        def front_g(src, tabsrc, t, slot, j, need_tab, mode="P"):
            bx = bf("xt%d" % slot)
            S.dma("sp", lambda e: e.dma_start(out=xt[slot][:], in_=src[t * 128:(t + 1) * 128, :]), writes=[bx])
            if need_tab:
                S.dma("sp", lambda e: e.dma_start(out=tab[slot][:], in_=tabsrc[t * 128:(t + 1) * 128, :]),
                      writes=[bf("tab%d" % slot)])
            if mode == "P":
                rs, rsb = rstd[:, 0:1], bf("rstd")
            else:
                rs, rsb = rstd_all[:, t:t + 1], bf("rsa%d" % t)
            if mode != "L":
                S.op("pool", lambda e: e.memset(ss[:, 0:1], 0.0), writes=[bf("ss")])
                S.op("act", lambda e: e.activation(out=junk[:], in_=xt[slot][:], func=AF.Square, accum_out=ss[:, 0:1]),
                     reads=[bx, bf("ss")], writes=[bf("xs"), bf("ss")])
                yield
                S.op("dve", lambda e: e.tensor_scalar(out=rs, in0=ss[:, 0:1], scalar1=1.0 / D_MODEL, scalar2=EPS,
                                                      op0=ALU.mult, op1=ALU.add), reads=[bf("ss")], writes=[rsb])
                S.op("pool", lambda e: e.tensor_tensor(out=rs, in0=rs, in1=neghalf[:, 0:1], op=ALU.pow),
                     reads=[rsb, bf("neghalf")], writes=[rsb])
                yield
            S.op("act", lambda e: e.activation(out=xs[:], in_=xt[slot][:], func=AF.Copy, scale=rs),
                 reads=[bx, rsb], writes=[bf("xs")])
            yield
            for kc in range(8):
                S.op("pe", lambda e, kc=kc: e.transpose(out=tpA[:, kc, :], in_=xs[:, kc * 128:(kc + 1) * 128],
                                                        identity=ident[:]),
                     reads=[bf("xs"), bf("ident")], writes=[bf("tpA")])
            yield
            S.op("dve", lambda e: e.tensor_copy(out=xnT[:, :, j * 128:(j + 1) * 128], in_=tpA[:]),
                 reads=[bf("tpA")], writes=[bf("xnT%d" % j)])

        def run(*gens):
            gens = [g for g in gens if g is not None]
            while gens:
                for g in list(gens):
                    try:
                        next(g)
                    except StopIteration:
                        gens.remove(g)

        def front(*a):
            run(front_g(*a))

        def proj_tm(bank, j, wcol):
            for kc in range(8):
                S.op("pe", lambda e, kc=kc: e.matmul(BK[bank][:], lhsT=xnT[:, kc, j * 128:(j + 1) * 128],
                                                     rhs=W3[:, kc, wcol:wcol + 512], start=(kc == 0), stop=(kc == 7)),
                     reads=[bf("xnT%d" % j), bf("Wc%d" % (kc * 4 + wcol // 1024))], writes=[bf("bk%d" % bank)])

        def rotary(bank, slot, w, dst, dst_buf):
            src = BK[bank][:].rearrange("p (h d) -> p h d", h=4)
            tb = bf("tab%d" % slot)
            bb = bf("bk%d" % bank)
            t1, t2 = w["t1"], w["t2"]
            S.op("dve", lambda e: e.tensor_tensor(out=t1, in0=src,
                                                  in1=tab[slot][:, 0:128].unsqueeze(1).broadcast_to([128, 4, 128]),
                                                  op=ALU.mult), reads=[bb, tb], writes=[wb(w, "F1")])
            S.op("dve", lambda e: e.tensor_tensor(out=t2[:, :, 0:64], in0=src[:, :, 64:128],
                                                  in1=tab[slot][:, 128:192].unsqueeze(1).broadcast_to([128, 4, 64]),
                                                  op=ALU.mult), reads=[bb, tb], writes=[wb(w, "F2")])
            S.op("dve", lambda e: e.tensor_tensor(out=t2[:, :, 64:128], in0=src[:, :, 0:64],
                                                  in1=tab[slot][:, 192:256].unsqueeze(1).broadcast_to([128, 4, 64]),
                                                  op=ALU.mult), reads=[bb, tb], writes=[wb(w, "F2")])
            S.op("pool", lambda e: e.tensor_tensor(out=dst, in0=t1, in1=t2, op=ALU.add),
                 reads=[wb(w, "F1"), wb(w, "F2")], writes=[dst_buf])

        def v_evac(bank, hp, w, plain, zt=None):
            bb = bf("bk%d" % bank)
            src = BK[bank][:].rearrange("p (h d) -> p h d", h=4)
            if plain:
                S.op("act", lambda e: e.activation(out=w["vb"], in_=src, func=AF.Copy), reads=[bb], writes=[wb(w, "vb")])
            zap = zeta[:, hp * 4:hp * 4 + 4] if zt is None else zpre[:, zt, hp * 4:hp * 4 + 4]
            S.op("dve", lambda e: e.tensor_tensor(out=w["vh"], in0=src,
                                                  in1=zap.unsqueeze(2).broadcast_to([128, 4, 128]),
                                                  op=ALU.mult), reads=[bb, bf("zeta"), bf("zpre")], writes=[wb(w, "vh")])

        def kv_update(hp, w, make_bf):
            kr, vh = w["kr"], w["vh"]
            for hl in range(4):
                S.op("pe", lambda e, hl=hl: e.matmul(BK[4][:, hl * 128:(hl + 1) * 128], lhsT=kr[:, hl, :], rhs=vh[:, hl, :],
                                                     start=True, stop=True),
                     reads=[wb(w, "kr"), wb(w, "vh")], writes=[bf("bk4")])
            for hl in range(4):
                h = hp * 4 + hl
                S.op("dve", lambda e, hl=hl, h=h: e.scalar_tensor_tensor(out=S32[:, h, :], in0=S32[:, h, :], scalar=float(GAM[h]),
                                                                         in1=BK[4][:, hl * 128:(hl + 1) * 128],
                                                                         op0=ALU.mult, op1=ALU.add),
                     reads=[bf("S32_%d" % hp), bf("bk4")], writes=[bf("S32_%d" % hp)])
            if make_bf:
                S.op("act", lambda e: e.activation(out=Sbf[:, hp * 4:hp * 4 + 4, :], in_=S32[:, hp * 4:hp * 4 + 4, :], func=AF.Copy),
                     reads=[bf("S32_%d" % hp)], writes=[bf("Sbf%d" % hp)])

        xall = [bf("xnT%d" % j) for j in range(4)]

        def lruA(cb, w, full, xlw, glw):
            pa = cb % 2
            pg = 2 + (cb % 2)
            xl, acc = w["xl"], w["acc"]
            for kc in range(8):
                S.op("pe", lambda e, kc=kc: e.matmul(BK[pa][:], lhsT=xlw(kc, cb)[0], rhs=xnT[:, kc, :], start=(kc == 0), stop=(kc == 7)),
                     reads=[bf("Wc%d" % xlw(kc, cb)[1])] + xall, writes=[bf("bk%d" % pa)])
            yield
            if full:
                for kc in range(8):
                    S.op("pe", lambda e, kc=kc: e.matmul(BK[pg][:], lhsT=glw(kc, cb)[0], rhs=xnT[:, kc, :], start=(kc == 0), stop=(kc == 7)),
                         reads=[bf("Wc%d" % glw(kc, cb)[1])] + xall, writes=[bf("bk%d" % pg)])
                yield
            S.op("pool", lambda e: e.tensor_copy(out=xl[:, 0:3], in_=hist[:, cb, :]), reads=[bf("hist")], writes=[wb(w, "xl")])
            if full:
                S.op("act", lambda e: e.activation(out=xl[:, 3:515], in_=BK[pa][:], func=AF.Copy),
                     reads=[bf("bk%d" % pa)], writes=[wb(w, "xl")])
            else:
                S.op("dve", lambda e: e.tensor_copy(out=xl[:, 3:515], in_=BK[pa][:]),
                     reads=[bf("bk%d" % pa)], writes=[wb(w, "xl")])
            S.op("pool", lambda e: e.tensor_copy(out=hist[:, cb, :], in_=xl[:, 512:515]), reads=[wb(w, "xl")], writes=[bf("hist")])
            yield
            S.op("act", lambda e: e.activation(out=acc[0], in_=xl[:, 3:515], func=AF.Identity,
                                               scale=convw[:, cb, 3:4], bias=convb[:, cb:cb + 1]),
                 reads=[wb(w, "xl"), bf("convw"), bf("convb")], writes=[wb(w, "acc0")])
            yield
            for n_, j_ in enumerate((2, 1, 0)):
                src_i = n_ % 2
                dst_i = 1 - src_i
                S.op("dve", lambda e, j_=j_, src_i=src_i, dst_i=dst_i: e.scalar_tensor_tensor(
                    out=acc[dst_i], in0=xl[:, j_:j_ + 512], scalar=convw[:, cb, j_:j_ + 1], in1=acc[src_i],
                    op0=ALU.mult, op1=ALU.add),
                    reads=[wb(w, "xl"), bf("convw"), wb(w, "acc%d" % src_i)], writes=[wb(w, "acc%d" % dst_i)])
                yield
            S.op("act", lambda e: e.activation(out=w["xcb"], in_=acc[1], func=AF.Copy), reads=[wb(w, "acc1")], writes=[wb(w, "xcb")])

        def lruB(cb, w, full):
            pg = 2 + (cb % 2)
            xc, thr, thi, av, a2, hh, xcb = w["acc"][1], w["thr"], w["thi"], w["av"], w["a2"], w["hh"], w["xcb"]
            bxc = wb(w, "acc1")
            S.op("pe", lambda e: e.matmul(BK[4][:], lhsT=GW[:, 0, cb, :], rhs=xcb, start=True, stop=True),
                 reads=[bf("GW"), wb(w, "xcb")], writes=[bf("bk4")])
            S.op("pe", lambda e: e.matmul(BK[5][:], lhsT=GW[:, 1, cb, :], rhs=xcb, start=True, stop=True),
                 reads=[bf("GW"), wb(w, "xcb")], writes=[bf("bk5")])
            yield
            S.op("act", lambda e: e.activation(out=thr, in_=BK[4][:], func=AF.Tanh, scale=0.5, bias=hba[:, cb:cb + 1]),
                 reads=[bf("bk4"), bf("hba")], writes=[wb(w, "xl")])
            S.op("act", lambda e: e.activation(out=thi, in_=BK[5][:], func=AF.Tanh, scale=0.5, bias=hbx[:, cb:cb + 1]),
                 reads=[bf("bk5"), bf("hbx")], writes=[wb(w, "F1")])
            yield
            S.op("act", lambda e: e.activation(out=av, in_=thr, func=AF.Exp, scale=hnsp[:, cb:cb + 1], bias=hnsp[:, cb:cb + 1]),
                 reads=[wb(w, "xl"), bf("hnsp")], writes=[wb(w, "acc0")])
            S.op("dve", lambda e: e.tensor_tensor(out=a2, in0=av, in1=av, op=ALU.mult),
                 reads=[wb(w, "acc0")], writes=[wb(w, "F2")])
            yield
            S.op("dve", lambda e: e.tensor_scalar(out=a2, in0=a2, scalar1=1.0, scalar2=-1.0 / 16.0, op0=ALU.min, op1=ALU.mult),
                 reads=[wb(w, "F2")], writes=[wb(w, "F2")])
            yield
            S.op("act", lambda e: e.activation(out=a2, in_=a2, func=AF.Sqrt, bias=sixteenth[:, 0:1]),
                 reads=[wb(w, "F2"), bf("sixteenth")], writes=[wb(w, "F2")])
            yield
            S.op("dve", lambda e: e.scalar_tensor_tensor(out=thi, in0=thi, scalar=1.0, in1=xc, op0=ALU.add, op1=ALU.mult),
                 reads=[wb(w, "F1"), bxc], writes=[wb(w, "F1")])
            S.op("dve", lambda e: e.tensor_tensor(out=thi, in0=thi, in1=a2, op=ALU.mult),
                 reads=[wb(w, "F1"), wb(w, "F2")], writes=[wb(w, "F1")])
            yield
            S.op("dve", lambda e: e.tensor_tensor_scan(out=hh, data0=av, data1=thi, initial=hstate[:, cb:cb + 1],
                                                       op0=ALU.mult, op1=ALU.add),
                 reads=[wb(w, "acc0"), wb(w, "F1"), bf("hstate%d" % cb)], writes=[wb(w, "F2")])
            S.op("pool", lambda e: e.tensor_copy(out=hstate[:, cb:cb + 1], in_=hh[:, 511:512]),
                 reads=[wb(w, "F2")], writes=[bf("hstate%d" % cb)])
            yield
            if full:
                S.op("act", lambda e: e.activation(out=thr, in_=BK[pg][:], func=AF.Tanh, scale=0.5),
                     reads=[bf("bk%d" % pg)], writes=[wb(w, "xl")])
                yield
                S.op("dve", lambda e: e.scalar_tensor_tensor(out=thr, in0=thr, scalar=1.0, in1=BK[pg][:], op0=ALU.add, op1=ALU.mult),
                     reads=[wb(w, "xl"), bf("bk%d" % pg)], writes=[wb(w, "xl")])
                S.op("dve", lambda e: e.tensor_tensor(out=YL[:, cb, :], in0=thr, in1=hh, op=ALU.mult),
                     reads=[wb(w, "xl"), wb(w, "F2")], writes=[bf("YL")])

        def lru_block(full, xlw, glw=None):
            run(lruA(0, WS[0], full, xlw, glw))
            for cb in range(8):
                run(lruB(cb, WS[cb % NWS], full),
                    lruA(cb + 1, WS[(cb + 1) % NWS], full, xlw, glw) if cb + 1 < 8 else None)

        def retA(t, hp, w, slot, j):
            proj_tm(0, j, 2048 + hp * 512)
            yield
            rotary(0, slot, w, w["qr"], wb(w, "qr"))
            yield
            proj_tm(1, j, 0 + hp * 512)
            yield
            rotary(1, slot, w, w["kr"], wb(w, "kr"))
            yield
            proj_tm(2, j, 1024 + hp * 512)
            yield
            v_evac(2, hp, w, True)
            yield
            proj_tm(2, j, 3072 + hp * 512)
            yield
            S.op("act", lambda e: e.activation(out=w["sg"], in_=BK[2][:].rearrange("p (h d) -> p h d", h=4), func=AF.Silu),
                 reads=[bf("bk2")], writes=[wb(w, "acc0")])

        def retB(t, hp, w):
            qr, kr, vb, qT, kT, Pm, ybf, sg, t1, t2 = (w[k] for k in ("qr", "kr", "vb", "qT", "kT", "Pm", "ybf", "sg", "t1", "t2"))
            stt = w["stt"]
            bst = wb(w, "stt")
            for hl in range(4):
                S.op("pe", lambda e, hl=hl: e.transpose(out=tpB[:, hl, :], in_=qr[:, hl, :], identity=ident[:]),
                     reads=[wb(w, "qr"), bf("ident")], writes=[bf("tpB")])
            for hl in range(4):
                S.op("pe", lambda e, hl=hl: e.transpose(out=tpB[:, 4 + hl, :], in_=kr[:, hl, :], identity=ident[:]),
                     reads=[wb(w, "kr"), bf("ident")], writes=[bf("tpB")])
            yield
            S.op("dve", lambda e: e.tensor_tensor(out=qT, in0=tpB[:, 0:4, :], in1=xibc[:, hp * 4:hp * 4 + 4, :], op=ALU.mult),
                 reads=[bf("tpB"), bf("GX")], writes=[wb(w, "qT")])
            S.op("act", lambda e: e.activation(out=kT, in_=tpB[:, 4:8, :], func=AF.Copy),
                 reads=[bf("tpB")], writes=[wb(w, "kT")])
            yield
            for hl in range(4):
                S.op("pe", lambda e, hl=hl: e.matmul(BK[5][:, hl * 128:(hl + 1) * 128], lhsT=kT[:, hl, :], rhs=qT[:, hl, :],
                                                     start=True, stop=True),
                     reads=[wb(w, "kT"), wb(w, "qT")], writes=[bf("bk5")])
            yield
            for hl in range(4):
                h = hp * 4 + hl
                S.op("dve", lambda e, hl=hl, h=h: e.scalar_tensor_tensor(out=Pm[:, hl, :], in0=BK[5][:, hl * 128:(hl + 1) * 128],
                                                                         scalar=cdec[:, h:h + 1], in1=causal[:],
                                                                         op0=ALU.mult, op1=ALU.mult),
                     reads=[bf("bk5"), bf("cdec"), bf("causal")], writes=[wb(w, "Pm")])
            yield
            for hl in range(4):
                h = hp * 4 + hl
                S.op("pe", lambda e, hl=hl: e.matmul(BK[3][:, hl * 128:(hl + 1) * 128], lhsT=Pm[:, hl, :], rhs=vb[:, hl, :],
                                                     start=True, stop=False),
                     reads=[wb(w, "Pm"), wb(w, "vb")], writes=[bf("bk3")])
                S.op("pe", lambda e, hl=hl, h=h: e.matmul(BK[3][:, hl * 128:(hl + 1) * 128], lhsT=qT[:, hl, :], rhs=Sbf[:, h, :],
                                                          start=False, stop=True),
                     reads=[wb(w, "qT"), bf("Sbf%d" % hp)], writes=[bf("bk3")])
            yield
            kv_update(hp, w, True)
            yield
            po = BK[3][:].rearrange("p (h d) -> p h d", h=4)
            S.op("dve", lambda e: e.tensor_reduce(out=stt[:, 0:4], in_=po, axis=AX.X, op=ALU.add), reads=[bf("bk3")], writes=[bst])
            S.op("act", lambda e: e.activation(out=t2, in_=po, func=AF.Square), reads=[bf("bk3")], writes=[wb(w, "F2")])
            yield
            S.op("dve", lambda e: e.tensor_reduce(out=stt[:, 4:8], in_=t2, axis=AX.X, op=ALU.add),
                 reads=[wb(w, "F2"), bst], writes=[bst])
            S.op("dve", lambda e: e.tensor_scalar(out=stt[:, 8:12], in0=stt[:, 0:4], scalar1=1.0 / 128.0, scalar2=None, op0=ALU.mult),
                 reads=[bst], writes=[bst])
            S.op("dve", lambda e: e.tensor_tensor(out=stt[:, 12:16], in0=stt[:, 8:12], in1=stt[:, 8:12], op=ALU.mult),
                 reads=[bst], writes=[bst])
            S.op("dve", lambda e: e.scalar_tensor_tensor(out=stt[:, 16:20], in0=stt[:, 4:8], scalar=1.0 / 128.0, in1=stt[:, 12:16],
                                                         op0=ALU.mult, op1=ALU.subtract), reads=[bst], writes=[bst])
            S.op("dve", lambda e: e.tensor_scalar(out=stt[:, 16:20], in0=stt[:, 16:20], scalar1=0.0, scalar2=EPS, op0=ALU.max, op1=ALU.add),
                 reads=[bst], writes=[bst])
            yield
            S.op("pool", lambda e: e.tensor_tensor(out=stt[:, 20:24], in0=stt[:, 16:20], in1=neghalf[:, 0:4], op=ALU.pow),
                 reads=[bst, bf("neghalf")], writes=[bst])
            yield
            S.op("dve", lambda e: e.tensor_tensor(out=t1, in0=po, in1=stt[:, 8:12].unsqueeze(2).broadcast_to([128, 4, 128]),
                                                  op=ALU.subtract), reads=[bf("bk3"), bst], writes=[wb(w, "F1")])
            S.op("pool", lambda e: e.tensor_tensor(out=sg, in0=sg, in1=stt[:, 20:24].unsqueeze(2).broadcast_to([128, 4, 128]),
                                                   op=ALU.mult), reads=[wb(w, "acc0"), bst], writes=[wb(w, "acc0")])
            yield
            S.op("dve", lambda e: e.tensor_tensor(out=ybf, in0=t1, in1=sg, op=ALU.mult),
                 reads=[wb(w, "F1"), wb(w, "acc0")], writes=[wb(w, "ybf")])
            yield
            for hl in range(4):
                S.op("pe", lambda e, hl=hl: e.transpose(out=tpB[:, hl, :], in_=ybf[:, hl, :], identity=ident[:]),
                     reads=[wb(w, "ybf"), bf("ident")], writes=[bf("tpB")])
            yield
            S.op("act", lambda e: e.activation(out=YR[:, hp * 4:hp * 4 + 4, t * 128:(t + 1) * 128], in_=tpB[:, 0:4, :], func=AF.Copy),
                 reads=[bf("tpB")], writes=[bf("YR")])

        def preA(t, hp, w, slot, j):
            proj_tm(0, j, 0 + hp * 512)
            yield
            rotary(0, slot, w, w["kr"], wb(w, "kr"))
            yield
            proj_tm(1, j, 1024 + hp * 512)
            yield
            v_evac(1, hp, w, False, zt=t)

        def preB(t, hp, w):
            kr, vh = w["kr"], w["vh"]
            for hl in range(4):
                S.op("pe", lambda e, hl=hl: e.matmul(BK[2 + hp][:, hl * 128:(hl + 1) * 128], lhsT=kr[:, hl, :], rhs=vh[:, hl, :],
                                                     start=(t == 0 and hl == 0), stop=(t == NT - 1 and hl == 3)),
                     reads=[wb(w, "kr"), wb(w, "vh")], writes=[bf("bk%d" % (2 + hp))])
            yield

        def pipeline(n, stA, stB, pre_front):
            run(pre_front(-1))
            run(stA(0))
            for u in range(n):
                run(stB(u), stA(u + 1) if u + 1 < n else None, pre_front(u))

        xlw_P = lambda kc, cb: (W3[:, kc, 2048 + cb * 128: 2048 + (cb + 1) * 128], kc * 4 + 2)
        for blk in range(NB):
            units = [(blk * 4 + j, hp) for j in range(4) for hp in range(2)]

            def pf(u, blk=blk, units=units):
                if u == -1:
                    return front_g(x_pre, tab_pre, blk * 4, (blk * 4) % 2, 0, True)
                t, hp = units[u]
                if hp == 0 and (t % 4) < 3:
                    return front_g(x_pre, tab_pre, t + 1, (t + 1) % 2, (t + 1) % 4, True)
                return None

            pipeline(len(units),
                     lambda u, units=units: preA(units[u][0], units[u][1], WS[u % NWS], units[u][0] % 2, units[u][0] % 4),
                     lambda u, units=units: preB(units[u][0], units[u][1], WS[u % NWS]),
                     pf)
            lru_block(False, xlw_P)

        for kc in range(8):
            load_w(kc * 4 + 2, W3[:, kc, 2048:3072], w_in, kc * 128, 0, gin[:, kc:kc + 1], "gin")
        for kc in range(8):
            load_w(kc * 4 + 3, W3[:, kc, 3072:4096], w_in, kc * 128, 3072, gin[:, kc:kc + 1], "gin")

        for hp in range(2):
            S.op("dve", lambda e, hp=hp: e.tensor_scalar(out=S32[:, hp * 4:hp * 4 + 4, :],
                                                         in0=BK[2 + hp][:].rearrange("p (h d) -> p h d", h=4),
                                                         scalar1=flag[:, 0:1], scalar2=None, op0=ALU.mult),
                 reads=[bf("bk%d" % (2 + hp)), bf("flag")], writes=[bf("S32_%d" % hp)])
        hs_all = [bf("hstate%d" % cb) for cb in range(8)]
        S.op("dve", lambda e: e.tensor_scalar(out=hstate[:], in0=hstate[:], scalar1=flag[:, 0:1], scalar2=None, op0=ALU.mult),
             reads=hs_all + [bf("flag")], writes=hs_all)
        S.op("dve", lambda e: e.tensor_scalar(out=hist[:], in0=hist[:], scalar1=flag[:, 0:1], scalar2=None, op0=ALU.mult),
             reads=[bf("hist"), bf("flag")], writes=[bf("hist")])
        for hp in range(2):
            S.op("act", lambda e, hp=hp: e.activation(out=Sbf[:, hp * 4:hp * 4 + 4, :], in_=S32[:, hp * 4:hp * 4 + 4, :], func=AF.Copy),
                 reads=[bf("S32_%d" % hp)], writes=[bf("Sbf%d" % hp)])

        unitsR = [(t, hp) for t in range(NT) for hp in range(2)]

        def pfR(u):
            if u == -1:
                return front_g(x_main, tab_main, 0, 0, 0, True, "R")
            t, hp = unitsR[u]
            if hp == 0 and t + 1 < NT:
                return front_g(x_main, tab_main, t + 1, (t + 1) % 2, (t + 1) % 4, True, "R")
            return None

        pipeline(len(unitsR),
                 lambda u: retA(unitsR[u][0], unitsR[u][1], WS[u % NWS], unitsR[u][0] % 2, unitsR[u][0] % 4),
                 lambda u: retB(unitsR[u][0], unitsR[u][1], WS[u % NWS]),
                 pfR)

        fence(["%s_%d" % (n_, i_) for n_ in ("qr", "vb", "qT", "kT", "Pm", "ybf") for i_ in range(NWS)], ["YL"])
        for kc in range(8):
            load_w(24 + kc, Wf[:, 24 + kc, :], w_in, kc * 128, 4096, gin[:, kc:kc + 1], "gin")
        for kc in range(8):
            load_w(kc, Wf[:, kc, :], w_in, kc * 128, 5120, gin[:, kc:kc + 1], "gin")
        for kc in range(16):
            load_w(8 + kc, Wf[:, 8 + kc, :], w_out, kc * 128, 0)
        small_load(gout, gout_d[:, :], "GX")
        xlw_L = lambda kc, cb: (Wf[:, 24 + kc, cb * 128:(cb + 1) * 128], 24 + kc)
        glw_L = lambda kc, cb: (Wf[:, kc, cb * 128:(cb + 1) * 128], kc)

        for blk in range(NB):
            for j in range(4):
                t = blk * 4 + j
                front(x_main, tab_main, t, t % 2, j, False, "L")
            lru_block(True, xlw_L, glw_L)
            for j in range(4):
                t = blk * 4 + j
                slot = t % 2
                bx = bf("xt%d" % slot)
                S.dma("sp", lambda e, t=t, slot=slot: e.dma_start(out=xt[slot][:], in_=x_main[t * 128:(t + 1) * 128, :]), writes=[bx])
                for ng in range(2):
                    bank = 2 * (j % 2) + ng
                    for kc in range(16):
                        if kc < 8:
                            lhs = YR[:, kc, t * 128:(t + 1) * 128]
                            rd = [bf("YR")]
                        else:
                            lhs = YL[:, kc - 8, j * 128:(j + 1) * 128]
                            rd = [bf("YL")]
                        S.op("pe", lambda e, lhs=lhs, kc=kc, ng=ng, bank=bank: e.matmul(
                            BK[bank][:], lhsT=lhs, rhs=Wf[:, 8 + kc, ng * 512:(ng + 1) * 512], start=(kc == 0), stop=(kc == 15)),
                            reads=rd + [bf("Wc%d" % (8 + kc))], writes=[bf("bk%d" % bank)])
                    S.op("dve", lambda e, slot=slot, ng=ng, bank=bank: e.tensor_tensor(
                        out=xt[slot][:, ng * 512:(ng + 1) * 512], in0=BK[bank][:], in1=xt[slot][:, ng * 512:(ng + 1) * 512], op=ALU.add),
                        reads=[bf("bk%d" % bank), bx], writes=[bx])
                S.op("pool", lambda e: e.memset(ss[:, 1:2], 0.0), writes=[bf("ss2")])
                S.op("act", lambda e, slot=slot: e.activation(out=junk[:], in_=xt[slot][:], func=AF.Square, accum_out=ss[:, 1:2]),
                     reads=[bx, bf("ss2")], writes=[bf("xs"), bf("ss2")])
                S.op("dve", lambda e: e.tensor_scalar(out=rstd[:, 1:2], in0=ss[:, 1:2], scalar1=1.0 / D_MODEL, scalar2=EPS,
                                                      op0=ALU.mult, op1=ALU.add), reads=[bf("ss2")], writes=[bf("rstd2")])
                S.op("pool", lambda e: e.tensor_tensor(out=rstd[:, 1:2], in0=rstd[:, 1:2], in1=neghalf[:, 0:1], op=ALU.pow),
                     reads=[bf("rstd2"), bf("neghalf")], writes=[bf("rstd2")])
                S.op("dve", lambda e, slot=slot: e.scalar_tensor_tensor(out=xt[slot][:], in0=xt[slot][:], scalar=rstd[:, 1:2], in1=gout,
                                                                        op0=ALU.mult, op1=ALU.mult),
                     reads=[bx, bf("rstd2"), bf("GX")], writes=[bx])
                S.dma("act", lambda e, t=t, slot=slot: e.dma_start(out=out_d[t * 128:(t + 1) * 128, :], in_=xt[slot][:]), reads=[bx])

        S.emit()
    return nc


def _rot_table(T):
    d = 128
    inv_freq = (10000.0 ** (-np.arange(0, d, 2, dtype=np.float32) / np.float32(d))).astype(np.float32)
    ang = (np.arange(T, dtype=np.float32)[:, None] * inv_freq[None, :]).astype(np.float32).astype(np.float64)
    c = np.cos(ang)
    s = np.sin(ang)
    return np.concatenate([c, c, -s, s], axis=1).astype(np.float32)


def _consts():
    h = np.arange(NH, dtype=np.float64)
    log_g = np.log1p(-np.exp2(-5.0 - h))
    s = np.arange(128, dtype=np.float64)
    causal = (s[None, :] >= s[:, None]).astype(np.float32)
    cdec = np.exp(-(s[:, None] + 1.0) * log_g[None, :])
    xi = np.exp((s[None, None, :] + 1.0) * log_g[None, :, None]) * np.ones((128, 1, 1))
    zeta = np.exp((127.0 - s[:, None]) * log_g[None, :])
    return causal, cdec.astype(np.float32), xi.astype(np.float32), zeta.astype(np.float32)


_PROG_CACHE = {}


def kernel(x, norm_in_g, w_in, conv_w, conv_b, gate_a_w, gate_a_b, gate_x_w, gate_x_b, lru_lambda, w_out, norm_out_g):
    x = np.asarray(x, dtype=np.float32)
    Bsz, T, Dm = x.shape
    assert Dm == D_MODEL and Bsz * 2 == 8
    TH = T // 2
    NT = TH // 128
    f32 = lambda a: np.ascontiguousarray(np.asarray(a, dtype=np.float32))
    tabfull = _rot_table(T)
    causal, cdec, xi, zeta = _consts()
    lg = np.log1p(-np.exp2(-5.0 - np.arange(NH, dtype=np.float64)))
    pos = np.arange(TH, dtype=np.float64)
    zpre = np.exp((TH - 1.0 - pos)[:, None] * lg[None, :]).reshape(NT, 128, NH).transpose(1, 0, 2)
    zpre = np.ascontiguousarray(zpre.astype(np.float32))
    shared = {
        "w_in": f32(w_in),
        "w_out": f32(w_out),
        "gin": f32(np.asarray(norm_in_g).reshape(8, 128).T),
        "gout": f32(np.broadcast_to(np.asarray(norm_out_g)[None, :], (128, D_MODEL))),
        "convw": f32(np.asarray(conv_w).reshape(4, 8, 128).transpose(2, 1, 0)),
        "convb": f32(np.asarray(conv_b).reshape(8, 128).T),
        "gaw": f32(gate_a_w),
        "gxw": f32(gate_x_w),
        "gab": f32(np.asarray(gate_a_b).T),
        "gxb": f32(np.asarray(gate_x_b).T),
        "lam": f32(np.asarray(lru_lambda).reshape(8, 128).T),
        "ident": np.eye(128, dtype=np.float32).astype(ml_dtypes.bfloat16),
        "causal": causal, "cdec": cdec, "xibc": xi, "zeta": zeta,
    }
    in_maps = []
    for c in range(8):
        b, half = c // 2, c % 2
        m = dict(shared)
        m["x_main"] = np.ascontiguousarray(x[b, half * TH:(half + 1) * TH])
        m["x_pre"] = np.ascontiguousarray(x[b, 0:TH])
        m["tab_main"] = np.ascontiguousarray(tabfull[half * TH:(half + 1) * TH])
        m["tab_pre"] = np.ascontiguousarray(tabfull[0:TH])
        m["flag"] = np.full((128, 1), float(half), dtype=np.float32)
        m["zpre"] = zpre
        in_maps.append(m)
    if NT not in _PROG_CACHE:
        _PROG_CACHE[NT] = build_program(NT)
    nc = _PROG_CACHE[NT]
    res = run_bass_kernel_spmd(nc, in_maps, core_ids=list(range(8)))
    out = np.empty((Bsz, T, D_MODEL), dtype=np.float32)
    for c in range(8):
        b, half = c // 2, c % 2
        out[b, half * TH:(half + 1) * TH] = res.results[c]["out"]
    return out
```

```python
import math
from contextlib import ExitStack

import numpy as np
import ml_dtypes

import concourse.bass as bass
import concourse.mybir as mybir
from concourse.bass_utils import run_bass_kernel_spmd

F32 = mybir.dt.float32
BF16 = mybir.dt.bfloat16
AF = mybir.ActivationFunctionType
ALU = mybir.AluOpType
AX = mybir.AxisListType

D_MODEL = 1024
NH = 8
NCB = 8
EPS = 1e-6


class Buf:
    __slots__ = ("name", "w", "rs")

    def __init__(self, name):
        self.name = name
        self.w = None
        self.rs = []


class _FakeIns:
    def __init__(self, name, out):
        self.name = name
        self.out = out


class _FakeEng:
    def __getattr__(self, name):
        def f(*a, **k):
            out = k.get("out", a[0] if a else None)
            return _FakeIns(name, out)
        return f


class _Op:
    __slots__ = ("idx", "eng", "fn", "dur", "kind", "preds", "succs", "npred", "ready", "start", "fin", "pos", "dsem", "dval")

    def __init__(self, idx, eng, fn, dur, kind):
        self.idx = idx
        self.eng = eng
        self.fn = fn
        self.dur = dur
        self.kind = kind
        self.preds = {}
        self.succs = []
        self.npred = 0
        self.ready = 0.0
        self.start = 0.0
        self.fin = 0.0
        self.pos = 0
        self.dsem = -1
        self.dval = 0


class Sched:
    ENG = ("pe", "act", "dve", "pool", "sp")
    LAT = 0.5

    def __init__(self, nc, stack, n_dma_sems=16):
        self.nc = nc
        self.sem = {e: stack.enter_context(nc.semaphore("s_" + e)) for e in self.ENG}
        self.dsem = [stack.enter_context(nc.semaphore("d%d" % i)) for i in range(n_dma_sems)]
        self.ops = []
        self.labels = None
        self.fake = _FakeEng()

    def _cost(self, eng, fn):
        try:
            info = fn(self.fake)
            ap = info.out
            cols = float(ap.free_size())
            nbytes = float(ap.nbytes())
        except Exception:
            cols, nbytes = 512.0, 65536.0
        if eng == "pe":
            return 0.07 + cols / 2400.0
        if eng == "dve":
            return 0.07 + cols / 960.0
        if eng == "act":
            return 0.2 + cols / 1200.0
        if eng == "pool":
            return 0.25 + cols / 420.0
        return 2.0 + nbytes / 150e3

    def _add(self, eng, fn, reads, writes, kind):
        op = _Op(len(self.ops), eng, fn, self._cost(eng if kind == "op" else "sp", fn), kind)
        preds = op.preds

        def add(p, wait, why=None):
            if p is op:
                return
            preds[p] = preds.get(p, False) or wait
            if self.labels is not None:
                self.labels[(p.idx, op.idx)] = why
        for b in reads:
            if b.w is not None:
                add(b.w, True, "RAW " + b.name)
            if b.name.startswith(("bk", "tp")):
                for r in b.rs:
                    if r.eng != eng or r.kind != "op":
                        add(r, True, "RR " + b.name)
        for b in writes:
            if b.w is not None:
                add(b.w, not (eng == "pe" and b.w.eng == "pe" and b.w.kind == "op" and kind == "op"), "WAW " + b.name)
            for r in b.rs:
                add(r, not (eng == "pe" and r.eng == "pe" and r.kind == "op" and kind == "op"), "WAR " + b.name)
        for b in reads:
            b.rs.append(op)
        for b in writes:
            b.w = op
            b.rs = []
        for p in preds:
            p.succs.append(op)
        op.npred = len(preds)
        self.ops.append(op)
        return op

    def op(self, eng, fn, reads=(), writes=()):
        return self._add(eng, fn, reads, writes, "op")

    def dma(self, eng, fn, reads=(), writes=()):
        return self._add(eng if eng in ("sp", "act") else "sp", fn, reads, writes, "dma")

    PRIO_WINDOW = 0.3

    def _schedule(self):
        import heapq
        bl = [0.0] * len(self.ops)
        for op in reversed(self.ops):
            m = 0.0
            for sx in op.succs:
                v = bl[sx.idx] + (self.LAT if sx.eng != op.eng else 0.05)
                if v > m:
                    m = v
            bl[op.idx] = m + op.dur
        W = self.PRIO_WINDOW
        ready = {e: [] for e in self.ENG}
        free = {e: 0.0 for e in self.ENG}
        order = {e: [] for e in self.ENG}
        for op in self.ops:
            if op.npred == 0:
                ready[op.eng].append(op)
        remaining = len(self.ops)
        while remaining:
            best = None
            for e in self.ENG:
                lst = ready[e]
                if not lst:
                    continue
                t0 = min(max(free[e], o.ready) for o in lst)
                cand = None
                for o in lst:
                    st = max(free[e], o.ready)
                    if st <= t0 + W:
                        k = (-bl[o.idx], o.idx) if W > 0 else (st, o.idx)
                        if cand is None or k < cand[0]:
                            cand = (k, o, st)
                key = (t0, cand[1].idx)
                if best is None or key < best[0]:
                    best = (key, e, cand[1], cand[2])
            assert best is not None, "scheduler deadlock"
            _, e, op, st = best
            ready[e].remove(op)
            op.start = st
            if op.kind == "dma":
                free[e] = op.start + 0.06
                op.fin = op.start + op.dur
            else:
                op.fin = op.start + op.dur
                free[e] = op.fin
            op.pos = len(order[e])
            order[e].append(op)
            remaining -= 1
            for sx in op.succs:
                sx.npred -= 1
                if not sx.preds[op]:
                    r = op.start
                else:
                    r = op.fin + (self.LAT if (sx.eng != op.eng or op.kind == "dma") else 0.2)
                if r > sx.ready:
                    sx.ready = r
                if sx.npred == 0:
                    ready[sx.eng].append(sx)
        self.order = order
        self.est_us = max(op.fin for op in self.ops)

    def emit(self):
        self._schedule()
        order = self.order
        nd = len(self.dsem)
        cnt = 0
        dcnt = [0] * nd
        dlast = [None] * nd
        extra = {}
        k = 0
        all_dmas = sorted((o for e_ in self.ENG for o in order[e_] if o.kind == "dma"), key=lambda o: (o.start, o.idx))
        for op in all_dmas:
            if op.kind == "dma":
                i = k % nd
                k += 1
                dcnt[i] += 16
                op.dsem, op.dval = i, dcnt[i]
                if dlast[i] is not None:
                    extra[op] = dlast[i]
                dlast[i] = op
        idx_on = {}
        for e in self.ENG:
            c = 0
            for op in order[e]:
                if op.kind == "op":
                    c += 1
                    idx_on[op] = c
        prog = {e: [] for e in self.ENG}
        for e in self.ENG:
            seen = {}
            for op in order[e]:
                waits = [p for p, w in op.preds.items() if w]
                if op in extra:
                    waits.append(extra[op])
                for p in waits:
                    if p.kind == "dma":
                        key, val, sem = ("d", p.dsem), p.dval, self.dsem[p.dsem]
                    else:
                        key, val, sem = p.eng, idx_on[p], self.sem[p.eng]
                    if seen.get(key, 0) >= val:
                        continue
                    seen[key] = val
                    prog[e].append(("w", sem, val))
                if op.kind == "dma":
                    prog[e].append(("o", op.fn, self.dsem[op.dsem], 16))
                else:
                    prog[e].append(("o", op.fn, self.sem[e], 1))
            if e == "sp":
                for i in range(nd):
                    if dcnt[i] > 0 and seen.get(("d", i), 0) < dcnt[i]:
                        prog[e].append(("w", self.dsem[i], dcnt[i]))
        nc = self.nc

        def replay(en, items):
            for it in items:
                if it[0] == "w":
                    en.wait_ge(it[1], it[2])
                else:
                    it[1](en).then_inc(it[2], it[3])

        with nc.Block() as block:
            @block.tensor
            def _(en):
                replay(en, prog["pe"])

            @block.scalar
            def _(en):
                replay(en, prog["act"])

            @block.vector
            def _(en):
                replay(en, prog["dve"])

            @block.gpsimd
            def _(en):
                replay(en, prog["pool"])

            @block.sync
            def _(en):
                replay(en, prog["sp"])


NWS = 2


def build_program(NT):
    assert NT % 4 == 0
    NB = NT // 4
    TH = NT * 128
    log_g = [math.log1p(-2.0 ** (-5.0 - h)) for h in range(NH)]
    GAM = [math.exp(128.0 * lg) for lg in log_g]

    nc = bass.Bass("TRN2", target_bir_lowering=False)
    din = lambda n, s, d=F32: nc.dram_tensor(n, s, d, kind="ExternalInput").ap()
    x_main = din("x_main", [TH, D_MODEL])
    x_pre = din("x_pre", [TH, D_MODEL])
    tab_main = din("tab_main", [TH, 256])
    tab_pre = din("tab_pre", [TH, 256])
    w_in = din("w_in", [D_MODEL, 6 * D_MODEL])
    w_out = din("w_out", [2 * D_MODEL, D_MODEL])
    gin_d = din("gin", [128, 8])
    gout_d = din("gout", [128, D_MODEL])
    convw_d = din("convw", [128, 8, 4])
    convb_d = din("convb", [128, 8])
    gaw_d = din("gaw", [8, 128, 128])
    gxw_d = din("gxw", [8, 128, 128])
    gab_d = din("gab", [128, 8])
    gxb_d = din("gxb", [128, 8])
    lam_d = din("lam", [128, 8])
    flag_d = din("flag", [128, 1])
    ident_d = din("ident", [128, 128], BF16)
    causal_d = din("causal", [128, 128])
    cdec_d = din("cdec", [128, 8])
    xibc_d = din("xibc", [128, 8, 128])
    zeta_d = din("zeta", [128, 8])
    zpre_d = din("zpre", [128, NT, 8])
    out_d = nc.dram_tensor("out", [TH, D_MODEL], F32, kind="ExternalOutput").ap()

    with ExitStack() as st:
        S = Sched(nc, st)
        sb = lambda n, s, d=F32: st.enter_context(nc.sbuf_tensor("sb_" + n, s, d))
        ps = lambda n, s, d=F32: st.enter_context(nc.psum_tensor("ps_" + n, s, d))

        Wt = sb("W", [128, 32768], BF16)
        W3 = Wt[:].rearrange("p (k c) -> p k c", c=4096)
        Wf = Wt[:].rearrange("p (k c) -> p k c", c=1024)
        GW = sb("GW", [128, 2, 8, 128], BF16)
        YR = sb("YR", [128, 8, TH], BF16)
        xt = [sb("xt%d" % i, [128, 1024]) for i in range(2)]
        tab = [sb("tab%d" % i, [128, 256]) for i in range(2)]
        xs = sb("xs", [128, 1024], BF16)
        junk = xs
        xnT = sb("xnT", [128, 8, 512], BF16)
        RWall = sb("RWall", [128, max(8192, 3072 * NWS)], BF16)
        YL = RWall[:, 0:4096].rearrange("p (c t) -> p c t", c=8)

        def mkset(i):
            d = {"i": i}
            v4 = lambda k: RWall[:, i * 3072 + k * 512: i * 3072 + (k + 1) * 512].rearrange("p (h d) -> p h d", h=4)
            d["qr"], d["vb"], d["qT"], d["kT"], d["Pm"], d["ybf"] = v4(0), v4(1), v4(2), v4(3), v4(4), v4(5)
            d["kr"] = sb("kr%d" % i, [128, 4, 128], BF16)[:]
            d["vh"] = sb("vh%d" % i, [128, 4, 128], BF16)[:]
            F1 = sb("F1_%d" % i, [128, 512])
            F2 = sb("F2_%d" % i, [128, 512])
            a0 = sb("acc0_%d" % i, [128, 512])
            a1 = sb("acc1_%d" % i, [128, 512])
            xl_ = sb("xl_%d" % i, [128, 515])
            d["xcb"] = sb("xcb_%d" % i, [128, 512], BF16)[:]
            d["stt"] = sb("stt_%d" % i, [128, 32])
            d["t1"] = F1[:].rearrange("p (h d) -> p h d", h=4)
            d["t2"] = F2[:].rearrange("p (h d) -> p h d", h=4)
            d["sg"] = a0[:].rearrange("p (h d) -> p h d", h=4)
            d["thi"], d["a2"], d["hh"], d["av"] = F1[:], F2[:], F2[:], a0[:]
            d["acc"] = [a0[:], a1[:]]
            d["xl"] = xl_[:]
            d["thr"] = xl_[:, 0:512]
            return d

        WS = [mkset(i_) for i_ in range(NWS)]
        S32 = sb("S32", [128, 8, 128])
        Sbf = sb("Sbf", [128, 8, 128], BF16)
        hist = sb("hist", [128, 8, 3])
        hstate = sb("hstate", [128, 8])
        GX = sb("GX", [128, 1024])
        xibc = GX[:].rearrange("p (h d) -> p h d", h=8)
        gout = GX[:]
        causal = sb("causal", [128, 128])
        cdec = sb("cdec", [128, 8])
        ident = sb("ident", [128, 128], BF16)
        gin = sb("gin", [128, 8])
        gink = sb("gink", [128, 8])
        convw = sb("convw", [128, 8, 4])
        convb = sb("convb", [128, 8])
        hba = sb("hba", [128, 8])
        hbx = sb("hbx", [128, 8])
        lam = sb("lam", [128, 8])
        nsp = sb("nsp", [128, 8])
        hnsp = sb("hnsp", [128, 8])
        zeta = sb("zeta", [128, 8])
        zpre = sb("zpre", [128, NT, 8])
        flag = sb("flag", [128, 1])
        neghalf = sb("neghalf", [128, 8])
        ss = sb("ss", [128, 4])
        rstd = sb("rstd", [128, 4])
        rstd_all = sb("rstd_all", [128, NT])
        fz = sb("fz", [128, 2])
        sixteenth = sb("sixteenth", [128, 1])

        tpA = ps("tpA", [128, 8, 128], BF16)
        tpB = ps("tpB", [128, 8, 128], BF16)
        BK = [ps("bk%d" % i, [128, 512]) for i in range(6)]

        B = {}

        def bf(name):
            if name not in B:
                B[name] = Buf(name)
            return B[name]

        def wb(w, name):
            return bf("%s_%d" % (name, w["i"]))

        def fence(old, new):
            S.op("pool", lambda e: e.memset(fz[:, 0:1], 0.0), reads=[bf(n) for n in old], writes=[bf(n) for n in new] + [bf("fz")])

        def small_load(dst, src, name):
            S.dma("sp", lambda e: e.dma_start(out=dst, in_=src), writes=[bf(name)])

        small_load(gin[:], gin_d[:, :], "gin")
        small_load(convw[:], convw_d[:, :, :], "convw")
        small_load(convb[:], convb_d[:, :], "convb")
        small_load(hba[:], gab_d[:, :], "hba")
        small_load(hbx[:], gxb_d[:, :], "hbx")
        small_load(lam[:], lam_d[:, :], "lam")
        small_load(flag[:], flag_d[:, :], "flag")
        small_load(ident[:], ident_d[:, :], "ident")
        small_load(causal[:], causal_d[:, :], "causal")
        small_load(cdec[:], cdec_d[:, :], "cdec")
        small_load(xibc, xibc_d[:, :, :], "GX")
        small_load(zeta[:], zeta_d[:, :], "zeta")
        small_load(zpre[:], zpre_d[:, :, :], "zpre")

        S.op("dve", lambda e: e.memset(neghalf[:], -0.5), writes=[bf("neghalf")])
        S.op("dve", lambda e: e.memset(sixteenth[:], 1.0 / 16.0), writes=[bf("sixteenth")])
        S.op("dve", lambda e: e.memset(S32[:], 0.0), writes=[bf("S32_0"), bf("S32_1")])
        S.op("dve", lambda e: e.memset(hist[:], 0.0), writes=[bf("hist")])
        S.op("dve", lambda e: e.memset(hstate[:], 0.0), writes=[bf("hstate%d" % c_) for c_ in range(8)])
        S.op("dve", lambda e: e.tensor_scalar(out=hba[:], in0=hba[:], scalar1=0.5, scalar2=None, op0=ALU.mult),
             reads=[bf("hba")], writes=[bf("hba")])
        S.op("dve", lambda e: e.tensor_scalar(out=hbx[:], in0=hbx[:], scalar1=0.5, scalar2=None, op0=ALU.mult),
             reads=[bf("hbx")], writes=[bf("hbx")])
        S.op("dve", lambda e: e.tensor_scalar(out=gink[:], in0=gin[:], scalar1=128.0 ** -0.5, scalar2=None, op0=ALU.mult),
             reads=[bf("gin")], writes=[bf("gink")])
        S.op("act", lambda e: e.activation(out=nsp[:], in_=lam[:], func=AF.Exp, scale=-1.0),
             reads=[bf("lam")], writes=[bf("nsp")])
        S.op("act", lambda e: e.activation(out=hnsp[:], in_=nsp[:], func=AF.Ln, bias=1.0),
             reads=[bf("nsp")], writes=[bf("hnsp")])
        S.op("dve", lambda e: e.tensor_scalar(out=nsp[:], in0=hnsp[:], scalar1=-8.0, scalar2=None, op0=ALU.mult),
             reads=[bf("hnsp")], writes=[bf("nsp")])
        S.op("dve", lambda e: e.tensor_scalar(out=hnsp[:], in0=hnsp[:], scalar1=-4.0, scalar2=None, op0=ALU.mult),
             reads=[bf("hnsp"), bf("nsp")], writes=[bf("hnsp")])

        for gi_, gd in enumerate((gaw_d, gxw_d)):
            S.dma("sp", lambda e, gd=gd, gi_=gi_: e.dma_start(out=xt[gi_][:].rearrange("p (n o) -> p n o", n=8),
                                                              in_=gd.rearrange("n i o -> i n o")),
                  writes=[bf("xt%d" % gi_)])
            S.op("dve", lambda e, gi_=gi_: e.tensor_copy(out=GW[:, gi_, :, :],
                                                         in_=xt[gi_][:].rearrange("p (n o) -> p n o", n=8)),
                 reads=[bf("xt%d" % gi_)], writes=[bf("GW")])

        wl_cnt = [0]

        def load_w(chunk, dst, src, r0, c0, scale_ap=None, scale_name=None):
            i = wl_cnt[0] % 2
            wl_cnt[0] += 1
            S.dma("sp", lambda e: e.dma_start(out=xt[i][:], in_=src[r0:r0 + 128, c0:c0 + 1024]),
                  writes=[bf("xt%d" % i)])
            eng = "dve" if (wl_cnt[0] % 2 == 0) else "act"
            if scale_ap is None:
                if eng == "dve":
                    S.op(eng, lambda e: e.tensor_copy(out=dst, in_=xt[i][:]), reads=[bf("xt%d" % i)], writes=[bf("Wc%d" % chunk)])
                else:
                    S.op(eng, lambda e: e.activation(out=dst, in_=xt[i][:], func=AF.Copy), reads=[bf("xt%d" % i)], writes=[bf("Wc%d" % chunk)])
            else:
                if eng == "dve":
                    S.op(eng, lambda e: e.tensor_scalar(out=dst, in0=xt[i][:], scalar1=scale_ap, scalar2=None, op0=ALU.mult),
                         reads=[bf("xt%d" % i), bf(scale_name)], writes=[bf("Wc%d" % chunk)])
                else:
                    S.op(eng, lambda e: e.activation(out=dst, in_=xt[i][:], func=AF.Copy, scale=scale_ap),
                         reads=[bf("xt%d" % i), bf(scale_name)], writes=[bf("Wc%d" % chunk)])

        for kc in range(8):
            load_w(kc * 4 + 0, W3[:, kc, 0:1024], w_in, kc * 128, 1024, gink[:, kc:kc + 1], "gink")
        for kc in range(8):
            load_w(kc * 4 + 1, W3[:, kc, 1024:2048], w_in, kc * 128, 2048, gin[:, kc:kc + 1], "gin")
        for kc in range(8):
            load_w(kc * 4 + 2, W3[:, kc, 2048:3072], w_in, kc * 128, 4096, gin[:, kc:kc + 1], "gin")

        def front_g(src, tabsrc, t, slot, j, need_tab, mode="P"):
            bx = bf("xt%d" % slot)
            S.dma("sp", lambda e: e.dma_start(out=xt[slot][:], in_=src[t * 128:(t + 1) * 128, :]), writes=[bx])
            if need_tab:
                S.dma("sp", lambda e: e.dma_start(out=tab[slot][:], in_=tabsrc[t * 128:(t + 1) * 128, :]),
                      writes=[bf("tab%d" % slot)])
            if mode == "P":
                rs, rsb = rstd[:, 0:1], bf("rstd")
            else:
                rs, rsb = rstd_all[:, t:t + 1], bf("rsa%d" % t)
            if mode != "L":
                S.op("pool", lambda e: e.memset(ss[:, 0:1], 0.0), writes=[bf("ss")])
                S.op("act", lambda e: e.activation(out=junk[:], in_=xt[slot][:], func=AF.Square, accum_out=ss[:, 0:1]),
                     reads=[bx, bf("ss")], writes=[bf("xs"), bf("ss")])
                yield
                S.op("dve", lambda e: e.tensor_scalar(out=rs, in0=ss[:, 0:1], scalar1=1.0 / D_MODEL, scalar2=EPS,
                                                      op0=ALU.mult, op1=ALU.add), reads=[bf("ss")], writes=[rsb])
                S.op("pool", lambda e: e.tensor_tensor(out=rs, in0=rs, in1=neghalf[:, 0:1], op=ALU.pow),
                     reads=[rsb, bf("neghalf")], writes=[rsb])
                yield
            S.op("act", lambda e: e.activation(out=xs[:], in_=xt[slot][:], func=AF.Copy, scale=rs),
                 reads=[bx, rsb], writes=[bf("xs")])
            yield
            for kc in range(8):
                S.op("pe", lambda e, kc=kc: e.transpose(out=tpA[:, kc, :], in_=xs[:, kc * 128:(kc + 1) * 128],
                                                        identity=ident[:]),
                     reads=[bf("xs"), bf("ident")], writes=[bf("tpA")])
            yield
            S.op("dve", lambda e: e.tensor_copy(out=xnT[:, :, j * 128:(j + 1) * 128], in_=tpA[:]),
                 reads=[bf("tpA")], writes=[bf("xnT%d" % j)])

        def run(*gens):
            gens = [g for g in gens if g is not None]
            while gens:
                for g in list(gens):
                    try:
                        next(g)
                    except StopIteration:
                        gens.remove(g)

        def front(*a):
            run(front_g(*a))

        def proj_tm(bank, j, wcol):
            for kc in range(8):
                S.op("pe", lambda e, kc=kc: e.matmul(BK[bank][:], lhsT=xnT[:, kc, j * 128:(j + 1) * 128],
                                                     rhs=W3[:, kc, wcol:wcol + 512], start=(kc == 0), stop=(kc == 7)),
                     reads=[bf("xnT%d" % j), bf("Wc%d" % (kc * 4 + wcol // 1024))], writes=[bf("bk%d" % bank)])

        def rotary(bank, slot, w, dst, dst_buf):
            src = BK[bank][:].rearrange("p (h d) -> p h d", h=4)
            tb = bf("tab%d" % slot)
            bb = bf("bk%d" % bank)
            t1, t2 = w["t1"], w["t2"]
            S.op("dve", lambda e: e.tensor_tensor(out=t1, in0=src,
                                                  in1=tab[slot][:, 0:128].unsqueeze(1).broadcast_to([128, 4, 128]),
                                                  op=ALU.mult), reads=[bb, tb], writes=[wb(w, "F1")])
            S.op("dve", lambda e: e.tensor_tensor(out=t2[:, :, 0:64], in0=src[:, :, 64:128],
                                                  in1=tab[slot][:, 128:192].unsqueeze(1).broadcast_to([128, 4, 64]),
                                                  op=ALU.mult), reads=[bb, tb], writes=[wb(w, "F2")])
            S.op("dve", lambda e: e.tensor_tensor(out=t2[:, :, 64:128], in0=src[:, :, 0:64],
                                                  in1=tab[slot][:, 192:256].unsqueeze(1).broadcast_to([128, 4, 64]),
                                                  op=ALU.mult), reads=[bb, tb], writes=[wb(w, "F2")])
            S.op("pool", lambda e: e.tensor_tensor(out=dst, in0=t1, in1=t2, op=ALU.add),
                 reads=[wb(w, "F1"), wb(w, "F2")], writes=[dst_buf])

        def v_evac(bank, hp, w, plain, zt=None):
            bb = bf("bk%d" % bank)
            src = BK[bank][:].rearrange("p (h d) -> p h d", h=4)
            if plain:
                S.op("act", lambda e: e.activation(out=w["vb"], in_=src, func=AF.Copy), reads=[bb], writes=[wb(w, "vb")])
            zap = zeta[:, hp * 4:hp * 4 + 4] if zt is None else zpre[:, zt, hp * 4:hp * 4 + 4]
            S.op("dve", lambda e: e.tensor_tensor(out=w["vh"], in0=src,
                                                  in1=zap.unsqueeze(2).broadcast_to([128, 4, 128]),
                                                  op=ALU.mult), reads=[bb, bf("zeta"), bf("zpre")], writes=[wb(w, "vh")])

        def kv_update(hp, w, make_bf):
            kr, vh = w["kr"], w["vh"]
            for hl in range(4):
                S.op("pe", lambda e, hl=hl: e.matmul(BK[4][:, hl * 128:(hl + 1) * 128], lhsT=kr[:, hl, :], rhs=vh[:, hl, :],
                                                     start=True, stop=True),
                     reads=[wb(w, "kr"), wb(w, "vh")], writes=[bf("bk4")])
            for hl in range(4):
                h = hp * 4 + hl
                S.op("dve", lambda e, hl=hl, h=h: e.scalar_tensor_tensor(out=S32[:, h, :], in0=S32[:, h, :], scalar=float(GAM[h]),
                                                                         in1=BK[4][:, hl * 128:(hl + 1) * 128],
                                                                         op0=ALU.mult, op1=ALU.add),
                     reads=[bf("S32_%d" % hp), bf("bk4")], writes=[bf("S32_%d" % hp)])
            if make_bf:
                S.op("act", lambda e: e.activation(out=Sbf[:, hp * 4:hp * 4 + 4, :], in_=S32[:, hp * 4:hp * 4 + 4, :], func=AF.Copy),
                     reads=[bf("S32_%d" % hp)], writes=[bf("Sbf%d" % hp)])

        xall = [bf("xnT%d" % j) for j in range(4)]

        def lruA(cb, w, full, xlw, glw):
            pa = cb % 2
            pg = 2 + (cb % 2)
            xl, acc = w["xl"], w["acc"]
            for kc in range(8):
                S.op("pe", lambda e, kc=kc: e.matmul(BK[pa][:], lhsT=xlw(kc, cb)[0], rhs=xnT[:, kc, :], start=(kc == 0), stop=(kc == 7)),
                     reads=[bf("Wc%d" % xlw(kc, cb)[1])] + xall, writes=[bf("bk%d" % pa)])
            yield
            if full:
                for kc in range(8):
                    S.op("pe", lambda e, kc=kc: e.matmul(BK[pg][:], lhsT=glw(kc, cb)[0], rhs=xnT[:, kc, :], start=(kc == 0), stop=(kc == 7)),
                         reads=[bf("Wc%d" % glw(kc, cb)[1])] + xall, writes=[bf("bk%d" % pg)])
                yield
            S.op("pool", lambda e: e.tensor_copy(out=xl[:, 0:3], in_=hist[:, cb, :]), reads=[bf("hist")], writes=[wb(w, "xl")])
            if full:
                S.op("act", lambda e: e.activation(out=xl[:, 3:515], in_=BK[pa][:], func=AF.Copy),
                     reads=[bf("bk%d" % pa)], writes=[wb(w, "xl")])
            else:
                S.op("dve", lambda e: e.tensor_copy(out=xl[:, 3:515], in_=BK[pa][:]),
                     reads=[bf("bk%d" % pa)], writes=[wb(w, "xl")])
            S.op("pool", lambda e: e.tensor_copy(out=hist[:, cb, :], in_=xl[:, 512:515]), reads=[wb(w, "xl")], writes=[bf("hist")])
            yield
            S.op("act", lambda e: e.activation(out=acc[0], in_=xl[:, 3:515], func=AF.Identity,
                                               scale=convw[:, cb, 3:4], bias=convb[:, cb:cb + 1]),
                 reads=[wb(w, "xl"), bf("convw"), bf("convb")], writes=[wb(w, "acc0")])
            yield
            for n_, j_ in enumerate((2, 1, 0)):
                src_i = n_ % 2
                dst_i = 1 - src_i
                S.op("dve", lambda e, j_=j_, src_i=src_i, dst_i=dst_i: e.scalar_tensor_tensor(
                    out=acc[dst_i], in0=xl[:, j_:j_ + 512], scalar=convw[:, cb, j_:j_ + 1], in1=acc[src_i],
                    op0=ALU.mult, op1=ALU.add),
                    reads=[wb(w, "xl"), bf("convw"), wb(w, "acc%d" % src_i)], writes=[wb(w, "acc%d" % dst_i)])
                yield
            S.op("act", lambda e: e.activation(out=w["xcb"], in_=acc[1], func=AF.Copy), reads=[wb(w, "acc1")], writes=[wb(w, "xcb")])

        def lruB(cb, w, full):
            pg = 2 + (cb % 2)
            xc, thr, thi, av, a2, hh, xcb = w["acc"][1], w["thr"], w["thi"], w["av"], w["a2"], w["hh"], w["xcb"]
            bxc = wb(w, "acc1")
            S.op("pe", lambda e: e.matmul(BK[4][:], lhsT=GW[:, 0, cb, :], rhs=xcb, start=True, stop=True),
                 reads=[bf("GW"), wb(w, "xcb")], writes=[bf("bk4")])
            S.op("pe", lambda e: e.matmul(BK[5][:], lhsT=GW[:, 1, cb, :], rhs=xcb, start=True, stop=True),
                 reads=[bf("GW"), wb(w, "xcb")], writes=[bf("bk5")])
            yield
            S.op("act", lambda e: e.activation(out=thr, in_=BK[4][:], func=AF.Tanh, scale=0.5, bias=hba[:, cb:cb + 1]),
                 reads=[bf("bk4"), bf("hba")], writes=[wb(w, "xl")])
            S.op("act", lambda e: e.activation(out=thi, in_=BK[5][:], func=AF.Tanh, scale=0.5, bias=hbx[:, cb:cb + 1]),
                 reads=[bf("bk5"), bf("hbx")], writes=[wb(w, "F1")])
            yield
            S.op("act", lambda e: e.activation(out=av, in_=thr, func=AF.Exp, scale=hnsp[:, cb:cb + 1], bias=hnsp[:, cb:cb + 1]),
                 reads=[wb(w, "xl"), bf("hnsp")], writes=[wb(w, "acc0")])
            S.op("dve", lambda e: e.tensor_tensor(out=a2, in0=av, in1=av, op=ALU.mult),
                 reads=[wb(w, "acc0")], writes=[wb(w, "F2")])
            yield
            S.op("dve", lambda e: e.tensor_scalar(out=a2, in0=a2, scalar1=1.0, scalar2=-1.0 / 16.0, op0=ALU.min, op1=ALU.mult),
                 reads=[wb(w, "F2")], writes=[wb(w, "F2")])
            yield
            S.op("act", lambda e: e.activation(out=a2, in_=a2, func=AF.Sqrt, bias=sixteenth[:, 0:1]),
                 reads=[wb(w, "F2"), bf("sixteenth")], writes=[wb(w, "F2")])
            yield
            S.op("dve", lambda e: e.scalar_tensor_tensor(out=thi, in0=thi, scalar=1.0, in1=xc, op0=ALU.add, op1=ALU.mult),
                 reads=[wb(w, "F1"), bxc], writes=[wb(w, "F1")])
            S.op("dve", lambda e: e.tensor_tensor(out=thi, in0=thi, in1=a2, op=ALU.mult),
                 reads=[wb(w, "F1"), wb(w, "F2")], writes=[wb(w, "F1")])
            yield
            S.op("dve", lambda e: e.tensor_tensor_scan(out=hh, data0=av, data1=thi, initial=hstate[:, cb:cb + 1],
                                                       op0=ALU.mult, op1=ALU.add),
                 reads=[wb(w, "acc0"), wb(w, "F1"), bf("hstate%d" % cb)], writes=[wb(w, "F2")])
            S.op("pool", lambda e: e.tensor_copy(out=hstate[:, cb:cb + 1], in_=hh[:, 511:512]),
                 reads=[wb(w, "F2")], writes=[bf("hstate%d" % cb)])
            yield
            if full:
                S.op("act", lambda e: e.activation(out=thr, in_=BK[pg][:], func=AF.Tanh, scale=0.5),
                     reads=[bf("bk%d" % pg)], writes=[wb(w, "xl")])
                yield
                S.op("dve", lambda e: e.scalar_tensor_tensor(out=thr, in0=thr, scalar=1.0, in1=BK[pg][:], op0=ALU.add, op1=ALU.mult),
                     reads=[wb(w, "xl"), bf("bk%d" % pg)], writes=[wb(w, "xl")])
                S.op("dve", lambda e: e.tensor_tensor(out=YL[:, cb, :], in0=thr, in1=hh, op=ALU.mult),
                     reads=[wb(w, "xl"), wb(w, "F2")], writes=[bf("YL")])

        def lru_block(full, xlw, glw=None):
            run(lruA(0, WS[0], full, xlw, glw))
            for cb in range(8):
                run(lruB(cb, WS[cb % NWS], full),
                    lruA(cb + 1, WS[(cb + 1) % NWS], full, xlw, glw) if cb + 1 < 8 else None)

        def retA(t, hp, w, slot, j):
            proj_tm(0, j, 2048 + hp * 512)
            yield
            rotary(0, slot, w, w["qr"], wb(w, "qr"))
            yield
            proj_tm(1, j, 0 + hp * 512)
            yield
            rotary(1, slot, w, w["kr"], wb(w, "kr"))
            yield
            proj_tm(2, j, 1024 + hp * 512)
            yield
            v_evac(2, hp, w, True)
            yield
            proj_tm(2, j, 3072 + hp * 512)
            yield
            S.op("act", lambda e: e.activation(out=w["sg"], in_=BK[2][:].rearrange("p (h d) -> p h d", h=4), func=AF.Silu),
                 reads=[bf("bk2")], writes=[wb(w, "acc0")])

        def retB(t, hp, w):
            qr, kr, vb, qT, kT, Pm, ybf, sg, t1, t2 = (w[k] for k in ("qr", "kr", "vb", "qT", "kT", "Pm", "ybf", "sg", "t1", "t2"))
            stt = w["stt"]
            bst = wb(w, "stt")
            for hl in range(4):
                S.op("pe", lambda e, hl=hl: e.transpose(out=tpB[:, hl, :], in_=qr[:, hl, :], identity=ident[:]),
                     reads=[wb(w, "qr"), bf("ident")], writes=[bf("tpB")])
            for hl in range(4):
                S.op("pe", lambda e, hl=hl: e.transpose(out=tpB[:, 4 + hl, :], in_=kr[:, hl, :], identity=ident[:]),
                     reads=[wb(w, "kr"), bf("ident")], writes=[bf("tpB")])
            yield
            S.op("dve", lambda e: e.tensor_tensor(out=qT, in0=tpB[:, 0:4, :], in1=xibc[:, hp * 4:hp * 4 + 4, :], op=ALU.mult),
                 reads=[bf("tpB"), bf("GX")], writes=[wb(w, "qT")])
            S.op("act", lambda e: e.activation(out=kT, in_=tpB[:, 4:8, :], func=AF.Copy),
                 reads=[bf("tpB")], writes=[wb(w, "kT")])
            yield
            for hl in range(4):
                S.op("pe", lambda e, hl=hl: e.matmul(BK[5][:, hl * 128:(hl + 1) * 128], lhsT=kT[:, hl, :], rhs=qT[:, hl, :],
                                                     start=True, stop=True),
                     reads=[wb(w, "kT"), wb(w, "qT")], writes=[bf("bk5")])
            yield
            for hl in range(4):
                h = hp * 4 + hl
                S.op("dve", lambda e, hl=hl, h=h: e.scalar_tensor_tensor(out=Pm[:, hl, :], in0=BK[5][:, hl * 128:(hl + 1) * 128],
                                                                         scalar=cdec[:, h:h + 1], in1=causal[:],
                                                                         op0=ALU.mult, op1=ALU.mult),
                     reads=[bf("bk5"), bf("cdec"), bf("causal")], writes=[wb(w, "Pm")])
            yield
            for hl in range(4):
                h = hp * 4 + hl
                S.op("pe", lambda e, hl=hl: e.matmul(BK[3][:, hl * 128:(hl + 1) * 128], lhsT=Pm[:, hl, :], rhs=vb[:, hl, :],
                                                     start=True, stop=False),
                     reads=[wb(w, "Pm"), wb(w, "vb")], writes=[bf("bk3")])
                S.op("pe", lambda e, hl=hl, h=h: e.matmul(BK[3][:, hl * 128:(hl + 1) * 128], lhsT=qT[:, hl, :], rhs=Sbf[:, h, :],
                                                          start=False, stop=True),
                     reads=[wb(w, "qT"), bf("Sbf%d" % hp)], writes=[bf("bk3")])
            yield
            kv_update(hp, w, True)
            yield
            po = BK[3][:].rearrange("p (h d) -> p h d", h=4)
            S.op("dve", lambda e: e.tensor_reduce(out=stt[:, 0:4], in_=po, axis=AX.X, op=ALU.add), reads=[bf("bk3")], writes=[bst])
            S.op("act", lambda e: e.activation(out=t2, in_=po, func=AF.Square), reads=[bf("bk3")], writes=[wb(w, "F2")])
            yield
            S.op("dve", lambda e: e.tensor_reduce(out=stt[:, 4:8], in_=t2, axis=AX.X, op=ALU.add),
                 reads=[wb(w, "F2"), bst], writes=[bst])
            S.op("dve", lambda e: e.tensor_scalar(out=stt[:, 8:12], in0=stt[:, 0:4], scalar1=1.0 / 128.0, scalar2=None, op0=ALU.mult),
                 reads=[bst], writes=[bst])
            S.op("dve", lambda e: e.tensor_tensor(out=stt[:, 12:16], in0=stt[:, 8:12], in1=stt[:, 8:12], op=ALU.mult),
                 reads=[bst], writes=[bst])
            S.op("dve", lambda e: e.scalar_tensor_tensor(out=stt[:, 16:20], in0=stt[:, 4:8], scalar=1.0 / 128.0, in1=stt[:, 12:16],
                                                         op0=ALU.mult, op1=ALU.subtract), reads=[bst], writes=[bst])
            S.op("dve", lambda e: e.tensor_scalar(out=stt[:, 16:20], in0=stt[:, 16:20], scalar1=0.0, scalar2=EPS, op0=ALU.max, op1=ALU.add),
                 reads=[bst], writes=[bst])
            yield
            S.op("pool", lambda e: e.tensor_tensor(out=stt[:, 20:24], in0=stt[:, 16:20], in1=neghalf[:, 0:4], op=ALU.pow),
                 reads=[bst, bf("neghalf")], writes=[bst])
            yield
            S.op("dve", lambda e: e.tensor_tensor(out=t1, in0=po, in1=stt[:, 8:12].unsqueeze(2).broadcast_to([128, 4, 128]),
                                                  op=ALU.subtract), reads=[bf("bk3"), bst], writes=[wb(w, "F1")])
            S.op("pool", lambda e: e.tensor_tensor(out=sg, in0=sg, in1=stt[:, 20:24].unsqueeze(2).broadcast_to([128, 4, 128]),
                                                   op=ALU.mult), reads=[wb(w, "acc0"), bst], writes=[wb(w, "acc0")])
            yield
            S.op("dve", lambda e: e.tensor_tensor(out=ybf, in0=t1, in1=sg, op=ALU.mult),
                 reads=[wb(w, "F1"), wb(w, "acc0")], writes=[wb(w, "ybf")])
            yield
            for hl in range(4):
                S.op("pe", lambda e, hl=hl: e.transpose(out=tpA[:, hl, :], in_=ybf[:, hl, :], identity=ident[:]),
                     reads=[wb(w, "ybf"), bf("ident")], writes=[bf("tpA")])
            yield
            S.op("act", lambda e: e.activation(out=YR[:, hp * 4:hp * 4 + 4, t * 128:(t + 1) * 128], in_=tpA[:, 0:4, :], func=AF.Copy),
                 reads=[bf("tpA")], writes=[bf("YR")])

        def preA(t, hp, w, slot, j):
            proj_tm(0, j, 0 + hp * 512)
            yield
            rotary(0, slot, w, w["kr"], wb(w, "kr"))
            yield
            proj_tm(1, j, 1024 + hp * 512)
            yield
            v_evac(1, hp, w, False, zt=t)

        def preB(t, hp, w):
            kr, vh = w["kr"], w["vh"]
            for hl in range(4):
                S.op("pe", lambda e, hl=hl: e.matmul(BK[2 + hp][:, hl * 128:(hl + 1) * 128], lhsT=kr[:, hl, :], rhs=vh[:, hl, :],
                                                     start=(t == 0 and hl == 0), stop=(t == NT - 1 and hl == 3)),
                     reads=[wb(w, "kr"), wb(w, "vh")], writes=[bf("bk%d" % (2 + hp))])
            yield

        def pipeline(n, stA, stB, pre_front):
            run(pre_front(-1))
            run(stA(0))
            for u in range(n):
                run(stB(u), stA(u + 1) if u + 1 < n else None, pre_front(u))

        xlw_P = lambda kc, cb: (W3[:, kc, 2048 + cb * 128: 2048 + (cb + 1) * 128], kc * 4 + 2)
        for blk in range(NB):
            units = [(blk * 4 + j, hp) for j in range(4) for hp in range(2)]

            def pf(u, blk=blk, units=units):
                if u == -1:
                    return front_g(x_pre, tab_pre, blk * 4, (blk * 4) % 2, 0, True)
                t, hp = units[u]
                if hp == 0 and (t % 4) < 3:
                    return front_g(x_pre, tab_pre, t + 1, (t + 1) % 2, (t + 1) % 4, True)
                return None

            pipeline(len(units),
                     lambda u, units=units: preA(units[u][0], units[u][1], WS[u % NWS], units[u][0] % 2, units[u][0] % 4),
                     lambda u, units=units: preB(units[u][0], units[u][1], WS[u % NWS]),
                     pf)
            lru_block(False, xlw_P)

        for kc in range(8):
            load_w(kc * 4 + 2, W3[:, kc, 2048:3072], w_in, kc * 128, 0, gin[:, kc:kc + 1], "gin")
        for kc in range(8):
            load_w(kc * 4 + 3, W3[:, kc, 3072:4096], w_in, kc * 128, 3072, gin[:, kc:kc + 1], "gin")

        for hp in range(2):
            S.op("dve", lambda e, hp=hp: e.tensor_scalar(out=S32[:, hp * 4:hp * 4 + 4, :],
                                                         in0=BK[2 + hp][:].rearrange("p (h d) -> p h d", h=4),
                                                         scalar1=flag[:, 0:1], scalar2=None, op0=ALU.mult),
                 reads=[bf("bk%d" % (2 + hp)), bf("flag")], writes=[bf("S32_%d" % hp)])
        hs_all = [bf("hstate%d" % cb) for cb in range(8)]
        S.op("dve", lambda e: e.tensor_scalar(out=hstate[:], in0=hstate[:], scalar1=flag[:, 0:1], scalar2=None, op0=ALU.mult),
             reads=hs_all + [bf("flag")], writes=hs_all)
        S.op("dve", lambda e: e.tensor_scalar(out=hist[:], in0=hist[:], scalar1=flag[:, 0:1], scalar2=None, op0=ALU.mult),
             reads=[bf("hist"), bf("flag")], writes=[bf("hist")])
        for hp in range(2):
            S.op("act", lambda e, hp=hp: e.activation(out=Sbf[:, hp * 4:hp * 4 + 4, :], in_=S32[:, hp * 4:hp * 4 + 4, :], func=AF.Copy),
                 reads=[bf("S32_%d" % hp)], writes=[bf("Sbf%d" % hp)])

        unitsR = [(t, hp) for t in range(NT) for hp in range(2)]

        def pfR(u):
            if u == -1:
                return front_g(x_main, tab_main, 0, 0, 0, True, "R")
            t, hp = unitsR[u]
            if hp == 0 and t + 1 < NT:
                return front_g(x_main, tab_main, t + 1, (t + 1) % 2, (t + 1) % 4, True, "R")
            return None

        pipeline(len(unitsR),
                 lambda u: retA(unitsR[u][0], unitsR[u][1], WS[u % NWS], unitsR[u][0] % 2, unitsR[u][0] % 4),
                 lambda u: retB(unitsR[u][0], unitsR[u][1], WS[u % NWS]),
                 pfR)

        fence(["%s_%d" % (n_, i_) for n_ in ("qr", "vb", "qT", "kT", "Pm", "ybf") for i_ in range(NWS)], ["YL"])
        for kc in range(8):
            load_w(24 + kc, Wf[:, 24 + kc, :], w_in, kc * 128, 4096, gin[:, kc:kc + 1], "gin")
        for kc in range(8):
            load_w(kc, Wf[:, kc, :], w_in, kc * 128, 5120, gin[:, kc:kc + 1], "gin")
        for kc in range(16):
            load_w(8 + kc, Wf[:, 8 + kc, :], w_out, kc * 128, 0)
        small_load(gout, gout_d[:, :], "GX")
        xlw_L = lambda kc, cb: (Wf[:, 24 + kc, cb * 128:(cb + 1) * 128], 24 + kc)
        glw_L = lambda kc, cb: (Wf[:, kc, cb * 128:(cb + 1) * 128], kc)

        for blk in range(NB):
            for j in range(4):
                t = blk * 4 + j
                front(x_main, tab_main, t, t % 2, j, False, "L")
            lru_block(True, xlw_L, glw_L)
            for j in range(4):
                t = blk * 4 + j
                slot = t % 2
                bx = bf("xt%d" % slot)
                S.dma("sp", lambda e, t=t, slot=slot: e.dma_start(out=xt[slot][:], in_=x_main[t * 128:(t + 1) * 128, :]), writes=[bx])
                for ng in range(2):
                    bank = 2 * (j % 2) + ng
                    for kc in range(16):
                        if kc < 8:
                            lhs = YR[:, kc, t * 128:(t + 1) * 128]
                            rd = [bf("YR")]
                        else:
                            lhs = YL[:, kc - 8, j * 128:(j + 1) * 128]
                            rd = [bf("YL")]
                        S.op("pe", lambda e, lhs=lhs, kc=kc, ng=ng, bank=bank: e.matmul(
                            BK[bank][:], lhsT=lhs, rhs=Wf[:, 8 + kc, ng * 512:(ng + 1) * 512], start=(kc == 0), stop=(kc == 15)),
                            reads=rd + [bf("Wc%d" % (8 + kc))], writes=[bf("bk%d" % bank)])
                    S.op("dve", lambda e, slot=slot, ng=ng, bank=bank: e.tensor_tensor(
                        out=xt[slot][:, ng * 512:(ng + 1) * 512], in0=BK[bank][:], in1=xt[slot][:, ng * 512:(ng + 1) * 512], op=ALU.add),
                        reads=[bf("bk%d" % bank), bx], writes=[bx])
                S.op("pool", lambda e: e.memset(ss[:, 1:2], 0.0), writes=[bf("ss2")])
                S.op("act", lambda e, slot=slot: e.activation(out=junk[:], in_=xt[slot][:], func=AF.Square, accum_out=ss[:, 1:2]),
                     reads=[bx, bf("ss2")], writes=[bf("xs"), bf("ss2")])
                S.op("dve", lambda e: e.tensor_scalar(out=rstd[:, 1:2], in0=ss[:, 1:2], scalar1=1.0 / D_MODEL, scalar2=EPS,
                                                      op0=ALU.mult, op1=ALU.add), reads=[bf("ss2")], writes=[bf("rstd2")])
                S.op("pool", lambda e: e.tensor_tensor(out=rstd[:, 1:2], in0=rstd[:, 1:2], in1=neghalf[:, 0:1], op=ALU.pow),
                     reads=[bf("rstd2"), bf("neghalf")], writes=[bf("rstd2")])
                S.op("dve", lambda e, slot=slot: e.scalar_tensor_tensor(out=xt[slot][:], in0=xt[slot][:], scalar=rstd[:, 1:2], in1=gout,
                                                                        op0=ALU.mult, op1=ALU.mult),
                     reads=[bx, bf("rstd2"), bf("GX")], writes=[bx])
                S.dma("act", lambda e, t=t, slot=slot: e.dma_start(out=out_d[t * 128:(t + 1) * 128, :], in_=xt[slot][:]), reads=[bx])

        S.emit()
    return nc


def _rot_table(T):
    d = 128
    inv_freq = (10000.0 ** (-np.arange(0, d, 2, dtype=np.float32) / np.float32(d))).astype(np.float32)
    ang = (np.arange(T, dtype=np.float32)[:, None] * inv_freq[None, :]).astype(np.float32).astype(np.float64)
    c = np.cos(ang)
    s = np.sin(ang)
    return np.concatenate([c, c, -s, s], axis=1).astype(np.float32)


def _consts():
    h = np.arange(NH, dtype=np.float64)
    log_g = np.log1p(-np.exp2(-5.0 - h))
    s = np.arange(128, dtype=np.float64)
    causal = (s[None, :] >= s[:, None]).astype(np.float32)
    cdec = np.exp(-(s[:, None] + 1.0) * log_g[None, :])
    xi = np.exp((s[None, None, :] + 1.0) * log_g[None, :, None]) * np.ones((128, 1, 1))
    zeta = np.exp((127.0 - s[:, None]) * log_g[None, :])
    return causal, cdec.astype(np.float32), xi.astype(np.float32), zeta.astype(np.float32)


_PROG_CACHE = {}


def kernel(x, norm_in_g, w_in, conv_w, conv_b, gate_a_w, gate_a_b, gate_x_w, gate_x_b, lru_lambda, w_out, norm_out_g):
    x = np.asarray(x, dtype=np.float32)
    Bsz, T, Dm = x.shape
    assert Dm == D_MODEL and Bsz * 2 == 8
    TH = T // 2
    NT = TH // 128
    f32 = lambda a: np.ascontiguousarray(np.asarray(a, dtype=np.float32))
    tabfull = _rot_table(T)
    causal, cdec, xi, zeta = _consts()
    lg = np.log1p(-np.exp2(-5.0 - np.arange(NH, dtype=np.float64)))
    pos = np.arange(TH, dtype=np.float64)
    zpre = np.exp((TH - 1.0 - pos)[:, None] * lg[None, :]).reshape(NT, 128, NH).transpose(1, 0, 2)
    zpre = np.ascontiguousarray(zpre.astype(np.float32))
    shared = {
        "w_in": f32(w_in),
        "w_out": f32(w_out),
        "gin": f32(np.asarray(norm_in_g).reshape(8, 128).T),
        "gout": f32(np.broadcast_to(np.asarray(norm_out_g)[None, :], (128, D_MODEL))),
        "convw": f32(np.asarray(conv_w).reshape(4, 8, 128).transpose(2, 1, 0)),
        "convb": f32(np.asarray(conv_b).reshape(8, 128).T),
        "gaw": f32(gate_a_w),
        "gxw": f32(gate_x_w),
        "gab": f32(np.asarray(gate_a_b).T),
        "gxb": f32(np.asarray(gate_x_b).T),
        "lam": f32(np.asarray(lru_lambda).reshape(8, 128).T),
        "ident": np.eye(128, dtype=np.float32).astype(ml_dtypes.bfloat16),
        "causal": causal, "cdec": cdec, "xibc": xi, "zeta": zeta,
    }
    in_maps = []
    for c in range(8):
        b, half = c // 2, c % 2
        m = dict(shared)
        m["x_main"] = np.ascontiguousarray(x[b, half * TH:(half + 1) * TH])
        m["x_pre"] = np.ascontiguousarray(x[b, 0:TH])
        m["tab_main"] = np.ascontiguousarray(tabfull[half * TH:(half + 1) * TH])
        m["tab_pre"] = np.ascontiguousarray(tabfull[0:TH])
        m["flag"] = np.full((128, 1), float(half), dtype=np.float32)
        m["zpre"] = zpre
        in_maps.append(m)
    if NT not in _PROG_CACHE:
        _PROG_CACHE[NT] = build_program(NT)
    nc = _PROG_CACHE[NT]
    res = run_bass_kernel_spmd(nc, in_maps, core_ids=list(range(8)))
    out = np.empty((Bsz, T, D_MODEL), dtype=np.float32)
    for c in range(8):
        b, half = c // 2, c % 2
        out[b, half * TH:(half + 1) * TH] = res.results[c]["out"]
    return out
```

```python
import math
from contextlib import ExitStack

import numpy as np
import ml_dtypes

import concourse.bass as bass
import concourse.mybir as mybir
from concourse.bass_utils import run_bass_kernel_spmd

F32 = mybir.dt.float32
BF16 = mybir.dt.bfloat16
AF = mybir.ActivationFunctionType
ALU = mybir.AluOpType
AX = mybir.AxisListType

D_MODEL = 1024
NH = 8
NCB = 8
EPS = 1e-6


class Buf:
    __slots__ = ("name", "w", "rs")

    def __init__(self, name):
        self.name = name
        self.w = None
        self.rs = []


class _FakeIns:
    def __init__(self, name, out):
        self.name = name
        self.out = out


class _FakeEng:
    def __getattr__(self, name):
        def f(*a, **k):
            out = k.get("out", a[0] if a else None)
            return _FakeIns(name, out)
        return f


class _Op:
    __slots__ = ("idx", "eng", "fn", "dur", "kind", "preds", "succs", "npred", "ready", "start", "fin", "pos", "dsem", "dval")

    def __init__(self, idx, eng, fn, dur, kind):
        self.idx = idx
        self.eng = eng
        self.fn = fn
        self.dur = dur
        self.kind = kind
        self.preds = {}
        self.succs = []
        self.npred = 0
        self.ready = 0.0
        self.start = 0.0
        self.fin = 0.0
        self.pos = 0
        self.dsem = -1
        self.dval = 0


class Sched:
    ENG = ("pe", "act", "dve", "pool", "sp")
    LAT = 0.5

    def __init__(self, nc, stack, n_dma_sems=16):
        self.nc = nc
        self.sem = {e: stack.enter_context(nc.semaphore("s_" + e)) for e in self.ENG}
        self.dsem = [stack.enter_context(nc.semaphore("d%d" % i)) for i in range(n_dma_sems)]
        self.ops = []
        self.labels = None
        self.fake = _FakeEng()

    def _cost(self, eng, fn):
        try:
            info = fn(self.fake)
            ap = info.out
            cols = float(ap.free_size())
            nbytes = float(ap.nbytes())
        except Exception:
            cols, nbytes = 512.0, 65536.0
        if eng == "pe":
            return 0.07 + cols / 2400.0
        if eng == "dve":
            return 0.07 + cols / 960.0
        if eng == "act":
            return 0.2 + cols / 1200.0
        if eng == "pool":
            return 0.25 + cols / 420.0
        return 2.0 + nbytes / 150e3

    def _add(self, eng, fn, reads, writes, kind):
        op = _Op(len(self.ops), eng, fn, self._cost(eng if kind == "op" else "sp", fn), kind)
        preds = op.preds

        def add(p, wait, why=None):
            if p is op:
                return
            preds[p] = preds.get(p, False) or wait
            if self.labels is not None:
                self.labels[(p.idx, op.idx)] = why
        for b in reads:
            if b.w is not None:
                add(b.w, True, "RAW " + b.name)
            if b.name.startswith(("bk", "tp")):
                for r in b.rs:
                    if r.eng != eng or r.kind != "op":
                        add(r, True, "RR " + b.name)
        for b in writes:
            if b.w is not None:
                add(b.w, not (eng == "pe" and b.w.eng == "pe" and b.w.kind == "op" and kind == "op"), "WAW " + b.name)
            for r in b.rs:
                add(r, not (eng == "pe" and r.eng == "pe" and r.kind == "op" and kind == "op"), "WAR " + b.name)
        for b in reads:
            b.rs.append(op)
        for b in writes:
            b.w = op
            b.rs = []
        for p in preds:
            p.succs.append(op)
        op.npred = len(preds)
        self.ops.append(op)
        return op

    def op(self, eng, fn, reads=(), writes=()):
        return self._add(eng, fn, reads, writes, "op")

    def dma(self, eng, fn, reads=(), writes=()):
        return self._add(eng if eng in ("sp", "act") else "sp", fn, reads, writes, "dma")

    PRIO_WINDOW = 0.3

    def _schedule(self):
        import heapq
        bl = [0.0] * len(self.ops)
        for op in reversed(self.ops):
            m = 0.0
            for sx in op.succs:
                v = bl[sx.idx] + (self.LAT if sx.eng != op.eng else 0.05)
                if v > m:
                    m = v
            bl[op.idx] = m + op.dur
        W = self.PRIO_WINDOW
        ready = {e: [] for e in self.ENG}
        free = {e: 0.0 for e in self.ENG}
        order = {e: [] for e in self.ENG}
        for op in self.ops:
            if op.npred == 0:
                ready[op.eng].append(op)
        remaining = len(self.ops)
        while remaining:
            best = None
            for e in self.ENG:
                lst = ready[e]
                if not lst:
                    continue
                t0 = min(max(free[e], o.ready) for o in lst)
                cand = None
                for o in lst:
                    st = max(free[e], o.ready)
                    if st <= t0 + W:
                        k = (-bl[o.idx], o.idx) if W > 0 else (st, o.idx)
                        if cand is None or k < cand[0]:
                            cand = (k, o, st)
                key = (t0, cand[1].idx)
                if best is None or key < best[0]:
                    best = (key, e, cand[1], cand[2])
            assert best is not None, "scheduler deadlock"
            _, e, op, st = best
            ready[e].remove(op)
            op.start = st
            if op.kind == "dma":
                free[e] = op.start + 0.06
                op.fin = op.start + op.dur
            else:
                op.fin = op.start + op.dur
                free[e] = op.fin
            op.pos = len(order[e])
            order[e].append(op)
            remaining -= 1
            for sx in op.succs:
                sx.npred -= 1
                if not sx.preds[op]:
                    r = op.start
                else:
                    r = op.fin + (self.LAT if (sx.eng != op.eng or op.kind == "dma") else 0.2)
                if r > sx.ready:
                    sx.ready = r
                if sx.npred == 0:
                    ready[sx.eng].append(sx)
        self.order = order
        self.est_us = max(op.fin for op in self.ops)

    def emit(self):
        self._schedule()
        order = self.order
        nd = len(self.dsem)
        cnt = 0
        dcnt = [0] * nd
        dlast = [None] * nd
        extra = {}
        k = 0
        all_dmas = sorted((o for e_ in self.ENG for o in order[e_] if o.kind == "dma"), key=lambda o: (o.start, o.idx))
        for op in all_dmas:
            if op.kind == "dma":
                i = k % nd
                k += 1
                dcnt[i] += 16
                op.dsem, op.dval = i, dcnt[i]
                if dlast[i] is not None:
                    extra[op] = dlast[i]
                dlast[i] = op
        idx_on = {}
        for e in self.ENG:
            c = 0
            for op in order[e]:
                if op.kind == "op":
                    c += 1
                    idx_on[op] = c
        prog = {e: [] for e in self.ENG}
        for e in self.ENG:
            seen = {}
            for op in order[e]:
                waits = [p for p, w in op.preds.items() if w]
                if op in extra:
                    waits.append(extra[op])
                for p in waits:
                    if p.kind == "dma":
                        key, val, sem = ("d", p.dsem), p.dval, self.dsem[p.dsem]
                    else:
                        key, val, sem = p.eng, idx_on[p], self.sem[p.eng]
                    if seen.get(key, 0) >= val:
                        continue
                    seen[key] = val
                    prog[e].append(("w", sem, val))
                if op.kind == "dma":
                    prog[e].append(("o", op.fn, self.dsem[op.dsem], 16))
                else:
                    prog[e].append(("o", op.fn, self.sem[e], 1))
            if e == "sp":
                for i in range(nd):
                    if dcnt[i] > 0 and seen.get(("d", i), 0) < dcnt[i]:
                        prog[e].append(("w", self.dsem[i], dcnt[i]))
        nc = self.nc

        def replay(en, items):
            for it in items:
                if it[0] == "w":
                    en.wait_ge(it[1], it[2])
                else:
                    it[1](en).then_inc(it[2], it[3])

        with nc.Block() as block:
            @block.tensor
            def _(en):
                replay(en, prog["pe"])

            @block.scalar
            def _(en):
                replay(en, prog["act"])

            @block.vector
            def _(en):
                replay(en, prog["dve"])

            @block.gpsimd
            def _(en):
                replay(en, prog["pool"])

            @block.sync
            def _(en):
                replay(en, prog["sp"])


NWS = 2


def build_program(NT):
    assert NT % 4 == 0
    NB = NT // 4
    TH = NT * 128
    log_g = [math.log1p(-2.0 ** (-5.0 - h)) for h in range(NH)]
    GAM = [math.exp(128.0 * lg) for lg in log_g]

    nc = bass.Bass("TRN2", target_bir_lowering=False)
    din = lambda n, s, d=F32: nc.dram_tensor(n, s, d, kind="ExternalInput").ap()
    x_main = din("x_main", [TH, D_MODEL])
    x_pre = din("x_pre", [TH, D_MODEL])
    tab_main = din("tab_main", [TH, 256])
    tab_pre = din("tab_pre", [TH, 256])
    w_in = din("w_in", [D_MODEL, 6 * D_MODEL])
    w_out = din("w_out", [2 * D_MODEL, D_MODEL])
    gin_d = din("gin", [128, 8])
    gout_d = din("gout", [128, D_MODEL])
    convw_d = din("convw", [128, 8, 4])
    convb_d = din("convb", [128, 8])
    gaw_d = din("gaw", [8, 128, 128])
    gxw_d = din("gxw", [8, 128, 128])
    gab_d = din("gab", [128, 8])
    gxb_d = din("gxb", [128, 8])
    lam_d = din("lam", [128, 8])
    flag_d = din("flag", [128, 1])
    ident_d = din("ident", [128, 128], BF16)
    causal_d = din("causal", [128, 128])
    cdec_d = din("cdec", [128, 8])
    xibc_d = din("xibc", [128, 8, 128])
    zeta_d = din("zeta", [128, 8])
    zpre_d = din("zpre", [128, NT, 8])
    out_d = nc.dram_tensor("out", [TH, D_MODEL], F32, kind="ExternalOutput").ap()

    with ExitStack() as st:
        S = Sched(nc, st)
        sb = lambda n, s, d=F32: st.enter_context(nc.sbuf_tensor("sb_" + n, s, d))
        ps = lambda n, s, d=F32: st.enter_context(nc.psum_tensor("ps_" + n, s, d))

        Wt = sb("W", [128, 32768], BF16)
        W3 = Wt[:].rearrange("p (k c) -> p k c", c=4096)
        Wf = Wt[:].rearrange("p (k c) -> p k c", c=1024)
        GW = sb("GW", [128, 2, 8, 128], BF16)
        YR = sb("YR", [128, 8, TH], BF16)
        xt = [sb("xt%d" % i, [128, 1024]) for i in range(2)]
        tab = [sb("tab%d" % i, [128, 256]) for i in range(2)]
        xs = sb("xs", [128, 1024], BF16)
        junk = xs
        xnT = sb("xnT", [128, 8, 512], BF16)
        RWall = sb("RWall", [128, max(8192, 3072 * NWS)], BF16)
        YL = RWall[:, 0:4096].rearrange("p (c t) -> p c t", c=8)

        def mkset(i):
            d = {"i": i}
            v4 = lambda k: RWall[:, i * 3072 + k * 512: i * 3072 + (k + 1) * 512].rearrange("p (h d) -> p h d", h=4)
            d["qr"], d["vb"], d["qT"], d["kT"], d["Pm"], d["ybf"] = v4(0), v4(1), v4(2), v4(3), v4(4), v4(5)
            d["kr"] = sb("kr%d" % i, [128, 4, 128], BF16)[:]
            d["vh"] = sb("vh%d" % i, [128, 4, 128], BF16)[:]
            F1 = sb("F1_%d" % i, [128, 512])
            F2 = sb("F2_%d" % i, [128, 512])
            a0 = sb("acc0_%d" % i, [128, 512])
            a1 = sb("acc1_%d" % i, [128, 512])
            xl_ = sb("xl_%d" % i, [128, 515])
            d["xcb"] = sb("xcb_%d" % i, [128, 512], BF16)[:]
            d["stt"] = sb("stt_%d" % i, [128, 32])
            d["t1"] = F1[:].rearrange("p (h d) -> p h d", h=4)
            d["t2"] = F2[:].rearrange("p (h d) -> p h d", h=4)
            d["sg"] = a0[:].rearrange("p (h d) -> p h d", h=4)
            d["thi"], d["a2"], d["hh"], d["av"] = F1[:], F2[:], F2[:], a0[:]
            d["acc"] = [a0[:], a1[:]]
            d["xl"] = xl_[:]
            d["thr"] = xl_[:, 0:512]
            return d

        WS = [mkset(i_) for i_ in range(NWS)]
        S32 = sb("S32", [128, 8, 128])
        Sbf = sb("Sbf", [128, 8, 128], BF16)
        hist = sb("hist", [128, 8, 3])
        hstate = sb("hstate", [128, 8])
        GX = sb("GX", [128, 1024])
        xibc = GX[:].rearrange("p (h d) -> p h d", h=8)
        gout = GX[:]
        causal = sb("causal", [128, 128])
        cdec = sb("cdec", [128, 8])
        ident = sb("ident", [128, 128], BF16)
        gin = sb("gin", [128, 8])
        gink = sb("gink", [128, 8])
        convw = sb("convw", [128, 8, 4])
        convb = sb("convb", [128, 8])
        hba = sb("hba", [128, 8])
        hbx = sb("hbx", [128, 8])
        lam = sb("lam", [128, 8])
        nsp = sb("nsp", [128, 8])
        hnsp = sb("hnsp", [128, 8])
        zeta = sb("zeta", [128, 8])
        zpre = sb("zpre", [128, NT, 8])
        flag = sb("flag", [128, 1])
        neghalf = sb("neghalf", [128, 8])
        ss = sb("ss", [128, 4])
        rstd = sb("rstd", [128, 4])
        rstd_all = sb("rstd_all", [128, NT])
        fz = sb("fz", [128, 2])
        sixteenth = sb("sixteenth", [128, 1])

        tpA = ps("tpA", [128, 8, 128], BF16)
        tpB = ps("tpB", [128, 8, 128], BF16)
        BK = [ps("bk%d" % i, [128, 512]) for i in range(6)]

        B = {}

        def bf(name):
            if name not in B:
                B[name] = Buf(name)
            return B[name]

        def wb(w, name):
            return bf("%s_%d" % (name, w["i"]))

        def fence(old, new):
            S.op("pool", lambda e: e.memset(fz[:, 0:1], 0.0), reads=[bf(n) for n in old], writes=[bf(n) for n in new] + [bf("fz")])

        def small_load(dst, src, name):
            S.dma("sp", lambda e: e.dma_start(out=dst, in_=src), writes=[bf(name)])

        small_load(gin[:], gin_d[:, :], "gin")
        small_load(convw[:], convw_d[:, :, :], "convw")
        small_load(convb[:], convb_d[:, :], "convb")
        small_load(hba[:], gab_d[:, :], "hba")
        small_load(hbx[:], gxb_d[:, :], "hbx")
        small_load(lam[:], lam_d[:, :], "lam")
        small_load(flag[:], flag_d[:, :], "flag")
        small_load(ident[:], ident_d[:, :], "ident")
        small_load(causal[:], causal_d[:, :], "causal")
        small_load(cdec[:], cdec_d[:, :], "cdec")
        small_load(xibc, xibc_d[:, :, :], "GX")
        small_load(zeta[:], zeta_d[:, :], "zeta")
        small_load(zpre[:], zpre_d[:, :, :], "zpre")

        S.op("dve", lambda e: e.memset(neghalf[:], -0.5), writes=[bf("neghalf")])
        S.op("dve", lambda e: e.memset(sixteenth[:], 1.0 / 16.0), writes=[bf("sixteenth")])
        S.op("dve", lambda e: e.memset(S32[:], 0.0), writes=[bf("S32_0"), bf("S32_1")])
        S.op("dve", lambda e: e.memset(hist[:], 0.0), writes=[bf("hist")])
        S.op("dve", lambda e: e.memset(hstate[:], 0.0), writes=[bf("hstate%d" % c_) for c_ in range(8)])
        S.op("dve", lambda e: e.tensor_scalar(out=hba[:], in0=hba[:], scalar1=0.5, scalar2=None, op0=ALU.mult),
             reads=[bf("hba")], writes=[bf("hba")])
        S.op("dve", lambda e: e.tensor_scalar(out=hbx[:], in0=hbx[:], scalar1=0.5, scalar2=None, op0=ALU.mult),
             reads=[bf("hbx")], writes=[bf("hbx")])
        S.op("dve", lambda e: e.tensor_scalar(out=gink[:], in0=gin[:], scalar1=128.0 ** -0.5, scalar2=None, op0=ALU.mult),
             reads=[bf("gin")], writes=[bf("gink")])
        S.op("act", lambda e: e.activation(out=nsp[:], in_=lam[:], func=AF.Exp, scale=-1.0),
             reads=[bf("lam")], writes=[bf("nsp")])
        S.op("act", lambda e: e.activation(out=hnsp[:], in_=nsp[:], func=AF.Ln, bias=1.0),
             reads=[bf("nsp")], writes=[bf("hnsp")])
        S.op("dve", lambda e: e.tensor_scalar(out=nsp[:], in0=hnsp[:], scalar1=-8.0, scalar2=None, op0=ALU.mult),
             reads=[bf("hnsp")], writes=[bf("nsp")])
        S.op("dve", lambda e: e.tensor_scalar(out=hnsp[:], in0=hnsp[:], scalar1=-4.0, scalar2=None, op0=ALU.mult),
             reads=[bf("hnsp"), bf("nsp")], writes=[bf("hnsp")])

        for gi_, gd in enumerate((gaw_d, gxw_d)):
            S.dma("sp", lambda e, gd=gd, gi_=gi_: e.dma_start(out=xt[gi_][:].rearrange("p (n o) -> p n o", n=8),
                                                              in_=gd.rearrange("n i o -> i n o")),
                  writes=[bf("xt%d" % gi_)])
            S.op("dve", lambda e, gi_=gi_: e.tensor_copy(out=GW[:, gi_, :, :],
                                                         in_=xt[gi_][:].rearrange("p (n o) -> p n o", n=8)),
                 reads=[bf("xt%d" % gi_)], writes=[bf("GW")])

        wl_cnt = [0]

        def load_w(chunk, dst, src, r0, c0, scale_ap=None, scale_name=None):
            i = wl_cnt[0] % 2
            wl_cnt[0] += 1
            S.dma("sp", lambda e: e.dma_start(out=xt[i][:], in_=src[r0:r0 + 128, c0:c0 + 1024]),
                  writes=[bf("xt%d" % i)])
            eng = "dve" if (wl_cnt[0] % 2 == 0) else "act"
            if scale_ap is None:
                if eng == "dve":
                    S.op(eng, lambda e: e.tensor_copy(out=dst, in_=xt[i][:]), reads=[bf("xt%d" % i)], writes=[bf("Wc%d" % chunk)])
                else:
                    S.op(eng, lambda e: e.activation(out=dst, in_=xt[i][:], func=AF.Copy), reads=[bf("xt%d" % i)], writes=[bf("Wc%d" % chunk)])
            else:
                if eng == "dve":
                    S.op(eng, lambda e: e.tensor_scalar(out=dst, in0=xt[i][:], scalar1=scale_ap, scalar2=None, op0=ALU.mult),
                         reads=[bf("xt%d" % i), bf(scale_name)], writes=[bf("Wc%d" % chunk)])
                else:
                    S.op(eng, lambda e: e.activation(out=dst, in_=xt[i][:], func=AF.Copy, scale=scale_ap),
                         reads=[bf("xt%d" % i), bf(scale_name)], writes=[bf("Wc%d" % chunk)])

        for kc in range(8):
            load_w(kc * 4 + 0, W3[:, kc, 0:1024], w_in, kc * 128, 1024, gink[:, kc:kc + 1], "gink")
        for kc in range(8):
            load_w(kc * 4 + 1, W3[:, kc, 1024:2048], w_in, kc * 128, 2048, gin[:, kc:kc + 1], "gin")
        for kc in range(8):
            load_w(kc * 4 + 2, W3[:, kc, 2048:3072], w_in, kc * 128, 4096, gin[:, kc:kc + 1], "gin")

        def front_g(src, tabsrc, t, slot, j, need_tab, mode="P"):
            bx = bf("xt%d" % slot)
            S.dma("sp", lambda e: e.dma_start(out=xt[slot][:], in_=src[t * 128:(t + 1) * 128, :]), writes=[bx])
            if need_tab:
                S.dma("sp", lambda e: e.dma_start(out=tab[slot][:], in_=tabsrc[t * 128:(t + 1) * 128, :]),
                      writes=[bf("tab%d" % slot)])
            if mode == "P":
                rs, rsb = rstd[:, 0:1], bf("rstd")
            else:
                rs, rsb = rstd_all[:, t:t + 1], bf("rsa%d" % t)
            if mode != "L":
                S.op("pool", lambda e: e.memset(ss[:, 0:1], 0.0), writes=[bf("ss")])
                S.op("act", lambda e: e.activation(out=junk[:], in_=xt[slot][:], func=AF.Square, accum_out=ss[:, 0:1]),
                     reads=[bx, bf("ss")], writes=[bf("xs"), bf("ss")])
                yield
                S.op("dve", lambda e: e.tensor_scalar(out=rs, in0=ss[:, 0:1], scalar1=1.0 / D_MODEL, scalar2=EPS,
                                                      op0=ALU.mult, op1=ALU.add), reads=[bf("ss")], writes=[rsb])
                S.op("pool", lambda e: e.tensor_tensor(out=rs, in0=rs, in1=neghalf[:, 0:1], op=ALU.pow),
                     reads=[rsb, bf("neghalf")], writes=[rsb])
                yield
            S.op("act", lambda e: e.activation(out=xs[:], in_=xt[slot][:], func=AF.Copy, scale=rs),
                 reads=[bx, rsb], writes=[bf("xs")])
            yield
            tp, tpn = (tpB, "tpB") if (mode != "R" and t % 2 == 1) else (tpA, "tpA")
            for kc in range(8):
                S.op("pe", lambda e, kc=kc: e.transpose(out=tp[:, kc, :], in_=xs[:, kc * 128:(kc + 1) * 128],
                                                        identity=ident[:]),
                     reads=[bf("xs"), bf("ident")], writes=[bf(tpn)])
            yield
            S.op("dve", lambda e: e.tensor_copy(out=xnT[:, :, j * 128:(j + 1) * 128], in_=tp[:]),
                 reads=[bf(tpn)], writes=[bf("xnT%d" % j)])

        def run(*gens):
            gens = [g for g in gens if g is not None]
            while gens:
                for g in list(gens):
                    try:
                        next(g)
                    except StopIteration:
                        gens.remove(g)

        def front(*a):
            run(front_g(*a))

        def proj_tm(bank, j, wcol):
            for kc in range(8):
                S.op("pe", lambda e, kc=kc: e.matmul(BK[bank][:], lhsT=xnT[:, kc, j * 128:(j + 1) * 128],
                                                     rhs=W3[:, kc, wcol:wcol + 512], start=(kc == 0), stop=(kc == 7)),
                     reads=[bf("xnT%d" % j), bf("Wc%d" % (kc * 4 + wcol // 1024))], writes=[bf("bk%d" % bank)])

        def rotary(bank, slot, w, dst, dst_buf):
            src = BK[bank][:].rearrange("p (h d) -> p h d", h=4)
            tb = bf("tab%d" % slot)
            bb = bf("bk%d" % bank)
            t1, t2 = w["t1"], w["t2"]
            S.op("dve", lambda e: e.tensor_tensor(out=t1, in0=src,
                                                  in1=tab[slot][:, 0:128].unsqueeze(1).broadcast_to([128, 4, 128]),
                                                  op=ALU.mult), reads=[bb, tb], writes=[wb(w, "F1")])
            S.op("dve", lambda e: e.tensor_tensor(out=t2[:, :, 0:64], in0=src[:, :, 64:128],
                                                  in1=tab[slot][:, 128:192].unsqueeze(1).broadcast_to([128, 4, 64]),
                                                  op=ALU.mult), reads=[bb, tb], writes=[wb(w, "F2")])
            S.op("dve", lambda e: e.tensor_tensor(out=t2[:, :, 64:128], in0=src[:, :, 0:64],
                                                  in1=tab[slot][:, 192:256].unsqueeze(1).broadcast_to([128, 4, 64]),
                                                  op=ALU.mult), reads=[bb, tb], writes=[wb(w, "F2")])
            S.op("pool", lambda e: e.tensor_tensor(out=dst, in0=t1, in1=t2, op=ALU.add),
                 reads=[wb(w, "F1"), wb(w, "F2")], writes=[dst_buf])

        def v_evac(bank, hp, w, plain, zt=None):
            bb = bf("bk%d" % bank)
            src = BK[bank][:].rearrange("p (h d) -> p h d", h=4)
            if plain:
                S.op("act", lambda e: e.activation(out=w["vb"], in_=src, func=AF.Copy), reads=[bb], writes=[wb(w, "vb")])
            zap = zeta[:, hp * 4:hp * 4 + 4] if zt is None else zpre[:, zt, hp * 4:hp * 4 + 4]
            S.op("dve", lambda e: e.tensor_tensor(out=w["vh"], in0=src,
                                                  in1=zap.unsqueeze(2).broadcast_to([128, 4, 128]),
                                                  op=ALU.mult), reads=[bb, bf("zeta"), bf("zpre")], writes=[wb(w, "vh")])

        def kv_update(hp, w, make_bf):
            kr, vh = w["kr"], w["vh"]
            for hl in range(4):
                S.op("pe", lambda e, hl=hl: e.matmul(BK[4][:, hl * 128:(hl + 1) * 128], lhsT=kr[:, hl, :], rhs=vh[:, hl, :],
                                                     start=True, stop=True),
                     reads=[wb(w, "kr"), wb(w, "vh")], writes=[bf("bk4")])
            for hl in range(4):
                h = hp * 4 + hl
                S.op("dve", lambda e, hl=hl, h=h: e.scalar_tensor_tensor(out=S32[:, h, :], in0=S32[:, h, :], scalar=float(GAM[h]),
                                                                         in1=BK[4][:, hl * 128:(hl + 1) * 128],
                                                                         op0=ALU.mult, op1=ALU.add),
                     reads=[bf("S32_%d" % hp), bf("bk4")], writes=[bf("S32_%d" % hp)])
            if make_bf:
                S.op("act", lambda e: e.activation(out=Sbf[:, hp * 4:hp * 4 + 4, :], in_=S32[:, hp * 4:hp * 4 + 4, :], func=AF.Copy),
                     reads=[bf("S32_%d" % hp)], writes=[bf("Sbf%d" % hp)])

        xall = [bf("xnT%d" % j) for j in range(4)]

        def lruA(cb, w, full, xlw, glw):
            pa = cb % 2
            pg = 2 + (cb % 2)
            xl, acc = w["xl"], w["acc"]
            for kc in range(8):
                S.op("pe", lambda e, kc=kc: e.matmul(BK[pa][:], lhsT=xlw(kc, cb)[0], rhs=xnT[:, kc, :], start=(kc == 0), stop=(kc == 7)),
                     reads=[bf("Wc%d" % xlw(kc, cb)[1])] + xall, writes=[bf("bk%d" % pa)])
            yield
            if full:
                for kc in range(8):
                    S.op("pe", lambda e, kc=kc: e.matmul(BK[pg][:], lhsT=glw(kc, cb)[0], rhs=xnT[:, kc, :], start=(kc == 0), stop=(kc == 7)),
                         reads=[bf("Wc%d" % glw(kc, cb)[1])] + xall, writes=[bf("bk%d" % pg)])
                yield
            S.op("pool", lambda e: e.tensor_copy(out=xl[:, 0:3], in_=hist[:, cb, :]), reads=[bf("hist")], writes=[wb(w, "xl")])
            if full:
                S.op("act", lambda e: e.activation(out=xl[:, 3:515], in_=BK[pa][:], func=AF.Copy),
                     reads=[bf("bk%d" % pa)], writes=[wb(w, "xl")])
            else:
                S.op("dve", lambda e: e.tensor_copy(out=xl[:, 3:515], in_=BK[pa][:]),
                     reads=[bf("bk%d" % pa)], writes=[wb(w, "xl")])
            S.op("pool", lambda e: e.tensor_copy(out=hist[:, cb, :], in_=xl[:, 512:515]), reads=[wb(w, "xl")], writes=[bf("hist")])
            yield
            S.op("act", lambda e: e.activation(out=acc[0], in_=xl[:, 3:515], func=AF.Identity,
                                               scale=convw[:, cb, 3:4], bias=convb[:, cb:cb + 1]),
                 reads=[wb(w, "xl"), bf("convw"), bf("convb")], writes=[wb(w, "acc0")])
            yield
            for n_, j_ in enumerate((2, 1, 0)):
                src_i = n_ % 2
                dst_i = 1 - src_i
                S.op("dve", lambda e, j_=j_, src_i=src_i, dst_i=dst_i: e.scalar_tensor_tensor(
                    out=acc[dst_i], in0=xl[:, j_:j_ + 512], scalar=convw[:, cb, j_:j_ + 1], in1=acc[src_i],
                    op0=ALU.mult, op1=ALU.add),
                    reads=[wb(w, "xl"), bf("convw"), wb(w, "acc%d" % src_i)], writes=[wb(w, "acc%d" % dst_i)])
                yield
            S.op("act", lambda e: e.activation(out=w["xcb"], in_=acc[1], func=AF.Copy), reads=[wb(w, "acc1")], writes=[wb(w, "xcb")])

        def lruB(cb, w, full):
            pg = 2 + (cb % 2)
            xc, thr, thi, av, a2, hh, xcb = w["acc"][1], w["thr"], w["thi"], w["av"], w["a2"], w["hh"], w["xcb"]
            bxc = wb(w, "acc1")
            S.op("pe", lambda e: e.matmul(BK[4][:], lhsT=GW[:, 0, cb, :], rhs=xcb, start=True, stop=True),
                 reads=[bf("GW"), wb(w, "xcb")], writes=[bf("bk4")])
            S.op("pe", lambda e: e.matmul(BK[5][:], lhsT=GW[:, 1, cb, :], rhs=xcb, start=True, stop=True),
                 reads=[bf("GW"), wb(w, "xcb")], writes=[bf("bk5")])
            yield
            S.op("act", lambda e: e.activation(out=thr, in_=BK[4][:], func=AF.Tanh, scale=0.5, bias=hba[:, cb:cb + 1]),
                 reads=[bf("bk4"), bf("hba")], writes=[wb(w, "xl")])
            S.op("act", lambda e: e.activation(out=thi, in_=BK[5][:], func=AF.Tanh, scale=0.5, bias=hbx[:, cb:cb + 1]),
                 reads=[bf("bk5"), bf("hbx")], writes=[wb(w, "F1")])
            yield
            S.op("act", lambda e: e.activation(out=av, in_=thr, func=AF.Exp, scale=hnsp[:, cb:cb + 1], bias=hnsp[:, cb:cb + 1]),
                 reads=[wb(w, "xl"), bf("hnsp")], writes=[wb(w, "acc0")])
            S.op("dve", lambda e: e.tensor_tensor(out=a2, in0=av, in1=av, op=ALU.mult),
                 reads=[wb(w, "acc0")], writes=[wb(w, "F2")])
            yield
            S.op("dve", lambda e: e.tensor_scalar(out=a2, in0=a2, scalar1=1.0, scalar2=-1.0 / 16.0, op0=ALU.min, op1=ALU.mult),
                 reads=[wb(w, "F2")], writes=[wb(w, "F2")])
            yield
            S.op("act", lambda e: e.activation(out=a2, in_=a2, func=AF.Sqrt, bias=sixteenth[:, 0:1]),
                 reads=[wb(w, "F2"), bf("sixteenth")], writes=[wb(w, "F2")])
            yield
            S.op("dve", lambda e: e.scalar_tensor_tensor(out=thi, in0=thi, scalar=1.0, in1=xc, op0=ALU.add, op1=ALU.mult),
                 reads=[wb(w, "F1"), bxc], writes=[wb(w, "F1")])
            S.op("dve", lambda e: e.tensor_tensor(out=thi, in0=thi, in1=a2, op=ALU.mult),
                 reads=[wb(w, "F1"), wb(w, "F2")], writes=[wb(w, "F1")])
            yield
            S.op("dve", lambda e: e.tensor_tensor_scan(out=hh, data0=av, data1=thi, initial=hstate[:, cb:cb + 1],
                                                       op0=ALU.mult, op1=ALU.add),
                 reads=[wb(w, "acc0"), wb(w, "F1"), bf("hstate%d" % cb)], writes=[wb(w, "F2")])
            S.op("pool", lambda e: e.tensor_copy(out=hstate[:, cb:cb + 1], in_=hh[:, 511:512]),
                 reads=[wb(w, "F2")], writes=[bf("hstate%d" % cb)])
            yield
            if full:
                S.op("act", lambda e: e.activation(out=thr, in_=BK[pg][:], func=AF.Tanh, scale=0.5),
                     reads=[bf("bk%d" % pg)], writes=[wb(w, "xl")])
                yield
                S.op("dve", lambda e: e.scalar_tensor_tensor(out=thr, in0=thr, scalar=1.0, in1=BK[pg][:], op0=ALU.add, op1=ALU.mult),
                     reads=[wb(w, "xl"), bf("bk%d" % pg)], writes=[wb(w, "xl")])
                S.op("dve", lambda e: e.tensor_tensor(out=YL[:, cb, :], in0=thr, in1=hh, op=ALU.mult),
                     reads=[wb(w, "xl"), wb(w, "F2")], writes=[bf("YL")])

        def lru_block(full, xlw, glw=None):
            run(lruA(0, WS[0], full, xlw, glw))
            for cb in range(8):
                run(lruB(cb, WS[cb % NWS], full),
                    lruA(cb + 1, WS[(cb + 1) % NWS], full, xlw, glw) if cb + 1 < 8 else None)

        def retA(t, hp, w, slot, j):
            proj_tm(0, j, 2048 + hp * 512)
            yield
            rotary(0, slot, w, w["qr"], wb(w, "qr"))
            yield
            proj_tm(1, j, 0 + hp * 512)
            yield
            rotary(1, slot, w, w["kr"], wb(w, "kr"))
            yield
            proj_tm(2, j, 1024 + hp * 512)
            yield
            v_evac(2, hp, w, True)
            yield
            proj_tm(2, j, 3072 + hp * 512)
            yield
            S.op("act", lambda e: e.activation(out=w["sg"], in_=BK[2][:].rearrange("p (h d) -> p h d", h=4), func=AF.Silu),
                 reads=[bf("bk2")], writes=[wb(w, "acc0")])

        def retB(t, hp, w):
            qr, kr, vb, qT, kT, Pm, ybf, sg, t1, t2 = (w[k] for k in ("qr", "kr", "vb", "qT", "kT", "Pm", "ybf", "sg", "t1", "t2"))
            stt = w["stt"]
            bst = wb(w, "stt")
            for hl in range(4):
                S.op("pe", lambda e, hl=hl: e.transpose(out=tpB[:, hl, :], in_=qr[:, hl, :], identity=ident[:]),
                     reads=[wb(w, "qr"), bf("ident")], writes=[bf("tpB")])
            for hl in range(4):
                S.op("pe", lambda e, hl=hl: e.transpose(out=tpB[:, 4 + hl, :], in_=kr[:, hl, :], identity=ident[:]),
                     reads=[wb(w, "kr"), bf("ident")], writes=[bf("tpB")])
            yield
            S.op("dve", lambda e: e.tensor_tensor(out=qT, in0=tpB[:, 0:4, :], in1=xibc[:, hp * 4:hp * 4 + 4, :], op=ALU.mult),
                 reads=[bf("tpB"), bf("GX")], writes=[wb(w, "qT")])
            S.op("act", lambda e: e.activation(out=kT, in_=tpB[:, 4:8, :], func=AF.Copy),
                 reads=[bf("tpB")], writes=[wb(w, "kT")])
            yield
            for hl in range(4):
                S.op("pe", lambda e, hl=hl: e.matmul(BK[5][:, hl * 128:(hl + 1) * 128], lhsT=kT[:, hl, :], rhs=qT[:, hl, :],
                                                     start=True, stop=True),
                     reads=[wb(w, "kT"), wb(w, "qT")], writes=[bf("bk5")])
            yield
            for hl in range(4):
                h = hp * 4 + hl
                S.op("dve", lambda e, hl=hl, h=h: e.scalar_tensor_tensor(out=Pm[:, hl, :], in0=BK[5][:, hl * 128:(hl + 1) * 128],
                                                                         scalar=cdec[:, h:h + 1], in1=causal[:],
                                                                         op0=ALU.mult, op1=ALU.mult),
                     reads=[bf("bk5"), bf("cdec"), bf("causal")], writes=[wb(w, "Pm")])
            yield
            for hl in range(4):
                h = hp * 4 + hl
                S.op("pe", lambda e, hl=hl: e.matmul(BK[3][:, hl * 128:(hl + 1) * 128], lhsT=Pm[:, hl, :], rhs=vb[:, hl, :],
                                                     start=True, stop=False),
                     reads=[wb(w, "Pm"), wb(w, "vb")], writes=[bf("bk3")])
                S.op("pe", lambda e, hl=hl, h=h: e.matmul(BK[3][:, hl * 128:(hl + 1) * 128], lhsT=qT[:, hl, :], rhs=Sbf[:, h, :],
                                                          start=False, stop=True),
                     reads=[wb(w, "qT"), bf("Sbf%d" % hp)], writes=[bf("bk3")])
            yield
            kv_update(hp, w, True)
            yield
            po = BK[3][:].rearrange("p (h d) -> p h d", h=4)
            S.op("dve", lambda e: e.tensor_reduce(out=stt[:, 0:4], in_=po, axis=AX.X, op=ALU.add), reads=[bf("bk3")], writes=[bst])
            S.op("act", lambda e: e.activation(out=t2, in_=po, func=AF.Square), reads=[bf("bk3")], writes=[wb(w, "F2")])
            yield
            S.op("dve", lambda e: e.tensor_reduce(out=stt[:, 4:8], in_=t2, axis=AX.X, op=ALU.add),
                 reads=[wb(w, "F2"), bst], writes=[bst])
            S.op("dve", lambda e: e.tensor_scalar(out=stt[:, 8:12], in0=stt[:, 0:4], scalar1=1.0 / 128.0, scalar2=None, op0=ALU.mult),
                 reads=[bst], writes=[bst])
            S.op("dve", lambda e: e.tensor_tensor(out=stt[:, 12:16], in0=stt[:, 8:12], in1=stt[:, 8:12], op=ALU.mult),
                 reads=[bst], writes=[bst])
            S.op("dve", lambda e: e.scalar_tensor_tensor(out=stt[:, 16:20], in0=stt[:, 4:8], scalar=1.0 / 128.0, in1=stt[:, 12:16],
                                                         op0=ALU.mult, op1=ALU.subtract), reads=[bst], writes=[bst])
            S.op("dve", lambda e: e.tensor_scalar(out=stt[:, 16:20], in0=stt[:, 16:20], scalar1=0.0, scalar2=EPS, op0=ALU.max, op1=ALU.add),
                 reads=[bst], writes=[bst])
            yield
            S.op("pool", lambda e: e.tensor_tensor(out=stt[:, 20:24], in0=stt[:, 16:20], in1=neghalf[:, 0:4], op=ALU.pow),
                 reads=[bst, bf("neghalf")], writes=[bst])
            yield
            S.op("dve", lambda e: e.tensor_tensor(out=t1, in0=po, in1=stt[:, 8:12].unsqueeze(2).broadcast_to([128, 4, 128]),
                                                  op=ALU.subtract), reads=[bf("bk3"), bst], writes=[wb(w, "F1")])
            S.op("pool", lambda e: e.tensor_tensor(out=sg, in0=sg, in1=stt[:, 20:24].unsqueeze(2).broadcast_to([128, 4, 128]),
                                                   op=ALU.mult), reads=[wb(w, "acc0"), bst], writes=[wb(w, "acc0")])
            yield
            S.op("dve", lambda e: e.tensor_tensor(out=ybf, in0=t1, in1=sg, op=ALU.mult),
                 reads=[wb(w, "F1"), wb(w, "acc0")], writes=[wb(w, "ybf")])
            yield
            for hl in range(4):
                S.op("pe", lambda e, hl=hl: e.transpose(out=tpA[:, hl, :], in_=ybf[:, hl, :], identity=ident[:]),
                     reads=[wb(w, "ybf"), bf("ident")], writes=[bf("tpA")])
            yield
            S.op("act", lambda e: e.activation(out=YR[:, hp * 4:hp * 4 + 4, t * 128:(t + 1) * 128], in_=tpA[:, 0:4, :], func=AF.Copy),
                 reads=[bf("tpA")], writes=[bf("YR")])

        def preA(t, hp, w, slot, j):
            proj_tm(0, j, 0 + hp * 512)
            yield
            rotary(0, slot, w, w["kr"], wb(w, "kr"))
            yield
            proj_tm(1, j, 1024 + hp * 512)
            yield
            v_evac(1, hp, w, False, zt=t)

        def preB(t, hp, w):
            kr, vh = w["kr"], w["vh"]
            for hl in range(4):
                S.op("pe", lambda e, hl=hl: e.matmul(BK[2 + hp][:, hl * 128:(hl + 1) * 128], lhsT=kr[:, hl, :], rhs=vh[:, hl, :],
                                                     start=(t == 0 and hl == 0), stop=(t == NT - 1 and hl == 3)),
                     reads=[wb(w, "kr"), wb(w, "vh")], writes=[bf("bk%d" % (2 + hp))])
            yield

        def pipeline(n, stA, stB, pre_front):
            run(pre_front(-1))
            run(stA(0))
            for u in range(n):
                run(stB(u), stA(u + 1) if u + 1 < n else None, pre_front(u))

        xlw_P = lambda kc, cb: (W3[:, kc, 2048 + cb * 128: 2048 + (cb + 1) * 128], kc * 4 + 2)
        for blk in range(NB):
            units = [(blk * 4 + j, hp) for j in range(4) for hp in range(2)]

            def pf(u, blk=blk, units=units):
                if u == -1:
                    return front_g(x_pre, tab_pre, blk * 4, (blk * 4) % 2, 0, True)
                t, hp = units[u]
                if hp == 0 and (t % 4) < 3:
                    return front_g(x_pre, tab_pre, t + 1, (t + 1) % 2, (t + 1) % 4, True)
                return None

            pipeline(len(units),
                     lambda u, units=units: preA(units[u][0], units[u][1], WS[u % NWS], units[u][0] % 2, units[u][0] % 4),
                     lambda u, units=units: preB(units[u][0], units[u][1], WS[u % NWS]),
                     pf)
            lru_block(False, xlw_P)

        for kc in range(8):
            load_w(kc * 4 + 2, W3[:, kc, 2048:3072], w_in, kc * 128, 0, gin[:, kc:kc + 1], "gin")
        for kc in range(8):
            load_w(kc * 4 + 3, W3[:, kc, 3072:4096], w_in, kc * 128, 3072, gin[:, kc:kc + 1], "gin")

        for hp in range(2):
            S.op("dve", lambda e, hp=hp: e.tensor_scalar(out=S32[:, hp * 4:hp * 4 + 4, :],
                                                         in0=BK[2 + hp][:].rearrange("p (h d) -> p h d", h=4),
                                                         scalar1=flag[:, 0:1], scalar2=None, op0=ALU.mult),
                 reads=[bf("bk%d" % (2 + hp)), bf("flag")], writes=[bf("S32_%d" % hp)])
        hs_all = [bf("hstate%d" % cb) for cb in range(8)]
        S.op("dve", lambda e: e.tensor_scalar(out=hstate[:], in0=hstate[:], scalar1=flag[:, 0:1], scalar2=None, op0=ALU.mult),
             reads=hs_all + [bf("flag")], writes=hs_all)
        S.op("dve", lambda e: e.tensor_scalar(out=hist[:], in0=hist[:], scalar1=flag[:, 0:1], scalar2=None, op0=ALU.mult),
             reads=[bf("hist"), bf("flag")], writes=[bf("hist")])
        for hp in range(2):
            S.op("act", lambda e, hp=hp: e.activation(out=Sbf[:, hp * 4:hp * 4 + 4, :], in_=S32[:, hp * 4:hp * 4 + 4, :], func=AF.Copy),
                 reads=[bf("S32_%d" % hp)], writes=[bf("Sbf%d" % hp)])

        unitsR = [(t, hp) for t in range(NT) for hp in range(2)]

        def pfR(u):
            if u == -1:
                return front_g(x_main, tab_main, 0, 0, 0, True, "R")
            t, hp = unitsR[u]
            if hp == 0 and t + 1 < NT:
                return front_g(x_main, tab_main, t + 1, (t + 1) % 2, (t + 1) % 4, True, "R")
            return None

        pipeline(len(unitsR),
                 lambda u: retA(unitsR[u][0], unitsR[u][1], WS[u % NWS], unitsR[u][0] % 2, unitsR[u][0] % 4),
                 lambda u: retB(unitsR[u][0], unitsR[u][1], WS[u % NWS]),
                 pfR)

        fence(["%s_%d" % (n_, i_) for n_ in ("qr", "vb", "qT", "kT", "Pm", "ybf") for i_ in range(NWS)], ["YL"])
        for kc in range(8):
            load_w(24 + kc, Wf[:, 24 + kc, :], w_in, kc * 128, 4096, gin[:, kc:kc + 1], "gin")
        for kc in range(8):
            load_w(kc, Wf[:, kc, :], w_in, kc * 128, 5120, gin[:, kc:kc + 1], "gin")
        for kc in range(16):
            load_w(8 + kc, Wf[:, 8 + kc, :], w_out, kc * 128, 0)
        small_load(gout, gout_d[:, :], "GX")
        xlw_L = lambda kc, cb: (Wf[:, 24 + kc, cb * 128:(cb + 1) * 128], 24 + kc)
        glw_L = lambda kc, cb: (Wf[:, kc, cb * 128:(cb + 1) * 128], kc)

        for blk in range(NB):
            for j in range(4):
                t = blk * 4 + j
                front(x_main, tab_main, t, t % 2, j, False, "L")
            lru_block(True, xlw_L, glw_L)
            for j in range(4):
                t = blk * 4 + j
                slot = t % 2
                bx = bf("xt%d" % slot)
                S.dma("sp", lambda e, t=t, slot=slot: e.dma_start(out=xt[slot][:], in_=x_main[t * 128:(t + 1) * 128, :]), writes=[bx])
                for ng in range(2):
                    bank = 2 * (j % 2) + ng
                    for kc in range(16):
                        if kc < 8:
                            lhs = YR[:, kc, t * 128:(t + 1) * 128]
                            rd = [bf("YR")]
                        else:
                            lhs = YL[:, kc - 8, j * 128:(j + 1) * 128]
                            rd = [bf("YL")]
                        S.op("pe", lambda e, lhs=lhs, kc=kc, ng=ng, bank=bank: e.matmul(
                            BK[bank][:], lhsT=lhs, rhs=Wf[:, 8 + kc, ng * 512:(ng + 1) * 512], start=(kc == 0), stop=(kc == 15)),
                            reads=rd + [bf("Wc%d" % (8 + kc))], writes=[bf("bk%d" % bank)])
                    S.op("dve", lambda e, slot=slot, ng=ng, bank=bank: e.tensor_tensor(
                        out=xt[slot][:, ng * 512:(ng + 1) * 512], in0=BK[bank][:], in1=xt[slot][:, ng * 512:(ng + 1) * 512], op=ALU.add),
                        reads=[bf("bk%d" % bank), bx], writes=[bx])
                S.op("pool", lambda e: e.memset(ss[:, 1:2], 0.0), writes=[bf("ss2")])
                S.op("act", lambda e, slot=slot: e.activation(out=junk[:], in_=xt[slot][:], func=AF.Square, accum_out=ss[:, 1:2]),
                     reads=[bx, bf("ss2")], writes=[bf("xs"), bf("ss2")])
                S.op("dve", lambda e: e.tensor_scalar(out=rstd[:, 1:2], in0=ss[:, 1:2], scalar1=1.0 / D_MODEL, scalar2=EPS,
                                                      op0=ALU.mult, op1=ALU.add), reads=[bf("ss2")], writes=[bf("rstd2")])
                S.op("pool", lambda e: e.tensor_tensor(out=rstd[:, 1:2], in0=rstd[:, 1:2], in1=neghalf[:, 0:1], op=ALU.pow),
                     reads=[bf("rstd2"), bf("neghalf")], writes=[bf("rstd2")])
                S.op("dve", lambda e, slot=slot: e.scalar_tensor_tensor(out=xt[slot][:], in0=xt[slot][:], scalar=rstd[:, 1:2], in1=gout,
                                                                        op0=ALU.mult, op1=ALU.mult),
                     reads=[bx, bf("rstd2"), bf("GX")], writes=[bx])
                S.dma("act", lambda e, t=t, slot=slot: e.dma_start(out=out_d[t * 128:(t + 1) * 128, :], in_=xt[slot][:]), reads=[bx])

        S.emit()
    return nc


def _rot_table(T):
    d = 128
    inv_freq = (10000.0 ** (-np.arange(0, d, 2, dtype=np.float32) / np.float32(d))).astype(np.float32)
    ang = (np.arange(T, dtype=np.float32)[:, None] * inv_freq[None, :]).astype(np.float32).astype(np.float64)
    c = np.cos(ang)
    s = np.sin(ang)
    return np.concatenate([c, c, -s, s], axis=1).astype(np.float32)


def _consts():
    h = np.arange(NH, dtype=np.float64)
    log_g = np.log1p(-np.exp2(-5.0 - h))
    s = np.arange(128, dtype=np.float64)
    causal = (s[None, :] >= s[:, None]).astype(np.float32)
    cdec = np.exp(-(s[:, None] + 1.0) * log_g[None, :])
    xi = np.exp((s[None, None, :] + 1.0) * log_g[None, :, None]) * np.ones((128, 1, 1))
    zeta = np.exp((127.0 - s[:, None]) * log_g[None, :])
    return causal, cdec.astype(np.float32), xi.astype(np.float32), zeta.astype(np.float32)


_PROG_CACHE = {}


def kernel(x, norm_in_g, w_in, conv_w, conv_b, gate_a_w, gate_a_b, gate_x_w, gate_x_b, lru_lambda, w_out, norm_out_g):
    x = np.asarray(x, dtype=np.float32)
    Bsz, T, Dm = x.shape
    assert Dm == D_MODEL and Bsz * 2 == 8
    TH = T // 2
    NT = TH // 128
    f32 = lambda a: np.ascontiguousarray(np.asarray(a, dtype=np.float32))
    tabfull = _rot_table(T)
    causal, cdec, xi, zeta = _consts()
    lg = np.log1p(-np.exp2(-5.0 - np.arange(NH, dtype=np.float64)))
    pos = np.arange(TH, dtype=np.float64)
    zpre = np.exp((TH - 1.0 - pos)[:, None] * lg[None, :]).reshape(NT, 128, NH).transpose(1, 0, 2)
    zpre = np.ascontiguousarray(zpre.astype(np.float32))
    shared = {
        "w_in": f32(w_in),
        "w_out": f32(w_out),
        "gin": f32(np.asarray(norm_in_g).reshape(8, 128).T),
        "gout": f32(np.broadcast_to(np.asarray(norm_out_g)[None, :], (128, D_MODEL))),
        "convw": f32(np.asarray(conv_w).reshape(4, 8, 128).transpose(2, 1, 0)),
        "convb": f32(np.asarray(conv_b).reshape(8, 128).T),
        "gaw": f32(gate_a_w),
        "gxw": f32(gate_x_w),
        "gab": f32(np.asarray(gate_a_b).T),
        "gxb": f32(np.asarray(gate_x_b).T),
        "lam": f32(np.asarray(lru_lambda).reshape(8, 128).T),
        "ident": np.eye(128, dtype=np.float32).astype(ml_dtypes.bfloat16),
        "causal": causal, "cdec": cdec, "xibc": xi, "zeta": zeta,
    }
    in_maps = []
    for c in range(8):
        b, half = c // 2, c % 2
        m = dict(shared)
        m["x_main"] = np.ascontiguousarray(x[b, half * TH:(half + 1) * TH])
        m["x_pre"] = np.ascontiguousarray(x[b, 0:TH])
        m["tab_main"] = np.ascontiguousarray(tabfull[half * TH:(half + 1) * TH])
        m["tab_pre"] = np.ascontiguousarray(tabfull[0:TH])
        m["flag"] = np.full((128, 1), float(half), dtype=np.float32)
        m["zpre"] = zpre
        in_maps.append(m)
    if NT not in _PROG_CACHE:
        _PROG_CACHE[NT] = build_program(NT)
    nc = _PROG_CACHE[NT]
    res = run_bass_kernel_spmd(nc, in_maps, core_ids=list(range(8)))
    out = np.empty((Bsz, T, D_MODEL), dtype=np.float32)
    for c in range(8):
        b, half = c // 2, c % 2
        out[b, half * TH:(half + 1) * TH] = res.results[c]["out"]
    return out
```
